# Optimizing a Trainium2 kernel written in Bass

```python
import jax, jax.numpy as jnp
from jax import lax
import numpy as np

D_MODEL = 1024
BATCH = 4
SEQ = 4096
DEPTH = 4

CHUNK = 64
QBLK = 128
EPS = 1e-6
LN_EPS = 1e-5
MAX_POS_OFFSET = 32768

MLA_HEADS = 8
MLA_Q_RANK = 256
MLA_KV_RANK = 128
MLA_NOPE = 64
MLA_ROPE = 32
MLA_VDIM = 64
MLA_WIDTH = MLA_HEADS * MLA_VDIM
ROPE_THETA = 10000.0

GM_GROUPS = 4
GM_GROUP_CH = 128
GM_WIDTH = GM_GROUPS * GM_GROUP_CH
GM_BLOCK = 128

RW_HEADS = 8
RW_HEAD = 64
RW_WIDTH = RW_HEADS * RW_HEAD
RW_DECAY_LORA = 64
RW_A_LORA = 64
RW_V_LORA = 32
RW_LN_EPS = 64e-5

N_BRANCH = 3
BRANCH_WIDTH = 512

GM_IN = 2 * GM_WIDTH
RW_IN = 3 * RW_WIDTH + RW_DECAY_LORA + RW_A_LORA
Z_IN = N_BRANCH * BRANCH_WIDTH
GATE_IN = N_BRANCH * D_MODEL
IN_SPLITS = [MLA_Q_RANK, MLA_KV_RANK, MLA_ROPE, GM_IN, RW_IN, Z_IN, GATE_IN]
D_IN = sum(IN_SPLITS)

kernel_name = "hybrid_mla_gmlp_rwkv7_streaming_trunk"


def rmsnorm(x, g, eps=EPS):
    xf = x.astype(jnp.float32)
    y = xf * lax.rsqrt(jnp.mean(xf * xf, axis=-1, keepdims=True) + eps)
    return (y * g.astype(jnp.float32)).astype(x.dtype)


def layernorm(x, g, b, eps=LN_EPS):
    xf = x.astype(jnp.float32)
    mu = jnp.mean(xf, axis=-1, keepdims=True)
    var = jnp.mean(jnp.square(xf - mu), axis=-1, keepdims=True)
    y = (xf - mu) * lax.rsqrt(var + eps)
    return (y * g.astype(jnp.float32) + b.astype(jnp.float32)).astype(x.dtype)


def split_last(x, sizes):
    cuts = [int(s) for s in np.cumsum(sizes)[:-1]]
    return jnp.split(x, cuts, axis=-1)


def rope_angles(positions):
    inv_freq = ROPE_THETA ** (-jnp.arange(0, MLA_ROPE, 2, dtype=jnp.float32) / MLA_ROPE)
    ang = positions.astype(jnp.float32)[..., None] * inv_freq
    return jnp.cos(ang), jnp.sin(ang)


def apply_rope(x, cos, sin):
    xf = x.astype(jnp.float32)
    x1, x2 = jnp.split(xf, 2, axis=-1)
    return jnp.concatenate([x1 * cos - x2 * sin, x2 * cos + x1 * sin], axis=-1).astype(x.dtype)


def chunk_causal_attention(q, k, v):
    B, S, H, dk = q.shape
    dv = v.shape[-1]
    nb = S // QBLK
    scale = dk ** -0.5
    key_chunk = jnp.arange(S) // CHUNK
    qb = q.reshape(B, nb, QBLK, H, dk).transpose(1, 0, 2, 3, 4)

    def one_block(args):
        q_blk, blk = args
        q_chunk = (blk * QBLK + jnp.arange(QBLK)) // CHUNK
        allowed = key_chunk[None, :] <= q_chunk[:, None]
        s = jnp.einsum('bqhd,bkhd->bhqk', q_blk, k).astype(jnp.float32) * scale
        s = jnp.where(allowed[None, None], s, jnp.finfo(jnp.float32).min)
        p = jax.nn.softmax(s, axis=-1).astype(v.dtype)
        return jnp.einsum('bhqk,bkhd->bqhd', p, v)

    out = lax.map(one_block, (qb, jnp.arange(nb)))
    return out.transpose(1, 0, 2, 3, 4).reshape(B, S, H, dv)


def mla_mixer(q_lat, kv_lat, k_rope, q_norm_g, w_uq, kv_norm_g, w_ukv, cos, sin):
    B, S, _ = q_lat.shape
    q = (rmsnorm(q_lat, q_norm_g) @ w_uq).reshape(B, S, MLA_HEADS, MLA_NOPE + MLA_ROPE)
    q_nope, q_pe = q[..., :MLA_NOPE], q[..., MLA_NOPE:]
    q_pe = apply_rope(q_pe, cos[:, :, None, :], sin[:, :, None, :])
    kv = (rmsnorm(kv_lat, kv_norm_g) @ w_ukv).reshape(B, S, MLA_HEADS, MLA_NOPE + MLA_VDIM)
    k_nope, v = kv[..., :MLA_NOPE], kv[..., MLA_NOPE:]
    k_pe = apply_rope(k_rope, cos, sin)
    k_pe = jnp.broadcast_to(k_pe[:, :, None, :], (B, S, MLA_HEADS, MLA_ROPE))
    q_full = jnp.concatenate([q_nope, q_pe], axis=-1)
    k_full = jnp.concatenate([k_nope, k_pe], axis=-1)
    return chunk_causal_attention(q_full, k_full, v).reshape(B, S, MLA_WIDTH)


def gmlp_mixer(p, ln_g, ln_b, w_s, b_s):
    B, S, _ = p.shape
    u, v = jnp.split(jax.nn.gelu(p, approximate=False), 2, axis=-1)
    v = layernorm(v, ln_g, ln_b)
    pos_chunk = jnp.arange(GM_BLOCK) // CHUNK
    mask = pos_chunk[None, :] <= pos_chunk[:, None]
    w = jnp.where(mask[None], w_s, 0)
    vb = v.reshape(B, S // GM_BLOCK, GM_BLOCK, GM_GROUPS, GM_GROUP_CH)
    s = jnp.einsum('gts,bnsgc->bntgc', w, vb) + b_s.T[None, None, :, :, None]
    return u * s.reshape(B, S, GM_WIDTH)


def wkv7_scan(r, decay, k, v, a, b):
    B, S, H, N = r.shape
    xs = tuple(jnp.moveaxis(t.astype(jnp.float32), 1, 0) for t in (r, decay, k, v, a, b))

    def step(state, inp):
        r_t, w_t, k_t, v_t, a_t, b_t = inp
        sa = jnp.einsum('bhij,bhj->bhi', state, a_t)
        state = (state * w_t[:, :, None, :] + sa[..., None] * b_t[:, :, None, :]
                 + v_t[..., None] * k_t[:, :, None, :])
        return state, jnp.einsum('bhij,bhj->bhi', state, r_t)

    s0 = jnp.zeros((B, H, N, N), jnp.float32)
    _, ys = lax.scan(step, s0, xs)
    return jnp.moveaxis(ys, 0, 1).astype(r.dtype)


def rwkv7_mixer(p, mu, w0, w2, a0, a2, k_k, k_a, r_k, lnx_g, lnx_b, v_first, v_mix):
    B, S, _ = p.shape
    p_prev = jnp.pad(p, ((0, 0), (1, 0), (0, 0)))[:, :-1]
    xs = p + (p_prev - p) * mu
    r, k, v, w_lat, a_lat = split_last(xs, [RW_WIDTH, RW_WIDTH, RW_WIDTH, RW_DECAY_LORA, RW_A_LORA])
    w_log = -jax.nn.softplus(-(w0 + jnp.tanh(w_lat) @ w2).astype(jnp.float32)) - 0.5
    decay = jnp.exp(-jnp.exp(w_log))
    if v_mix is None:
        v_first = v
    else:
        v0, v1, v2 = v_mix
        v = v + (v_first - v) * jax.nn.sigmoid(v0 + (v @ v1) @ v2)
    a = jax.nn.sigmoid(a0 + a_lat @ a2)

    def heads(t):
        return t.reshape(B, S, RW_HEADS, RW_HEAD)

    kk = heads(k * k_k).astype(jnp.float32)
    kk = kk / jnp.maximum(jnp.sqrt(jnp.sum(kk * kk, axis=-1, keepdims=True)), 1e-12)
    k = k * (1 + (a - 1) * k_a)
    rh, kh, vh, ah = heads(r), heads(k), heads(v), heads(a)
    y = wkv7_scan(rh, heads(decay), kh, vh, -kk, kk * ah.astype(jnp.float32))
    y = layernorm(y, lnx_g.reshape(RW_HEADS, RW_HEAD), lnx_b.reshape(RW_HEADS, RW_HEAD), eps=RW_LN_EPS)
    bonus = jnp.sum(rh * kh * r_k, axis=-1, keepdims=True) * vh
    return (y + bonus).reshape(B, S, RW_WIDTH), v_first


def setup_inputs(seed: int = 0) -> dict:
    key = jax.random.key(seed)
    ks = jax.random.split(key, 32)
    L = DEPTH
    Lv = max(DEPTH - 1, 0)
    D = D_MODEL

    def nrm(i, shape, scale):
        return jax.random.normal(ks[i], shape, jnp.float32) * scale

    x = nrm(0, (BATCH, SEQ, D), 1.0)
    c = nrm(1, (BATCH, D), 1.0)
    offset = jax.random.randint(ks[2], (BATCH, 1), 0, MAX_POS_OFFSET, dtype=jnp.int32)
    positions = offset + jnp.arange(SEQ, dtype=jnp.int32)[None, :]
    return {
        "x": x,
        "c": c,
        "positions": positions,
        "pre_g": 1.0 + nrm(3, (L, D), 0.02),
        "post_g": 1.0 + nrm(4, (L, D), 0.02),
        "w_ada": nrm(5, (L, D, 3 * D), 0.5 * D ** -0.5),
        "b_ada": nrm(6, (L, 3 * D), 0.02),
        "w_in": nrm(7, (L, D, D_IN), D ** -0.5),
        "mla_q_norm": 1.0 + nrm(8, (L, MLA_Q_RANK), 0.02),
        "mla_w_uq": nrm(9, (L, MLA_Q_RANK, MLA_HEADS * (MLA_NOPE + MLA_ROPE)), MLA_Q_RANK ** -0.5),
        "mla_kv_norm": 1.0 + nrm(10, (L, MLA_KV_RANK), 0.02),
        "mla_w_ukv": nrm(11, (L, MLA_KV_RANK, MLA_HEADS * (MLA_NOPE + MLA_VDIM)), MLA_KV_RANK ** -0.5),
        "gm_ln_g": 1.0 + nrm(12, (L, GM_WIDTH), 0.02),
        "gm_ln_b": nrm(13, (L, GM_WIDTH), 0.02),
        "gm_w_s": nrm(14, (L, GM_GROUPS, GM_BLOCK, GM_BLOCK), GM_BLOCK ** -0.5),
        "gm_b_s": 1.0 + nrm(15, (L, GM_GROUPS, GM_BLOCK), 0.02),
        "rw_mu": jax.random.uniform(ks[16], (L, RW_IN), jnp.float32),
        "rw_w0": jax.random.uniform(ks[17], (L, RW_WIDTH), jnp.float32, minval=-3.0, maxval=0.5),
        "rw_w2": nrm(18, (L, RW_DECAY_LORA, RW_WIDTH), 0.1),
        "rw_a0": nrm(19, (L, RW_WIDTH), 0.1),
        "rw_a2": nrm(20, (L, RW_A_LORA, RW_WIDTH), RW_A_LORA ** -0.5),
        "rw_k_k": 0.85 + nrm(21, (L, RW_WIDTH), 0.02),
        "rw_k_a": 1.0 + nrm(22, (L, RW_WIDTH), 0.02),
        "rw_r_k": nrm(23, (L, RW_HEADS, RW_HEAD), 0.1),
        "rw_lnx_g": 1.0 + nrm(24, (L, RW_WIDTH), 0.02),
        "rw_lnx_b": nrm(25, (L, RW_WIDTH), 0.02),
        "rw_v0": 1.0 + nrm(26, (Lv, RW_WIDTH), 0.1),
        "rw_v1": nrm(27, (Lv, RW_WIDTH, RW_V_LORA), RW_WIDTH ** -0.5),
        "rw_v2": nrm(28, (Lv, RW_V_LORA, RW_WIDTH), 0.5 * RW_V_LORA ** -0.5),
        "w_br": nrm(29, (L, N_BRANCH, BRANCH_WIDTH, D), BRANCH_WIDTH ** -0.5),
        "w_out": nrm(30, (L, D, D), D ** -0.5),
    }


def reference(x, c, positions, pre_g, post_g, w_ada, b_ada, w_in,
              mla_q_norm, mla_w_uq, mla_kv_norm, mla_w_ukv,
              gm_ln_g, gm_ln_b, gm_w_s, gm_b_s,
              rw_mu, rw_w0, rw_w2, rw_a0, rw_a2, rw_k_k, rw_k_a, rw_r_k,
              rw_lnx_g, rw_lnx_b, rw_v0, rw_v1, rw_v2,
              w_br, w_out):
    B, S, D = x.shape
    cos, sin = rope_angles(positions)
    c_act = jax.nn.silu(c)
    v_first = None
    for l in range(DEPTH):
        shift, scale, gate = jnp.split(c_act @ w_ada[l] + b_ada[l], 3, axis=-1)
        h = rmsnorm(x, pre_g[l]) * (1 + scale[:, None]) + shift[:, None]
        q_lat, kv_lat, k_rope, gm_in, rw_in, z, g_logit = split_last(h @ w_in[l], IN_SPLITS)

        y_mla = mla_mixer(q_lat, kv_lat, k_rope, mla_q_norm[l], mla_w_uq[l],
                          mla_kv_norm[l], mla_w_ukv[l], cos, sin)
        y_gm = gmlp_mixer(gm_in, gm_ln_g[l], gm_ln_b[l], gm_w_s[l], gm_b_s[l])
        v_mix = None if l == 0 else (rw_v0[l - 1], rw_v1[l - 1], rw_v2[l - 1])
        y_rw, v_first = rwkv7_mixer(rw_in, rw_mu[l], rw_w0[l], rw_w2[l], rw_a0[l], rw_a2[l],
                                    rw_k_k[l], rw_k_a[l], rw_r_k[l], rw_lnx_g[l], rw_lnx_b[l],
                                    v_first, v_mix)

        branches = jnp.stack([y_mla, y_gm, y_rw], axis=2)
        branches = branches * jax.nn.silu(z).reshape(B, S, N_BRANCH, BRANCH_WIDTH)
        proj = jnp.einsum('bsnw,nwd->bsnd', branches, w_br[l])
        merged = jnp.sum(proj * jax.nn.sigmoid(g_logit).reshape(B, S, N_BRANCH, D), axis=2)
        y = merged @ w_out[l]
        x = x + gate[:, None] * rmsnorm(y, post_g[l])
    return x
```

```python
import numpy as np
import concourse.bass as bass
import concourse.mybir as mybir
from concourse.bass_utils import run_bass_kernel_spmd

F32 = mybir.dt.float32
BF16 = mybir.dt.bfloat16
I32 = mybir.dt.int32
AF = mybir.ActivationFunctionType
ALU = mybir.AluOpType
AX = mybir.AxisListType

ENGS = ("pe", "dve", "act", "pool", "sp")
EPOCH = 30000
NDMA = 12


class V:
    __slots__ = ("ap", "key")

    def __init__(self, ap, key):
        self.ap = ap
        self.key = key

    def __getitem__(self, idx):
        return V(self.ap[idx], self.key)

    def k(self, sub):
        return V(self.ap, (self.key, sub))

    def re(self, pat, **kw):
        return V(self.ap.rearrange(pat, **kw), self.key)

    def bc(self, shape):
        return V(self.ap.to_broadcast(shape), self.key)

    def with_ap(self, ap):
        return V(ap, self.key)


class Prog:
    def __init__(self, nc):
        self.nc = nc
        self.engobj = {"pe": nc.tensor, "dve": nc.vector, "act": nc.scalar,
                       "pool": nc.gpsimd, "sp": nc.sync}
        self.stream = {e: [] for e in ENGS}
        self.cnt = {e: 0 for e in ENGS}
        self.seen = {e: {} for e in ENGS}
        self.last_w = {}
        self.readers = {}
        self.sems = {}
        self.dma_slot_uses = {}
        self.dma_rr = {e: 0 for e in ENGS}
        self.ntile = 0
        self.out_events = []

    def sb(self, shape, dt, name=None):
        self.ntile += 1
        name = name or f"t{self.ntile}"
        nm = f"{name}_{self.ntile}"
        if getattr(self, "phase_guards", None) is not None:
            g = self.nc.sbuf_tensor(nm, list(shape), dt)
            h = g.__enter__()
            self.phase_guards.append(g)
        else:
            h = self.nc.alloc_sbuf_tensor(nm, list(shape), dt)
        return V(h.ap(), nm)

    def phase_begin(self):
        if not hasattr(self, "phase_stack"):
            self.phase_stack = []
        self.phase_stack.append(getattr(self, "phase_guards", None))
        self.phase_guards = []

    def phase_end(self):
        self.barrier()
        for g in reversed(self.phase_guards):
            g.__exit__(None, None, None)
        self.phase_guards = self.phase_stack.pop()

    def barrier(self):
        evs = []
        for e2 in ENGS:
            n = self.cnt[e2]
            if n > 0:
                epoch, idx = divmod(n - 1, EPOCH)
                evs.append((("e", e2, epoch), idx + 1))
        for sk, uses in self.dma_slot_uses.items():
            evs.append((sk, 16 * uses))
        for e in ENGS:
            for ev in evs:
                self._need(e, ev)
        self.last_w = {}
        self.readers = {}

    def ps(self, shape, dt=F32, name=None):
        self.ntile += 1
        name = name or f"p{self.ntile}"
        h = self.nc.alloc_psum_tensor(f"{name}_{self.ntile}", list(shape), dt)
        return V(h.ap() if hasattr(h, "ap") else h[:], f"{name}_{self.ntile}")

    def dram(self, name, shape, dt, kind="Internal"):
        h = self.nc.dram_tensor(name, list(shape), dt, kind=kind)
        return V(h.ap(), "dram_" + name)

    def _sem(self, key):
        if key not in self.sems:
            self.sems[key] = self.nc.alloc_semaphore("s_" + "_".join(str(x) for x in key))
        return self.sems[key]

    def _need(self, eng, ev):
        if ev is None:
            return
        sk, val = ev
        if self.seen[eng].get(sk, 0) >= val:
            return
        self.seen[eng][sk] = val
        self.stream[eng].append(("wait", sk, val))

    def _deps(self, eng, reads, writes, self_sk):
        for r in reads:
            ev = self.last_w.get(r.key)
            if ev is not None:
                if ev[0] == self_sk and eng == "pe":
                    continue
                self._need(eng, ev)
        for w in writes:
            ev = self.last_w.get(w.key)
            if ev is not None and ev[0] != self_sk:
                self._need(eng, ev)
            for rv in self.readers.get(w.key, ()):
                if rv[0] != self_sk:
                    self._need(eng, rv)

    def _commit(self, ev, reads, writes):
        for r in reads:
            self.readers.setdefault(r.key, []).append(ev)
            lst = self.readers[r.key]
            if len(lst) > 24:
                d = {}
                for sk, v in lst:
                    d[sk] = max(d.get(sk, 0), v)
                self.readers[r.key] = list(d.items())
        for w in writes:
            self.last_w[w.key] = ev
            self.readers[w.key] = []

    def op(self, eng, fn, reads=(), writes=()):
        n = self.cnt[eng]
        epoch, idx = divmod(n, EPOCH)
        sk = ("e", eng, epoch)
        self._deps(eng, reads, writes, sk)
        ev = (sk, idx + 1)
        self.stream[eng].append(("op", fn, sk, 1))
        self.cnt[eng] = n + 1
        self._commit(ev, reads, writes)
        return ev

    def dma(self, q, out, in_, is_output=False):
        slot = self.dma_rr[q] % NDMA
        self.dma_rr[q] += 1
        sk = ("d", q, slot)
        uses = self.dma_slot_uses.get(sk, 0)
        if uses > 0:
            self._need(q, (sk, 16 * uses))
        self._deps(q, [in_], [out], sk)
        ev = (sk, 16 * (uses + 1))
        self.dma_slot_uses[sk] = uses + 1
        o_ap, i_ap = out.ap, in_.ap
        self.stream[q].append(("op", lambda e: e.dma_start(out=o_ap, in_=i_ap), sk, 16))
        self._commit(ev, [in_], [out])
        if is_output:
            self.out_events.append(ev)
        return ev

    def finish(self):
        for ev in self.out_events:
            self._need("sp", ev)

    def emit(self):
        self.finish()
        nc = self.nc
        for sk in set(x[1] for e in ENGS for x in self.stream[e] if x[0] == "wait") | \
                set(x[2] for e in ENGS for x in self.stream[e] if x[0] == "op"):
            self._sem(sk)
        with nc.Block() as block:
            def run(e):
                def body(engine):
                    for item in self.stream[e]:
                        if item[0] == "wait":
                            engine.wait_ge(self.sems[item[1]], item[2])
                        else:
                            inst = item[1](engine)
                            inst.then_inc(self.sems[item[2]], item[3])
                return body
            block.tensor(run("pe"))
            block.vector(run("dve"))
            block.scalar(run("act"))
            block.gpsimd(run("pool"))
            block.sync(run("sp"))

    def mm(self, out, lhsT, rhs, start=True, stop=True):
        o, l, r = out.ap, lhsT.ap, rhs.ap
        return self.op("pe", lambda e: e.matmul(o, l, r, start=start, stop=stop),
                       reads=[lhsT, rhs], writes=[out])

    def tr(self, out, in_, ident):
        o, i, d = out.ap, in_.ap, ident.ap
        return self.op("pe", lambda e: e.transpose(o, i, d), reads=[in_, ident], writes=[out])

    def act(self, out, in_, func, bias=None, scale=1.0, accum=None, eng="act"):
        o, i = out.ap, in_.ap
        reads = [in_]
        kw = {}
        if isinstance(bias, V):
            reads.append(bias); kw["bias"] = bias.ap
        elif bias is not None:
            kw["bias"] = bias
        if isinstance(scale, V):
            reads.append(scale); kw["scale"] = scale.ap
        else:
            kw["scale"] = scale
        writes = [out]
        if accum is not None:
            writes.append(accum); kw["accum_out"] = accum.ap
        return self.op(eng, lambda e: e.activation(out=o, in_=i, func=func, **kw),
                       reads=reads, writes=writes)

    def tt(self, out, a, b, op, eng="dve"):
        o, x, y = out.ap, a.ap, b.ap
        return self.op(eng, lambda e: e.tensor_tensor(out=o, in0=x, in1=y, op=op),
                       reads=[a, b], writes=[out])

    def ts(self, out, a, s1, op0, s2=None, op1=None, eng="dve", accum=None):
        o, x = out.ap, a.ap
        reads = [a]
        s1v = s1.ap if isinstance(s1, V) else s1
        s2v = s2.ap if isinstance(s2, V) else s2
        if isinstance(s1, V): reads.append(s1)
        if isinstance(s2, V): reads.append(s2)
        kw = {}
        writes = [out]
        if op1 is not None:
            kw["op1"] = op1
        if accum is not None:
            kw["accum_out"] = accum.ap; writes.append(accum)
        return self.op(eng, lambda e: e.tensor_scalar(out=o, in0=x, scalar1=s1v, scalar2=s2v, op0=op0, **kw),
                       reads=reads, writes=writes)

    def stt(self, out, a, s, b, op0, op1, eng="dve"):
        o, x, y = out.ap, a.ap, b.ap
        reads = [a, b]
        sv = s.ap if isinstance(s, V) else s
        if isinstance(s, V): reads.append(s)
        return self.op(eng, lambda e: e.scalar_tensor_tensor(out=o, in0=x, scalar=sv, in1=y, op0=op0, op1=op1),
                       reads=reads, writes=[out])

    def copy(self, out, in_, eng="dve"):
        o, i = out.ap, in_.ap
        if eng == "act":
            return self.op(eng, lambda e: e.copy(out=o, in_=i), reads=[in_], writes=[out])
        return self.op(eng, lambda e: e.tensor_copy(out=o, in_=i), reads=[in_], writes=[out])

    def memset(self, out, val, eng="dve"):
        o = out.ap
        return self.op(eng, lambda e: e.memset(o, val), reads=[], writes=[out])

import math

TT = 512
CH = 64
D = 1024
KC = 8
EPS = 1e-6
C0 = math.exp(-0.5)
SCALE = 96 ** -0.5
NCH_W = 49


class RR:
    def __init__(self, mk, n):
        self.t = [mk(i) for i in range(n)]
        self.i = 0

    def get(self):
        t = self.t[self.i % len(self.t)]
        self.i += 1
        return t


W_GROUPS = [
    (0, 4, "lat"), (4, 2, "gmu"), (6, 4, "gmv"), (10, 4, "rwrk"), (14, 4, "rwv"),
    (18, 1, "rwl"), (19, 4, "z0"), (23, 2, "z1"),
]
G_BASE = 25


DEBUG_STOP = None
DEBUG_FLAGS = {}


def build(T, do_pro, do_body, first):
    nc = bass.Bass("TRN2", target_bir_lowering=False)
    P = Prog(nc)
    NT = T // TT
    NB = T // 128
    din = lambda name, shape, dt=F32: P.dram(name, shape, dt, kind="ExternalInput")
    dout = lambda name, shape, dt=F32: P.dram(name, shape, dt, kind="ExternalOutput")

    xT = din("xT", [D, T])
    c_ident = din("c_ident", [128, 128])
    c_onesblk = din("c_onesblk", [128, 128])
    cvec = din("cvec", [128, 8])

    ident = P.sb([128, 128], F32, "ident")
    onesblk = P.sb([128, 128], F32, "onesblk")
    ones = P.sb([128, 128], F32, "ones")
    P.dma("sp", ident, c_ident)
    P.dma("sp", onesblk, c_onesblk)
    P.memset(ones, 1.0)
    cact2 = P.sb([128, 8, 2], F32, "cact2")
    ctmp = P.sb([128, 8], F32, "ctmp")
    P.dma("sp", ctmp, cvec)
    P.act(cact2[:, :, 0], ctmp, AF.Silu)
    P.act(cact2[:, :, 1], ctmp, AF.Silu)

    pp = RR(lambda i: P.ps([128, 512], F32, f"pp{i}"), 2)
    pa = RR(lambda i: P.ps([128, 512], F32, f"pa{i}"), 2)
    po_t = P.ps([128, 512], F32, "po")
    px = RR(lambda i: P.ps([128, 512], F32, f"px{i}"), 3)

    eps_sb = P.sb([128, 4], F32, "eps")
    P.memset(eps_sb[:, 0:1], EPS)
    P.memset(eps_sb[:, 1:2], 1e-5)
    P.memset(eps_sb[:, 2:3], 64e-5)
    P.memset(eps_sb[:, 3:4], 0.0)
    cur = {}

    def mkpools(nsq=2, nrstd=1, nscr=0, nwst=0):
        cur["sq"] = RR(lambda i: P.sb([128, TT], F32, f"sq{i}"), nsq)
        cur["rstd"] = RR(lambda i: P.sb([128, TT], F32, f"rstd{i}"), nrstd) if nrstd else None
        cur["scr"] = RR(lambda i: P.sb([128, TT], F32, f"scr{i}"), nscr) if nscr else None
        cur["wst"] = RR(lambda i: P.sb([128, 8, 512], BF16, f"wst{i}"), nwst) if nwst else None

    def recip(v):
        P.op("dve", (lambda o, i: (lambda e: e.reciprocal(out=o, in_=i)))(v.ap, v.ap), reads=[v], writes=[v])

    def rms_rstd(chunks, n_feat):
        pt = px.get()
        n = len(chunks)
        for i, cv in enumerate(chunks):
            sq = cur["sq"].get()
            P.act(sq, cv, AF.Square)
            P.mm(pt[:, 0:TT], ones, sq, start=(i == 0), stop=(i == n - 1))
        r = cur["rstd"].get()
        P.act(r, pt[:, 0:TT], AF.Sqrt, bias=eps_sb[:, 0:1], scale=1.0 / n_feat)
        recip(r)
        return r

    if do_pro:
        yA = din("yA", [D, T]); yB = din("yB", [D, T])
        wada_p = din("wada_p", [D, D]); bada_p = din("bada_p", [128, 8]); postg_p = din("postg_p", [128, 8])
        xT_out = dout("xT_out", [D, T])
        gp = P.sb([128, 8], F32, "gp")
    if do_body:
        wcat = din("wcat", [D, NCH_W * 128])
        wada_b = din("wada_b", [D, 2 * D]); bada_b = din("bada_b", [128, 16]); preg = din("preg", [128, 8])
        vecs = din("vecs", [128, 40])
        pos_in = din("pos", [64, T], I32)
        invf = din("invf", [64, 2])
        wuq = din("wuq", [256, 4 * 192]); wukvk = din("wukvk", [128, 512]); wukvv = din("wukvv", [128, 256])
        lng = din("lng", [1, 512]); lnb = din("lnb", [1, 512])
        wsT = din("wsT", [128, 2, 128]); bs = din("bs", [1, 2, 128])
        wa2 = din("wa2", [128, 256]); rkblk = din("rkblk", [128, 2, 2])
        lxg = din("lxg", [1, 256]); lxb = din("lxb", [1, 256])
        c_maskS = din("c_maskS", [64, 128]); c_maskL = din("c_maskL", [64, 64])
        wbr = din("wbr", [768, D]); wout = din("wout", [D, D])
        yP = dout("yP", [D, T])
        if first:
            vf_out = dout("vf_out", [256, T])
        else:
            vf_in = din("vf_in", [256, T])
            v1 = din("v1", [512, 32]); v2 = din("v2", [32, 256])
        sc1 = P.sb([128, 8], F32, "sc1")
        ss_ = P.sb([128, 16], F32, "ss_")
        shift = ss_[:, 0:8]

    P.phase_begin()
    wada_pool = RR(lambda i: P.sb([128, 8, 128], F32, f"wada{i}"), 2)

    def adaln(w_dram, bias_sb, nch, out_sb):
        for oc in range(nch):
            wt = wada_pool.get()
            P.dma("sp", wt, w_dram[:, oc * 128:(oc + 1) * 128].re("(kc p) c -> p kc c", p=128))
            pt = px.get()
            for kc in range(8):
                P.mm(pt[:, 0:2], wt[:, kc, :], cact2[:, kc, :], start=(kc == 0), stop=(kc == 7))
            P.tt(out_sb[:, oc:oc + 1], pt[:, 0:1], bias_sb[:, oc:oc + 1], ALU.add)

    if do_pro:
        bp = P.sb([128, 8], F32, "bp"); pg = P.sb([128, 8], F32, "pg"); gate = P.sb([128, 8], F32, "gate")
        P.dma("sp", bp, bada_p); P.dma("sp", pg, postg_p)
        adaln(wada_p, bp, 8, gate)
        P.tt(gp, gate, pg, ALU.mult)
    if do_body:
        bb = P.sb([128, 16], F32, "bb"); P.dma("sp", bb, bada_b)
        pgb = P.sb([128, 8], F32, "pgb"); P.dma("sp", pgb, preg)
        adaln(wada_b, bb, 16, ss_)
        P.stt(sc1, ss_[:, 8:16], 1.0, pgb, ALU.add, ALU.mult)
    P.phase_end()

    if do_body:
        vec = P.sb([128, 40], F32, "vec"); P.dma("sp", vec, vecs)
        omka = P.sb([128, 2], F32, "omka")
        P.ts(omka, vec[:, 18:20], -1.0, ALU.mult, 1.0, ALU.add)
        maskS = P.sb([64, 128], F32, "maskS"); P.dma("sp", maskS, c_maskS)
        maskL = P.sb([64, 64], F32, "maskL"); P.dma("sp", maskL, c_maskL)

        def load_bf(dram_view, shape, name):
            t = P.sb(shape, BF16, name)
            P.dma("pool", t, dram_view)
            return t

        wuq_sb = load_bf(wuq.re("(kc p) c -> p kc c", p=128), [128, 2, 768], "wuq")
        wukvk_sb = load_bf(wukvk, [128, 512], "wukvk")
        wukvv_sb = load_bf(wukvv, [128, 256], "wukvv")
        wsT_sb = load_bf(wsT, [128, 2, 128], "wsT")
        P.memset(wsT_sb[64:128, :, 0:64], 0.0, eng="pool")
        bs_sb = P.sb([1, 2, 128], F32, "bs"); P.dma("sp", bs_sb, bs)
        wa2_sb = P.sb([128, 256], F32, "wa2"); P.dma("sp", wa2_sb, wa2)
        rkblk_sb = P.sb([128, 2, 2], F32, "rkblk"); P.dma("sp", rkblk_sb, rkblk)
        LG = P.sb([128, 256], F32, "LG"); P.dma("sp", LG, lng[:, 0:256].with_ap(lng[:, 0:256].ap.partition_broadcast(128)))
        LBt = P.sb([128, 256], F32, "LB"); P.dma("sp", LBt, lnb[:, 0:256].with_ap(lnb[:, 0:256].ap.partition_broadcast(128)))
        LXG = P.sb([64, 256], F32, "LXG"); P.dma("sp", LXG, lxg.with_ap(lxg.ap.partition_broadcast(64)))
        LXB = P.sb([64, 256], F32, "LXB"); P.dma("sp", LXB, lxb.with_ap(lxb.ap.partition_broadcast(64)))
        wbr_sb = load_bf(wbr.re("(j p) c -> p j c", p=128), [128, 6, D], "wbr")
        wout_sb = load_bf(wout.re("(j p) c -> p j c", p=128), [128, 8, D], "wout")
        if not first:
            v1_sb = P.sb([128, 4, 32], F32, "v1"); P.dma("sp", v1_sb, v1.re("(j p) c -> p j c", p=128))
            v2_sb = P.sb([32, 256], F32, "v2"); P.dma("sp", v2_sb, v2)
        invf_sb = P.sb([64, 2], F32, "invf"); P.dma("sp", invf_sb, invf)
        TWO_PI = 2.0 * math.pi
        Kc = P.sb([128, 4, T], BF16, "Kc")
        Vc = P.sb([128, NB, 4, 65], BF16, "Vc")
        P.memset(Kc[0:32, :, :], 0.0, eng="pool")
        P.memset(Kc[0:1, :, :], 1.0, eng="pool")
        P.memset(Vc[:, :, :, 64:65], 1.0, eng="pool")
        kmax2 = P.sb([1, 4], F32, "kmax2"); P.memset(kmax2, 0.0)
        Hst = P.sb([64, 4, 64], F32, "Hst"); P.memset(Hst, 0.0)
        rwlast = P.sb([128, 9, 1], F32, "rwlast"); P.memset(rwlast, 0.0)
        hT = P.sb([128, 8, TT], BF16, "hT")
        zs = P.sb([128, 6, TT], BF16, "zs")
        brT = P.sb([128, 6, TT], BF16, "brT")
        small = P.sb([128, 64], F32, "small")
        st6 = P.sb([128, 4, 6], F32, "st6"); mv = P.sb([128, 4, 2], F32, "mv"); rs4 = P.sb([128, 4], F32, "rs4")
        nm4 = P.sb([128, 4], F32, "nm4")

        def wload(c_start, n):
            w = cur["wst"].get()
            P.dma("pool", w[:, :, 0:n * 128],
                  wcat[:, c_start * 128:(c_start + n) * 128].re("(kc p) c -> p kc c", p=128))
            return w

        def proj_fm(w, j, M0=0, M1=128):
            pt = pp.get()
            for kc in range(8):
                P.mm(pt[0:M1 - M0, 0:TT], w[:, kc, j * 128 + M0:j * 128 + M1], hT[:, kc, :],
                     start=(kc == 0), stop=(kc == 7))
            return pt
    P.barrier()

    for ti in range(NT):
        c0t = ti * TT
        P.phase_begin()
        mkpools(nsq=2, nrstd=1)
        xt = P.sb([128, 8, TT], F32, "xt")
        P.dma("sp", xt, xT[:, c0t:c0t + TT].re("(kc p) t -> p kc t", p=128))
        if do_pro:
            ya_p = RR(lambda i: P.sb([128, TT], F32, f"ya{i}"), 2)
            yb_p = RR(lambda i: P.sb([128, TT], F32, f"yb{i}"), 2)

            def ysum_chunk(kc):
                ya = ya_p.get(); yb = yb_p.get()
                P.dma("act", ya, yA[kc * 128:(kc + 1) * 128, c0t:c0t + TT])
                P.dma("act", yb, yB[kc * 128:(kc + 1) * 128, c0t:c0t + TT])
                P.tt(ya, ya, yb, ALU.add, eng="pool")
                return ya
            pt = px.get()
            for kc in range(8):
                ys_ = ysum_chunk(kc)
                sq = cur["sq"].get()
                P.act(sq, ys_, AF.Square)
                P.mm(pt[:, 0:TT], ones, sq, start=(kc == 0), stop=(kc == 7))
            r = cur["rstd"].get()
            P.act(r, pt[:, 0:TT], AF.Sqrt, bias=eps_sb[:, 0:1], scale=1.0 / D)
            recip(r)
            for kc in range(8):
                ys_ = ysum_chunk(kc)
                P.tt(ys_, ys_, r, ALU.mult)
                P.stt(xt[:, kc, :], ys_, gp[:, kc:kc + 1], xt[:, kc, :], ALU.mult, ALU.add)
            P.dma("sp", xT_out[:, c0t:c0t + TT].re("(kc p) t -> p kc t", p=128), xt, is_output=True)
        if not do_body:
            P.phase_end()
            continue
        r = rms_rstd([xt[:, kc, :] for kc in range(8)], D)
        for kc in range(8):
            t = cur["sq"].get()
            P.tt(t, xt[:, kc, :], r, ALU.mult)
            P.ts(hT[:, kc, :], t, sc1[:, kc:kc + 1], ALU.mult, shift[:, kc:kc + 1], ALU.add)
        P.phase_end()

        if DEBUG_STOP == 0:
            break
        P.phase_begin()
        mkpools(nsq=2, nrstd=1, nscr=3, nwst=2)
        scr = cur["scr"]
        lat = P.sb([128, 3, TT], F32, "lat")
        kr = P.sb([64, 2, TT], F32, "kr")
        qn = P.sb([128, 2, TT], BF16, "qn")
        kvn = P.sb([128, TT], BF16, "kvn")
        pt_pool = RR(lambda i: P.sb([128, TT], BF16, f"ptp{i}"), 3)
        attn_tok = P.sb([128, 4, 256], F32, "attn_tok")
        CC = P.sb([64, TT], F32, "CC"); SSt = P.sb([64, TT], F32, "SS")
        ki = P.sb([64, TT], I32, "ki")
        qT = [P.sb([128, TT], BF16, f"qT{h}") for h in range(4)]
        for h in range(4):
            P.memset(qT[h][0:32, :], 0.0, eng="pool")
        ksq = P.sb([128, TT], F32, "ksq"); P.memset(ksq[0:32, :], 0.0)
        ang = cur["sq"].get()
        a64 = ang[0:64, :]
        P.dma("sp", ki, pos_in[:, c0t:c0t + TT])
        P.copy(a64, ki)
        P.ts(a64, a64, invf_sb[:, 0:1], ALU.mult)
        kq = cur["rstd"].get()[0:64, :]
        P.ts(kq, a64, 1.0 / TWO_PI, ALU.mult)
        P.copy(ki, kq)
        P.copy(kq, ki)
        P.stt(a64, kq, -6.28125, a64, ALU.mult, ALU.add)
        P.stt(a64, kq, -(TWO_PI - 6.28125), a64, ALU.mult, ALU.add)
        m = cur["sq"].get()[0:64, :]
        P.ts(m, a64, math.pi, ALU.is_gt, -TWO_PI, ALU.mult)
        P.tt(a64, a64, m, ALU.add)
        P.ts(m, a64, -math.pi, ALU.is_lt, TWO_PI, ALU.mult)
        P.tt(a64, a64, m, ALU.add)
        P.act(SSt, a64, AF.Sin)
        P.ts(SSt, SSt, invf_sb[:, 1:2], ALU.mult)
        P.ts(a64, a64, math.pi / 2, ALU.add)
        P.ts(m, a64, math.pi, ALU.is_gt, -TWO_PI, ALU.mult)
        P.tt(a64, a64, m, ALU.add)
        P.act(CC, a64, AF.Sin)
        w = wload(0, 4)
        for j in range(3):
            pt = proj_fm(w, j)
            P.copy(lat[:, j, :], pt[:, 0:TT], eng="act")
        for s in range(2):
            pt = proj_fm(w, 3, 64 * s, 64 * s + 64)
            P.copy(kr[:, s, :], pt[0:64, 0:TT], eng="act")
        w = wload(19, 4)
        for j in range(4):
            pt = proj_fm(w, j)
            P.act(zs[:, j, :], pt[:, 0:TT], AF.Silu)
        w = wload(23, 2)
        for j in range(2):
            pt = proj_fm(w, j)
            P.act(zs[:, 4 + j, :], pt[:, 0:TT], AF.Silu)
        r = rms_rstd([lat[:, 0, :], lat[:, 1, :]], 256)
        for j in range(2):
            t = cur["sq"].get()
            P.tt(t, lat[:, j, :], r, ALU.mult)
            P.ts(qn[:, j, :], t, vec[:, 9 + j:10 + j], ALU.mult)
        r = rms_rstd([lat[:, 2, :]], 128)
        t = cur["sq"].get()
        P.tt(t, lat[:, 2, :], r, ALU.mult)
        P.ts(kvn, t, vec[:, 11:12], ALU.mult)
        krot = scr.get()
        t1 = scr.get()
        P.tt(krot[32:64, :], kr[32:64, 0, :], CC[32:64, :], ALU.mult)
        P.tt(t1[32:64, :], kr[32:64, 1, :], SSt[32:64, :], ALU.mult)
        P.tt(krot[32:64, :], krot[32:64, :], t1[32:64, :], ALU.add)
        P.act(ksq[32:64, :], krot[32:64, :], AF.Square)
        for h in range(4):
            P.copy(Kc[32:64, h, c0t:c0t + TT], krot[32:64, :], eng="pool")
            pt = pp.get()
            P.mm(pt[:, 0:TT], wukvk_sb[:, h * 128:(h + 1) * 128], kvn)
            P.copy(Kc[64:128, h, c0t:c0t + TT], pt[64:128, 0:TT], eng="act")
            P.act(ksq[64:128, :], pt[64:128, 0:TT], AF.Square)
            p1 = px.get()
            P.mm(p1[0:1, 0:TT], ones[:, 0:1], ksq)
            P.op("dve", (lambda o, i: (lambda e: e.tensor_reduce(out=o, in_=i, op=ALU.max, axis=AX.X)))(small[0:1, h:h + 1].ap, p1[0:1, 0:TT].ap),
                 reads=[p1], writes=[small])
            P.tt(kmax2[0:1, h:h + 1], kmax2[0:1, h:h + 1], small[0:1, h:h + 1], ALU.max)
        for tb in range(4):
            pt = pp.get()
            P.mm(pt[:, 0:256], kvn[:, tb * 128:(tb + 1) * 128], wukvv_sb)
            blk = ti * 4 + tb
            P.copy(Vc[:, blk, :, 0:64], pt[:, 0:256].re("p (h d) -> p h d", h=4), eng="act")
        for h in range(4):
            pq = pp.get()
            for j in range(2):
                P.mm(pq[:, 0:TT], wuq_sb[:, j, h * 192:h * 192 + 128], qn[:, j, :], start=(j == 0), stop=(j == 1))
            psw = pp.get()
            for j in range(2):
                P.mm(psw[0:64, 0:TT], wuq_sb[:, j, h * 192 + 128:h * 192 + 192], qn[:, j, :], start=(j == 0), stop=(j == 1))
            qf = scr.get()
            P.copy(qf, pq[:, 0:TT], eng="act")
            sq = cur["sq"].get()
            P.act(sq, qf, AF.Square)
            p1 = px.get()
            P.mm(p1[0:1, 0:TT], ones[:, 0:1], sq)
            t1 = scr.get()
            t2 = scr.get()
            P.tt(t1[32:64, :], qf[32:64, :], CC[32:64, :], ALU.mult)
            P.tt(t2[32:64, :], psw[32:64, 0:TT], SSt[32:64, :], ALU.mult)
            P.tt(qT[h][32:64, :], t1[32:64, :], t2[32:64, :], ALU.add)
            P.copy(qT[h][64:128, :], qf[64:128, :], eng="pool")
            P.act(t1[0:1, :], p1[0:1, 0:TT], AF.Sqrt, scale=kmax2[0:1, h:h + 1])
            P.ts(qT[h][0:1, :], t1[0:1, :], -1.0, ALU.mult)
            nkb = 4 * ti + 4
            po = po_t
            first_mm = True
            for kb in range(nkb):
                j = kb - 4 * ti
                cq = 128 * j if j > 0 else 0
                sT = pa.get()
                P.mm(sT[:, cq:TT], Kc[:, h, kb * 128:(kb + 1) * 128], qT[h][:, cq:TT])
                pt_ = pt_pool.get()
                P.act(pt_[:, cq:TT], sT[:, cq:TT], AF.Exp, scale=SCALE)
                if j >= 0:
                    P.memset(pt_[64:128, cq:cq + 64], 0.0, eng="pool")
                for qb in range(4):
                    if 128 * qb < cq:
                        continue
                    o_ap, l_ap, r_ap = po[:, qb * 128:qb * 128 + 65].ap, pt_[:, qb * 128:(qb + 1) * 128].ap, Vc[:, kb, h, :].ap
                    st_, sp_ = first_mm, (kb == nkb - 1 and qb == 3)
                    P.op("pe", (lambda o, l, r_, a, b: (lambda e: e.matmul(o, l, r_, start=a, stop=b, skip_group_check=True)))(o_ap, l_ap, r_ap, st_, sp_),
                         reads=[pt_, Vc], writes=[po])
                    first_mm = False
            rec = small[:, 8:12]
            pov = po[:, 0:512].re("p (q c) -> p q c", q=4)
            P.op("dve", (lambda o, i: (lambda e: e.reciprocal(out=o, in_=i)))(rec.ap, pov[:, :, 64].ap), reads=[po], writes=[small])
            P.tt(attn_tok[:, :, h * 64:(h + 1) * 64], pov[:, :, 0:64], rec.with_ap(rec.ap.unsqueeze(2).to_broadcast([128, 4, 64])), ALU.mult)
        for j in range(2):
            ptp = px.get()
            for qb in range(4):
                P.tr(ptp[:, qb * 128:(qb + 1) * 128], attn_tok[:, qb, j * 128:(j + 1) * 128], ident)
            P.tt(brT[:, j, :], ptp[:, 0:TT], zs[:, j, :], ALU.mult)
        P.phase_end()

        if DEBUG_STOP == 1:
            break
        P.phase_begin()
        mkpools(nsq=2, nrstd=0, nscr=0, nwst=2)
        uT = P.sb([128, 2, TT], F32, "uT")
        vt = P.sb([128, 4, 512], F32, "vt")
        vnb = P.sb([128, 4, 256], BF16, "vnb")
        w = wload(4, 2)
        for j in range(2):
            pt = proj_fm(w, j)
            P.act(uT[:, j, :], pt[:, 0:TT], AF.Gelu)
        w = wload(6, 4)
        for tb in range(4):
            pt = pp.get()
            for kc in range(8):
                P.mm(pt[:, 0:512], hT[:, kc, tb * 128:(tb + 1) * 128], w[:, kc, 0:512], start=(kc == 0), stop=(kc == 7))
            P.act(vt[:, tb, :], pt[:, 0:512], AF.Gelu)
        for tb in range(4):
            P.op("dve", (lambda o, i: (lambda e: e.bn_stats(out=o, in_=i)))(st6[:, tb, :].ap, vt[:, tb, :].ap), reads=[vt], writes=[st6])
            P.op("dve", (lambda o, i: (lambda e: e.bn_aggr(out=o, in_=i)))(mv[:, tb, :].ap, st6[:, tb, :].ap), reads=[st6], writes=[mv])
        P.act(rs4, mv[:, :, 1], AF.Sqrt, bias=eps_sb[:, 1:2])
        recip(rs4)
        P.stt(nm4, mv[:, :, 0], -1.0, rs4, ALU.mult, ALU.mult)
        for tb in range(4):
            t = cur["sq"].get()
            P.ts(t[:, 0:256], vt[:, tb, 0:256], rs4[:, tb:tb + 1], ALU.mult, nm4[:, tb:tb + 1], ALU.add)
            P.tt(t[:, 0:256], t[:, 0:256], LG, ALU.mult)
            P.tt(vnb[:, tb, :], t[:, 0:256], LBt, ALU.add)
        for g in range(2):
            pg_ = px.get()
            for tb in range(4):
                P.mm(pg_[:, tb * 128:(tb + 1) * 128], vnb[:, tb, g * 128:(g + 1) * 128], wsT_sb[:, g, :], start=True, stop=False)
                P.mm(pg_[:, tb * 128:(tb + 1) * 128], ones[0:1, 0:128], bs_sb[0:1, g, :], start=False, stop=True)
            t = cur["sq"].get()
            P.tt(t, uT[:, g, :], zs[:, 2 + g, :], ALU.mult, eng="pool")
            P.tt(brT[:, 2 + g, :], pg_[:, 0:TT], t, ALU.mult)
        P.phase_end()

        if DEBUG_STOP == 2:
            break
        P.phase_begin()
        big = lambda name: P.sb([128, 2, TT], F32, name)
        e_inc = big("e_inc"); BT = big("BT"); KT = big("KT"); vm = big("vm"); rk = big("rk")
        AR = P.sb([128, 2, 8, 2, 64], F32, "AR")
        P.phase_begin()
        rws = P.sb([128, 9, TT], F32, "rws")
        P.phase_begin()
        mkpools(nsq=2, nrstd=0, nscr=0, nwst=2)
        rwraw = P.sb([128, 9, TT + 1], F32, "rwraw")
        P.copy(rwraw[:, :, 0:1], rwlast)
        for (cs_, n_, base) in ((10, 4, 0), (14, 4, 4), (18, 1, 8)):
            w = wload(cs_, n_)
            for j in range(n_):
                pt = proj_fm(w, j)
                P.copy(rwraw[:, base + j, 1:TT + 1], pt[:, 0:TT], eng="act")
        for idx in range(9):
            t = cur["sq"].get()
            P.tt(t, rwraw[:, idx, 0:TT], rwraw[:, idx, 1:TT + 1], ALU.subtract)
            P.stt(rws[:, idx, :], t, vec[:, idx:idx + 1], rwraw[:, idx, 1:TT + 1], ALU.mult, ALU.add)
        P.copy(rwlast, rwraw[:, :, TT:TT + 1])
        P.phase_end()
        if DEBUG_STOP == 10:
            P.phase_end(); P.phase_end(); break
        P.phase_begin()
        mkpools(nsq=2, nrstd=0, nscr=2, nwst=0)
        scr = cur["scr"]
        lo_sb = P.sb([32, TT], F32, "lo_sb")
        tmp1 = lambda name: P.sb([128, TT], F32, name)
        sgw = tmp1("sgw"); asig = tmp1("asig"); cs = tmp1("cs"); exl = tmp1("exl"); e_exc = tmp1("e_exc")
        e_neg = tmp1("e_neg"); kkn = tmp1("kkn"); k2 = tmp1("k2")
        v4 = lambda x: x.re("p (c t) -> p c t", t=64)
        th = P.sb([64, TT], F32, "th")
        P.act(th[0:64, :], rws[0:64, 8, :], AF.Tanh)
        if not first:
            plo = px.get()
            for j in range(4):
                P.mm(plo[0:32, 0:TT], v1_sb[:, j, :], rws[:, 4 + j, :], start=(j == 0), stop=(j == 3))
            P.copy(lo_sb, plo[0:32, 0:TT], eng="act")
        for hp in range(2):
            pw = px.get()
            P.mm(pw[:, 0:TT], wa2_sb[0:64, hp * 128:(hp + 1) * 128], th[0:64, :])
            P.act(sgw, pw[:, 0:TT], AF.Sigmoid, bias=vec[:, 12 + hp:13 + hp])
            pa_ = px.get()
            P.mm(pa_[:, 0:TT], wa2_sb[64:128, hp * 128:(hp + 1) * 128], rws[64:128, 8, :])
            P.act(asig, pa_[:, 0:TT], AF.Sigmoid, bias=vec[:, 14 + hp:15 + hp])
            for c in range(8):
                o_, a_, b_ = cs[:, c * 64:(c + 1) * 64].ap, ones[:, 0:64].ap, sgw[:, c * 64:(c + 1) * 64].ap
                P.op("dve", (lambda o, a, b: (lambda e: e.tensor_tensor_scan(out=o, data0=a, data1=b, initial=0.0, op0=ALU.mult, op1=ALU.add)))(o_, a_, b_),
                     reads=[ones, sgw], writes=[cs])
            P.tt(exl, cs, sgw, ALU.subtract)
            P.act(e_inc[:, hp, :], cs, AF.Exp, scale=-C0)
            P.act(e_exc, exl, AF.Exp, scale=-C0)
            P.act(e_neg, cs, AF.Exp, scale=C0)
            if first:
                P.copy(vm[:, hp, :], rws[:, 4 + hp, :], eng="pool")
            else:
                vf = exl
                P.dma("sp", vf, vf_in[hp * 128:(hp + 1) * 128, c0t:c0t + TT])
                pv = px.get()
                P.mm(pv[:, 0:TT], v2_sb[0:32, hp * 128:(hp + 1) * 128], lo_sb[0:32, :])
                sv = scr.get()
                P.act(sv, pv[:, 0:TT], AF.Sigmoid, bias=vec[:, 20 + hp:21 + hp])
                P.tt(vf, vf, rws[:, 4 + hp, :], ALU.subtract)
                P.tt(vf, vf, sv, ALU.mult)
                P.tt(vm[:, hp, :], vf, rws[:, 4 + hp, :], ALU.add)
            kk = scr.get()
            P.ts(kk, rws[:, 2 + hp, :], vec[:, 16 + hp:17 + hp], ALU.mult)
            sq = cur["sq"].get()
            P.act(sq, kk, AF.Square)
            pn = px.get()
            P.mm(pn[:, 0:TT], onesblk, sq)
            nr = cur["sq"].get()
            P.act(nr, pn[:, 0:TT], AF.Sqrt)
            P.ts(nr, nr, 1e-12, ALU.max)
            recip(nr)
            P.tt(kkn, kk, nr, ALU.mult)
            t = cur["sq"].get()
            P.ts(t, asig, vec[:, 18 + hp:19 + hp], ALU.mult, omka[:, hp:hp + 1], ALU.add)
            P.tt(k2, rws[:, 2 + hp, :], t, ALU.mult)
            P.stt(AR[:, hp, :, 0, :], v4(kkn), -1.0, v4(e_exc), ALU.mult, ALU.mult)
            P.tt(AR[:, hp, :, 1, :], v4(rws[:, hp, :]), v4(e_inc[:, hp, :]), ALU.mult)
            P.tt(BT[:, hp, :], kkn, asig, ALU.mult)
            P.tt(BT[:, hp, :], BT[:, hp, :], e_neg, ALU.mult)
            P.tt(KT[:, hp, :], k2, e_neg, ALU.mult)
            P.tt(rk[:, hp, :], rws[:, hp, :], k2, ALU.mult, eng="pool")
        if first:
            P.dma("sp", vf_out[:, c0t:c0t + TT].re("(j p) t -> p j t", p=128), vm, is_output=True)
        P.phase_end()
        P.phase_end()
        if DEBUG_STOP == 11:
            P.phase_end(); break
        sel = ident[:, 64:128]
        AR2 = P.sb([64, 2, 8, 2, 64], F32, "AR2")
        BT2 = P.sb([64, 2, TT], F32, "BT2"); KT2 = P.sb([64, 2, TT], F32, "KT2")
        G = P.sb([64, 8, 4], F32, "G")
        e4 = e_inc.re("p h (c t) -> p h c t", t=64)
        for hp in range(2):
            for half in range(2):
                pl = px.get()
                P.mm(pl[0:64, 0:512], sel, AR[:, hp, 4 * half:4 * half + 4, :, :].re("p c a t -> p (c a t)"))
                P.copy(AR2[:, hp, 4 * half:4 * half + 4, :, :].re("p c a t -> p (c a t)"), pl[0:64, 0:512], eng="act")
            pl = px.get()
            P.mm(pl[0:64, 0:512], sel, BT[:, hp, :])
            P.copy(BT2[:, hp, :], pl[0:64, 0:512], eng="act")
            pl = px.get()
            P.mm(pl[0:64, 0:512], sel, KT[:, hp, :])
            P.copy(KT2[:, hp, :], pl[0:64, 0:512], eng="act")
            pl = px.get()
            P.mm(pl[0:64, 0:8], sel, e4[:, hp, :, 63])
            P.copy(G[:, :, 2 * hp + 1], pl[0:64, 0:8], eng="act")
            P.copy(G[:, :, 2 * hp], e4[0:64, hp, :, 63], eng="pool")
        ARh = lambda hp, hh: (AR if hh == 0 else AR2)[0:64, hp]
        BTh = lambda hp, hh: (BT if hh == 0 else BT2)[0:64, hp]
        KTh = lambda hp, hh: (KT if hh == 0 else KT2)[0:64, hp]
        tokp = RR(lambda i: P.sb([64, 2, 3, 128], F32, f"tok{i}"), 1)
        bco = P.sb([64, 2, 2], F32, "bco")
        mk64 = lambda name, n=1: RR(lambda i: P.sb([64, 4, 64], F32, f"{name}{i}"), n)
        S1m_p = RR(lambda i: P.sb([64, 4, 128], F32, f"S1m{i}"), 1)
        S2m_p = RR(lambda i: P.sb([64, 4, 128], F32, f"S2m{i}"), 1)
        Pm_p = mk64("Pm", 2); PTm_p = mk64("PTm", 2); ST_p = mk64("ST", 2)
        W0_p = mk64("W0"); U0_p = mk64("U0"); Y0_p = mk64("Y0"); W1_p = mk64("W1"); U_p = mk64("U"); Y_p = mk64("Y")
        yn_p = mk64("yn"); yo_p = mk64("yo")
        HG = P.sb([64, 4, 64], F32, "HG")
        v4h = lambda x: x.re("p (h x) -> p h x", h=4)

        for c in range(8):
            heads = [(hp, hh) for hp in range(2) for hh in range(2)]
            cc = slice(c * 64, (c + 1) * 64)
            tok = tokp.get()
            for hp in range(2):
                ptk = px.get()
                for s_, src in enumerate((BT, KT, vm)):
                    P.tr(ptk[0:64, s_ * 128:(s_ + 1) * 128], src[:, hp, cc], ident)
                pbq = px.get()
                P.mm(pbq[0:64, 0:2], rk[:, hp, cc], rkblk_sb[:, hp, :])
                P.copy(tok[:, hp, :, :], ptk[0:64, 0:384].re("p (s j) -> p s j", s=3), eng="act")
                P.copy(bco[:, hp, :], pbq[0:64, 0:2], eng="act")
            if DEBUG_STOP == 12:
                break
            p1 = px.get(); p2 = px.get(); p3 = px.get()
            for h, (hp, hh) in enumerate(heads):
                arr = ARh(hp, hh)[:, c, :, :].re("p a t -> p (a t)")
                P.mm(p1[0:64, h * 128:(h + 1) * 128], BTh(hp, hh)[:, cc], arr)
                P.mm(p2[0:64, h * 128:(h + 1) * 128], KTh(hp, hh)[:, cc], arr)
                P.mm(p3[0:64, h * 64:(h + 1) * 64], ARh(hp, hh)[:, c, 0, :], BTh(hp, hh)[:, cc])
            if DEBUG_STOP == 13:
                break
            S1m = S1m_p.get(); S2m = S2m_p.get(); Pm = Pm_p.get()
            mS = maskS.with_ap(maskS.ap.unsqueeze(1).to_broadcast([64, 4, 128]))
            mL = maskL.with_ap(maskL.ap.unsqueeze(1).to_broadcast([64, 4, 64]))
            P.tt(S1m, v4h(p1[0:64, 0:512]), mS, ALU.mult)
            P.tt(S2m, v4h(p2[0:64, 0:512]), mS, ALU.mult)
            P.tt(Pm, v4h(p3[0:64, 0:256]), mL, ALU.mult)
            PTm = PTm_p.get()
            P.copy(PTm, S1m[:, :, 0:64], eng="pool")
            ST = ST_p.get()
            idb = ident.with_ap(ident[0:64, 0:64].ap.unsqueeze(1).to_broadcast([64, 4, 64]))
            P.tt(ST, S1m[:, :, 0:64], idb, ALU.add)
            for it in range(5):
                pq_ = px.get()
                for h in range(4):
                    P.mm(pq_[0:64, h * 64:(h + 1) * 64], PTm[:, h, :], Pm[:, h, :])
                    if it < 4:
                        P.mm(pq_[0:64, 256 + h * 64:256 + (h + 1) * 64], Pm[:, h, :], PTm[:, h, :])
                Pn = Pm_p.get(); PTn = PTm_p.get()
                P.copy(Pn, v4h(pq_[0:64, 0:256]), eng="act")
                if it < 4:
                    P.copy(PTn, v4h(pq_[0:64, 256:512]), eng="act")
                ps_ = px.get()
                for h in range(4):
                    P.mm(ps_[0:64, h * 64:(h + 1) * 64], Pn[:, h, :], ST[:, h, :])
                STn = ST_p.get()
                P.tt(STn, v4h(ps_[0:64, 0:256]), ST, ALU.add)
                Pm, PTm, ST = Pn, PTn, STn
            if DEBUG_STOP == 14:
                break
            vtok = lambda hp, hh: tok[:, hp, 2, hh * 64:(hh + 1) * 64]
            pw0 = px.get()
            for h, (hp, hh) in enumerate(heads):
                P.mm(pw0[0:64, h * 64:(h + 1) * 64], S2m[:, h, 0:64], vtok(hp, hh))
                P.mm(pw0[0:64, 256 + h * 64:256 + (h + 1) * 64], S2m[:, h, 64:128], vtok(hp, hh))
            W0 = W0_p.get(); Y0 = Y0_p.get()
            P.copy(W0, v4h(pw0[0:64, 0:256]), eng="act")
            P.copy(Y0, v4h(pw0[0:64, 256:512]), eng="act")
            pu0 = px.get()
            for h in range(4):
                P.mm(pu0[0:64, h * 64:(h + 1) * 64], ST[:, h, :], W0[:, h, :])
            U0 = U0_p.get()
            P.copy(U0, v4h(pu0[0:64, 0:256]), eng="act")
            pw1 = px.get()
            for h, (hp, hh) in enumerate(heads):
                P.mm(pw1[0:64, h * 64:(h + 1) * 64], ARh(hp, hh)[:, c, 0, :], Hst[:, h, :])
            W1 = W1_p.get()
            P.copy(W1, v4h(pw1[0:64, 0:256]))
            pu = px.get()
            for h in range(4):
                P.mm(pu[0:64, h * 64:(h + 1) * 64], ST[:, h, :], W1[:, h, :])
            U = U_p.get()
            P.tt(U, v4h(pu[0:64, 0:256]), U0, ALU.add)
            py = px.get()
            for h, (hp, hh) in enumerate(heads):
                P.mm(py[0:64, h * 64:(h + 1) * 64], ARh(hp, hh)[:, c, 1, :], Hst[:, h, :], start=True, stop=False)
                P.mm(py[0:64, h * 64:(h + 1) * 64], S1m[:, h, 64:128], U[:, h, :], start=False, stop=True)
            Y = Y_p.get()
            P.tt(Y, v4h(py[0:64, 0:256]), Y0, ALU.add)
            if DEBUG_STOP == 15:
                break
            ph = px.get()
            for h, (hp, hh) in enumerate(heads):
                P.mm(ph[0:64, h * 64:(h + 1) * 64], tok[:, hp, 1, hh * 64:(hh + 1) * 64], vtok(hp, hh), start=True, stop=False)
                P.mm(ph[0:64, h * 64:(h + 1) * 64], tok[:, hp, 0, hh * 64:(hh + 1) * 64], U[:, h, :], start=False, stop=True)
            Gb = G.with_ap(G[:, c, :].ap.unsqueeze(2).to_broadcast([64, 4, 64]))
            P.tt(HG, v4h(ph[0:64, 0:256]), Hst, ALU.add)
            P.tt(Hst, HG, Gb, ALU.mult)
            if DEBUG_STOP == 16:
                break
            for h in range(4):
                P.op("dve", (lambda o, i: (lambda e: e.bn_stats(out=o, in_=i)))(st6[0:64, h, :].ap, Y[:, h, :].ap), reads=[Y], writes=[st6])
                P.op("dve", (lambda o, i: (lambda e: e.bn_aggr(out=o, in_=i)))(mv[0:64, h, :].ap, st6[0:64, h, :].ap), reads=[st6], writes=[mv])
            P.act(rs4[0:64, :], mv[0:64, :, 1], AF.Sqrt, bias=eps_sb[0:64, 2:3])
            recip(rs4[0:64, :])
            yn = yn_p.get(); yo = yo_p.get()
            for h in range(4):
                P.ts(yn[:, h, :], Y[:, h, :], mv[0:64, h, 0:1], ALU.subtract, rs4[0:64, h:h + 1], ALU.mult)
            ynf = yn.re("p h x -> p (h x)")
            P.tt(ynf, ynf, LXG, ALU.mult, eng="pool")
            P.tt(ynf, ynf, LXB, ALU.add, eng="pool")
            for h, (hp, hh) in enumerate(heads):
                P.stt(yo[:, h, :], vtok(hp, hh), bco[:, hp, hh:hh + 1], yn[:, h, :], ALU.mult, ALU.add)
            pyt = px.get()
            for hp in range(2):
                P.tr(pyt[:, hp * 64:(hp + 1) * 64], yo[:, 2 * hp:2 * hp + 2, :].re("p h x -> p (h x)"), ident[0:64, 0:64])
            P.tt(brT[:, 4:6, cc], pyt[:, 0:128].re("p (h t) -> p h t", h=2), zs[:, 4:6, cc], ALU.mult)
        if DEBUG_STOP is not None and 12 <= DEBUG_STOP <= 16:
            P.phase_end(); break
        P.phase_end()

        if DEBUG_STOP == 3:
            break
        P.phase_begin()
        mkpools(nsq=1, nrstd=0, nscr=3, nwst=2)
        scr = cur["scr"]
        mT = P.sb([128, 8, TT], BF16, "mT")
        sg = RR(lambda i: P.sb([128, 3, TT], F32, f"sg{i}"), 1)
        ystage = RR(lambda i: P.sb([128, TT], F32, f"ystage{i}"), 2)
        for dc in range(8):
            w = wload(G_BASE + dc * 3, 3)
            sgt = sg.get()
            for n in range(3):
                pt = proj_fm(w, n)
                P.act(sgt[:, n, :], pt[:, 0:TT], AF.Sigmoid)
            ts_ = []
            for n in range(3):
                pj = pp.get()
                for j in range(2):
                    P.mm(pj[:, 0:TT], wbr_sb[:, n * 2 + j, dc * 128:(dc + 1) * 128], brT[:, n * 2 + j, :], start=(j == 0), stop=(j == 1))
                t = scr.get()
                P.tt(t, pj[:, 0:TT], sgt[:, n, :], ALU.mult)
                ts_.append(t)
            P.tt(ts_[0], ts_[0], ts_[1], ALU.add, eng="pool")
            P.tt(mT[:, dc, :], ts_[0], ts_[2], ALU.add, eng="pool")
        for dc2 in range(8):
            pt = pp.get()
            for dc in range(8):
                P.mm(pt[:, 0:TT], wout_sb[:, dc, dc2 * 128:(dc2 + 1) * 128], mT[:, dc, :], start=(dc == 0), stop=(dc == 7))
            ys = ystage.get()
            P.copy(ys, pt[:, 0:TT], eng="act")
            P.dma("sp", yP[dc2 * 128:(dc2 + 1) * 128, c0t:c0t + TT], ys, is_output=True)
        P.phase_end()

    P.emit()
    return nc


_Q, _KV, _KR = 0, 256, 384
_GMU, _GMV = 416, 928
_RWR, _RWK, _RWV, _RWW, _RWA = 1440, 1952, 2464, 2976, 3040
_ZM, _ZG, _ZR = 3104, 3616, 4128
_G = 4640


def _pc(v, n):
    return np.ascontiguousarray(np.asarray(v, np.float32).reshape(n, 128).T)


def consts():
    ident = np.eye(128, dtype=np.float32)
    onesblk = np.zeros((128, 128), np.float32)
    onesblk[:64, :64] = 1.0
    onesblk[64:, 64:] = 1.0
    s = np.arange(64)[:, None]
    t = np.arange(64)[None, :]
    maskS = np.concatenate([(s < t), (s <= t)], axis=1).astype(np.float32)
    maskL = (np.arange(64)[None, :] < np.arange(64)[:, None]).astype(np.float32)
    return dict(c_ident=ident, c_onesblk=onesblk, c_maskS=maskS, c_maskL=maskL)


def prep_body(inp, l, b, hs, T):
    W = np.asarray(inp["w_in"][l], np.float32)
    own = slice(256 * hs, 256 * hs + 256)
    oth = slice(256 * (1 - hs), 256 * (1 - hs) + 256)
    z32 = np.zeros((1024, 32), np.float32)
    kr = W[:, _KR:_KR + 32]
    krsw = np.concatenate([kr[:, 16:32], kr[:, 0:16]], axis=1)
    cols = [W[:, _Q:_Q + 256], W[:, _KV:_KV + 128], z32, kr, z32, krsw,
            W[:, _GMU:_GMU + 512][:, own],
            W[:, _GMV:_GMV + 512][:, own], W[:, _GMV:_GMV + 512][:, oth],
            W[:, _RWR:_RWR + 512][:, own], W[:, _RWK:_RWK + 512][:, own],
            W[:, _RWV:_RWV + 512][:, own], W[:, _RWV:_RWV + 512][:, oth],
            W[:, _RWW:_RWW + 64], W[:, _RWA:_RWA + 64],
            W[:, _ZM:_ZM + 512][:, own], W[:, _ZG:_ZG + 512][:, own], W[:, _ZR:_ZR + 512][:, own]]
    Wg = W[:, _G:_G + 3072].reshape(1024, 3, 8, 128)
    cols.append(np.ascontiguousarray(Wg.transpose(0, 2, 1, 3)).reshape(1024, 3072))
    wcat = np.ascontiguousarray(np.concatenate(cols, axis=1))
    assert wcat.shape[1] == NCH_W * 128, wcat.shape
    wada = np.asarray(inp["w_ada"][l], np.float32)
    bada = np.asarray(inp["b_ada"][l], np.float32)
    mu = np.asarray(inp["rw_mu"][l], np.float32)
    mu_r, mu_k, mu_v = mu[0:512], mu[512:1024], mu[1024:1536]
    mu_l = mu[1536:1664]
    vec = np.zeros((128, 40), np.float32)
    vec[:, 0:2] = _pc(mu_r[own], 2); vec[:, 2:4] = _pc(mu_k[own], 2)
    vec[:, 4:6] = _pc(mu_v[own], 2); vec[:, 6:8] = _pc(mu_v[oth], 2)
    vec[:, 8] = mu_l
    vec[:, 9:11] = _pc(inp["mla_q_norm"][l], 2)
    vec[:, 11] = np.asarray(inp["mla_kv_norm"][l], np.float32)
    vec[:, 12:14] = _pc(np.asarray(inp["rw_w0"][l])[own], 2)
    vec[:, 14:16] = _pc(np.asarray(inp["rw_a0"][l])[own], 2)
    vec[:, 16:18] = _pc(np.asarray(inp["rw_k_k"][l])[own], 2)
    vec[:, 18:20] = _pc(np.asarray(inp["rw_k_a"][l])[own], 2)
    if l > 0:
        vec[:, 20:22] = _pc(np.asarray(inp["rw_v0"][l - 1])[own], 2)
    pos = np.ascontiguousarray(np.broadcast_to(np.asarray(inp["positions"][b], np.int32)[None, :T], (64, T)))
    invf = np.zeros((64, 2), np.float32)
    f = (10000.0 ** (-np.arange(0, 32, 2, dtype=np.float32) / 32)).astype(np.float32)
    invf[32:48, 0] = f; invf[48:64, 0] = f
    invf[32:48, 1] = -1.0; invf[48:64, 1] = 1.0
    wuq_full = np.asarray(inp["mla_w_uq"][l], np.float32)
    wuq = np.zeros((256, 4, 192), np.float32)
    for hh in range(4):
        h = 4 * hs + hh
        nope = wuq_full[:, h * 96:h * 96 + 64]
        rope = wuq_full[:, h * 96 + 64:h * 96 + 96]
        wuq[:, hh, 32:64] = rope
        wuq[:, hh, 64:128] = nope
        wuq[:, hh, 160:176] = rope[:, 16:32]
        wuq[:, hh, 176:192] = rope[:, 0:16]
    wukv = np.asarray(inp["mla_w_ukv"][l], np.float32)
    wukvk = np.zeros((128, 4, 128), np.float32)
    wukvv = np.zeros((128, 4, 64), np.float32)
    for hh in range(4):
        h = 4 * hs + hh
        wukvk[:, hh, 64:128] = wukv[:, h * 128:h * 128 + 64]
        wukvv[:, hh, :] = wukv[:, h * 128 + 64:h * 128 + 128]
    lng = np.asarray(inp["gm_ln_g"][l], np.float32); lnb = np.asarray(inp["gm_ln_b"][l], np.float32)
    lng = np.concatenate([lng[own], lng[oth]])[None, :]
    lnb = np.concatenate([lnb[own], lnb[oth]])[None, :]
    ws = np.asarray(inp["gm_w_s"][l], np.float32)[2 * hs:2 * hs + 2]
    wsT = np.ascontiguousarray(ws.transpose(2, 0, 1))
    bs = np.ascontiguousarray(np.asarray(inp["gm_b_s"][l], np.float32)[2 * hs:2 * hs + 2][None])
    wa2 = np.concatenate([np.asarray(inp["rw_w2"][l], np.float32)[:, own],
                          np.asarray(inp["rw_a2"][l], np.float32)[:, own]], axis=0)
    rk = np.asarray(inp["rw_r_k"][l], np.float32)[4 * hs:4 * hs + 4]
    rkblk = np.zeros((128, 2, 2), np.float32)
    for hp in range(2):
        for hh in range(2):
            rkblk[64 * hh:64 * hh + 64, hp, hh] = rk[2 * hp + hh]
    d = dict(wcat=wcat, wada_b=np.ascontiguousarray(wada[:, 0:2048]), bada_b=_pc(bada[0:2048], 16),
             preg=_pc(inp["pre_g"][l], 8), vecs=vec, pos=pos, invf=invf,
             wuq=wuq.reshape(256, 768), wukvk=wukvk.reshape(128, 512), wukvv=wukvv.reshape(128, 256),
             lng=np.ascontiguousarray(lng), lnb=np.ascontiguousarray(lnb), wsT=wsT, bs=bs,
             wa2=np.ascontiguousarray(wa2), rkblk=rkblk,
             lxg=np.ascontiguousarray(np.asarray(inp["rw_lnx_g"][l], np.float32)[own][None, :]),
             lxb=np.ascontiguousarray(np.asarray(inp["rw_lnx_b"][l], np.float32)[own][None, :]),
             wbr=np.ascontiguousarray(np.asarray(inp["w_br"][l], np.float32)[:, own, :].reshape(768, 1024)),
             wout=np.ascontiguousarray(np.asarray(inp["w_out"][l], np.float32)))
    c = consts()
    d["c_maskS"] = c["c_maskS"]; d["c_maskL"] = c["c_maskL"]
    if l > 0:
        v1 = np.asarray(inp["rw_v1"][l - 1], np.float32)
        d["v1"] = np.ascontiguousarray(np.concatenate([v1[own], v1[oth]], axis=0))
        d["v2"] = np.ascontiguousarray(np.asarray(inp["rw_v2"][l - 1], np.float32)[:, own])
    return d


def prep_common(inp, b):
    c = consts()
    return dict(c_ident=c["c_ident"], c_onesblk=c["c_onesblk"], cvec=_pc(inp["c"][b], 8))


def prep_pro(inp, lp):
    wada = np.asarray(inp["w_ada"][lp], np.float32)
    bada = np.asarray(inp["b_ada"][lp], np.float32)
    return dict(wada_p=np.ascontiguousarray(wada[:, 2048:3072]), bada_p=_pc(bada[2048:3072], 8),
                postg_p=_pc(inp["post_g"][lp], 8))


_PROGS = {}


def get_prog(T, do_pro, do_body, first):
    key = (T, do_pro, do_body, first)
    if key not in _PROGS:
        _PROGS[key] = build(T, do_pro, do_body, first)
    return _PROGS[key]


def run_layers(inp, T, B, n_layers):
    ncore = 2 * B
    xT = [np.ascontiguousarray(np.asarray(inp["x"][b], np.float32)[:T].T) for b in range(B)]
    yparts = None
    vf = None
    for l in range(n_layers + 1):
        do_pro = l > 0
        do_body = l < n_layers
        nc = build(T, do_pro, do_body, l == 0)
        maps = []
        for core in range(ncore):
            b, hs = divmod(core, 2)
            m = prep_common(inp, b)
            m["xT"] = xT[b]
            if do_pro:
                m.update(prep_pro(inp, l - 1))
                m["yA"] = yparts[2 * b]; m["yB"] = yparts[2 * b + 1]
            if do_body:
                m.update(prep_body(inp, l, b, hs, T))
                if l > 0:
                    m["vf_in"] = vf[core]
            maps.append(m)
        res = run_bass_kernel_spmd(nc, maps, core_ids=list(range(ncore))).results
        if do_pro:
            xT = [np.asarray(res[2 * b]["xT_out"]) for b in range(B)]
        if do_body:
            yparts = [np.asarray(res[c]["yP"]) for c in range(ncore)]
            if l == 0:
                vf = [np.asarray(res[c]["vf_out"]) for c in range(ncore)]
    return np.stack([x.T for x in xT], axis=0)


def kernel(**inputs):
    out = run_layers(inputs, 4096, 4, 4)
    return np.ascontiguousarray(out.astype(np.float32))
```

```python
import numpy as np
import concourse.bass as bass
import concourse.mybir as mybir
from concourse.bass_utils import run_bass_kernel_spmd

F32 = mybir.dt.float32
BF16 = mybir.dt.bfloat16
I32 = mybir.dt.int32
AF = mybir.ActivationFunctionType
ALU = mybir.AluOpType
AX = mybir.AxisListType

ENGS = ("pe", "dve", "act", "pool", "sp")
EPOCH = 30000
NDMA = 12


class V:
    __slots__ = ("ap", "key")

    def __init__(self, ap, key):
        self.ap = ap
        self.key = key

    def __getitem__(self, idx):
        return V(self.ap[idx], self.key)

    def k(self, sub):
        return V(self.ap, (self.key, sub))

    def re(self, pat, **kw):
        return V(self.ap.rearrange(pat, **kw), self.key)

    def bc(self, shape):
        return V(self.ap.to_broadcast(shape), self.key)

    def with_ap(self, ap):
        return V(ap, self.key)


class Prog:
    def __init__(self, nc):
        self.nc = nc
        self.engobj = {"pe": nc.tensor, "dve": nc.vector, "act": nc.scalar,
                       "pool": nc.gpsimd, "sp": nc.sync}
        self.stream = {e: [] for e in ENGS}
        self.cnt = {e: 0 for e in ENGS}
        self.seen = {e: {} for e in ENGS}
        self.last_w = {}
        self.readers = {}
        self.sems = {}
        self.dma_slot_uses = {}
        self.dma_rr = {e: 0 for e in ENGS}
        self.ntile = 0
        self.out_events = []

    def sb(self, shape, dt, name=None):
        self.ntile += 1
        name = name or f"t{self.ntile}"
        nm = f"{name}_{self.ntile}"
        if getattr(self, "phase_guards", None) is not None:
            g = self.nc.sbuf_tensor(nm, list(shape), dt)
            h = g.__enter__()
            self.phase_guards.append(g)
        else:
            h = self.nc.alloc_sbuf_tensor(nm, list(shape), dt)
        return V(h.ap(), nm)

    def phase_begin(self):
        if not hasattr(self, "phase_stack"):
            self.phase_stack = []
        self.phase_stack.append(getattr(self, "phase_guards", None))
        self.phase_guards = []

    def phase_end(self):
        self.barrier()
        for g in reversed(self.phase_guards):
            g.__exit__(None, None, None)
        self.phase_guards = self.phase_stack.pop()

    def barrier(self):
        evs = []
        for e2 in ENGS:
            n = self.cnt[e2]
            if n > 0:
                epoch, idx = divmod(n - 1, EPOCH)
                evs.append((("e", e2, epoch), idx + 1))
        for sk, uses in self.dma_slot_uses.items():
            evs.append((sk, 16 * uses))
        for e in ENGS:
            for ev in evs:
                self._need(e, ev)
        self.last_w = {}
        self.readers = {}

    def ps(self, shape, dt=F32, name=None):
        self.ntile += 1
        name = name or f"p{self.ntile}"
        h = self.nc.alloc_psum_tensor(f"{name}_{self.ntile}", list(shape), dt)
        return V(h.ap() if hasattr(h, "ap") else h[:], f"{name}_{self.ntile}")

    def dram(self, name, shape, dt, kind="Internal"):
        h = self.nc.dram_tensor(name, list(shape), dt, kind=kind)
        return V(h.ap(), "dram_" + name)

    def _sem(self, key):
        if key not in self.sems:
            self.sems[key] = self.nc.alloc_semaphore("s_" + "_".join(str(x) for x in key))
        return self.sems[key]

    def _need(self, eng, ev):
        if ev is None:
            return
        sk, val = ev
        if self.seen[eng].get(sk, 0) >= val:
            return
        self.seen[eng][sk] = val
        self.stream[eng].append(("wait", sk, val))

    def _deps(self, eng, reads, writes, self_sk):
        for r in reads:
            ev = self.last_w.get(r.key)
            if ev is not None:
                if ev[0] == self_sk and eng == "pe":
                    continue
                self._need(eng, ev)
        for w in writes:
            ev = self.last_w.get(w.key)
            if ev is not None and ev[0] != self_sk:
                self._need(eng, ev)
            for rv in self.readers.get(w.key, ()):
                if rv[0] != self_sk:
                    self._need(eng, rv)

    def _commit(self, ev, reads, writes):
        for r in reads:
            self.readers.setdefault(r.key, []).append(ev)
            lst = self.readers[r.key]
            if len(lst) > 24:
                d = {}
                for sk, v in lst:
                    d[sk] = max(d.get(sk, 0), v)
                self.readers[r.key] = list(d.items())
        for w in writes:
            self.last_w[w.key] = ev
            self.readers[w.key] = []

    def op(self, eng, fn, reads=(), writes=()):
        n = self.cnt[eng]
        epoch, idx = divmod(n, EPOCH)
        sk = ("e", eng, epoch)
        self._deps(eng, reads, writes, sk)
        ev = (sk, idx + 1)
        self.stream[eng].append(("op", fn, sk, 1))
        self.cnt[eng] = n + 1
        self._commit(ev, reads, writes)
        return ev

    def dma(self, q, out, in_, is_output=False):
        slot = self.dma_rr[q] % NDMA
        self.dma_rr[q] += 1
        sk = ("d", q, slot)
        uses = self.dma_slot_uses.get(sk, 0)
        if uses > 0:
            self._need(q, (sk, 16 * uses))
        self._deps(q, [in_], [out], sk)
        ev = (sk, 16 * (uses + 1))
        self.dma_slot_uses[sk] = uses + 1
        o_ap, i_ap = out.ap, in_.ap
        self.stream[q].append(("op", lambda e: e.dma_start(out=o_ap, in_=i_ap), sk, 16))
        self._commit(ev, [in_], [out])
        if is_output:
            self.out_events.append(ev)
        return ev

    def finish(self):
        for ev in self.out_events:
            self._need("sp", ev)

    def emit(self):
        self.finish()
        nc = self.nc
        for sk in set(x[1] for e in ENGS for x in self.stream[e] if x[0] == "wait") | \
                set(x[2] for e in ENGS for x in self.stream[e] if x[0] == "op"):
            self._sem(sk)
        with nc.Block() as block:
            def run(e):
                def body(engine):
                    for item in self.stream[e]:
                        if item[0] == "wait":
                            engine.wait_ge(self.sems[item[1]], item[2])
                        else:
                            inst = item[1](engine)
                            inst.then_inc(self.sems[item[2]], item[3])
                return body
            block.tensor(run("pe"))
            block.vector(run("dve"))
            block.scalar(run("act"))
            block.gpsimd(run("pool"))
            block.sync(run("sp"))

    def mm(self, out, lhsT, rhs, start=True, stop=True):
        o, l, r = out.ap, lhsT.ap, rhs.ap
        return self.op("pe", lambda e: e.matmul(o, l, r, start=start, stop=stop),
                       reads=[lhsT, rhs], writes=[out])

    def tr(self, out, in_, ident):
        o, i, d = out.ap, in_.ap, ident.ap
        return self.op("pe", lambda e: e.transpose(o, i, d), reads=[in_, ident], writes=[out])

    def act(self, out, in_, func, bias=None, scale=1.0, accum=None, eng="act"):
        o, i = out.ap, in_.ap
        reads = [in_]
        kw = {}
        if isinstance(bias, V):
            reads.append(bias); kw["bias"] = bias.ap
        elif bias is not None:
            kw["bias"] = bias
        if isinstance(scale, V):
            reads.append(scale); kw["scale"] = scale.ap
        else:
            kw["scale"] = scale
        writes = [out]
        if accum is not None:
            writes.append(accum); kw["accum_out"] = accum.ap
        return self.op(eng, lambda e: e.activation(out=o, in_=i, func=func, **kw),
                       reads=reads, writes=writes)

    def tt(self, out, a, b, op, eng="dve"):
        o, x, y = out.ap, a.ap, b.ap
        return self.op(eng, lambda e: e.tensor_tensor(out=o, in0=x, in1=y, op=op),
                       reads=[a, b], writes=[out])

    def ts(self, out, a, s1, op0, s2=None, op1=None, eng="dve", accum=None):
        o, x = out.ap, a.ap
        reads = [a]
        s1v = s1.ap if isinstance(s1, V) else s1
        s2v = s2.ap if isinstance(s2, V) else s2
        if isinstance(s1, V): reads.append(s1)
        if isinstance(s2, V): reads.append(s2)
        kw = {}
        writes = [out]
        if op1 is not None:
            kw["op1"] = op1
        if accum is not None:
            kw["accum_out"] = accum.ap; writes.append(accum)
        return self.op(eng, lambda e: e.tensor_scalar(out=o, in0=x, scalar1=s1v, scalar2=s2v, op0=op0, **kw),
                       reads=reads, writes=writes)

    def stt(self, out, a, s, b, op0, op1, eng="dve"):
        o, x, y = out.ap, a.ap, b.ap
        reads = [a, b]
        sv = s.ap if isinstance(s, V) else s
        if isinstance(s, V): reads.append(s)
        return self.op(eng, lambda e: e.scalar_tensor_tensor(out=o, in0=x, scalar=sv, in1=y, op0=op0, op1=op1),
                       reads=reads, writes=[out])

    def copy(self, out, in_, eng="dve"):
        o, i = out.ap, in_.ap
        if eng == "act":
            return self.op(eng, lambda e: e.copy(out=o, in_=i), reads=[in_], writes=[out])
        return self.op(eng, lambda e: e.tensor_copy(out=o, in_=i), reads=[in_], writes=[out])

    def memset(self, out, val, eng="dve"):
        o = out.ap
        return self.op(eng, lambda e: e.memset(o, val), reads=[], writes=[out])

import math

TT = 512
CH = 64
D = 1024
KC = 8
EPS = 1e-6
C0 = math.exp(-0.5)
SCALE = 96 ** -0.5
NCH_W = 49


class RR:
    def __init__(self, mk, n):
        self.t = [mk(i) for i in range(n)]
        self.i = 0

    def get(self):
        t = self.t[self.i % len(self.t)]
        self.i += 1
        return t


W_GROUPS = [
    (0, 4, "lat"), (4, 2, "gmu"), (6, 4, "gmv"), (10, 4, "rwrk"), (14, 4, "rwv"),
    (18, 1, "rwl"), (19, 4, "z0"), (23, 2, "z1"),
]
G_BASE = 25


DEBUG_STOP = None
DEBUG_FLAGS = {}


def build(T, do_pro, do_body, first):
    nc = bass.Bass("TRN2", target_bir_lowering=False)
    P = Prog(nc)
    NT = T // TT
    NB = T // 128
    din = lambda name, shape, dt=F32: P.dram(name, shape, dt, kind="ExternalInput")
    dout = lambda name, shape, dt=F32: P.dram(name, shape, dt, kind="ExternalOutput")

    xT = din("xT", [D, T])
    c_ident = din("c_ident", [128, 128])
    c_onesblk = din("c_onesblk", [128, 128])
    cvec = din("cvec", [128, 8])

    ident = P.sb([128, 128], F32, "ident")
    onesblk = P.sb([128, 128], F32, "onesblk")
    ones = P.sb([128, 128], F32, "ones")
    P.dma("sp", ident, c_ident)
    P.dma("sp", onesblk, c_onesblk)
    P.memset(ones, 1.0)
    cact2 = P.sb([128, 8, 2], F32, "cact2")
    ctmp = P.sb([128, 8], F32, "ctmp")
    P.dma("sp", ctmp, cvec)
    P.act(cact2[:, :, 0], ctmp, AF.Silu)
    P.act(cact2[:, :, 1], ctmp, AF.Silu)

    pp = RR(lambda i: P.ps([128, 512], F32, f"pp{i}"), 2)
    pa = RR(lambda i: P.ps([128, 512], F32, f"pa{i}"), 2)
    po_t = P.ps([128, 512], F32, "po")
    px = RR(lambda i: P.ps([128, 512], F32, f"px{i}"), 3)

    eps_sb = P.sb([128, 4], F32, "eps")
    P.memset(eps_sb[:, 0:1], EPS)
    P.memset(eps_sb[:, 1:2], 1e-5)
    P.memset(eps_sb[:, 2:3], 64e-5)
    P.memset(eps_sb[:, 3:4], 0.0)
    cur = {}

    def mkpools(nsq=2, nrstd=1, nscr=0, nwst=0):
        cur["sq"] = RR(lambda i: P.sb([128, TT], F32, f"sq{i}"), nsq)
        cur["rstd"] = RR(lambda i: P.sb([128, TT], F32, f"rstd{i}"), nrstd) if nrstd else None
        cur["scr"] = RR(lambda i: P.sb([128, TT], F32, f"scr{i}"), nscr) if nscr else None
        cur["wst"] = RR(lambda i: P.sb([128, 8, 512], BF16, f"wst{i}"), nwst) if nwst else None

    def recip(v):
        P.op("dve", (lambda o, i: (lambda e: e.reciprocal(out=o, in_=i)))(v.ap, v.ap), reads=[v], writes=[v])

    def rms_rstd(chunks, n_feat):
        pt = px.get()
        n = len(chunks)
        for i, cv in enumerate(chunks):
            sq = cur["sq"].get()
            P.act(sq, cv, AF.Square)
            P.mm(pt[:, 0:TT], ones, sq, start=(i == 0), stop=(i == n - 1))
        r = cur["rstd"].get()
        P.act(r, pt[:, 0:TT], AF.Sqrt, bias=eps_sb[:, 0:1], scale=1.0 / n_feat)
        recip(r)
        return r

    if do_pro:
        yA = din("yA", [D, T]); yB = din("yB", [D, T])
        wada_p = din("wada_p", [D, D]); bada_p = din("bada_p", [128, 8]); postg_p = din("postg_p", [128, 8])
        xT_out = dout("xT_out", [D, T])
        gp = P.sb([128, 8], F32, "gp")
    if do_body:
        wcat = din("wcat", [D, NCH_W * 128])
        wada_b = din("wada_b", [D, 2 * D]); bada_b = din("bada_b", [128, 16]); preg = din("preg", [128, 8])
        vecs = din("vecs", [128, 40])
        pos_in = din("pos", [64, T], I32)
        invf = din("invf", [64, 2])
        wuq = din("wuq", [256, 4 * 192]); wukvk = din("wukvk", [128, 512]); wukvv = din("wukvv", [128, 256])
        lng = din("lng", [1, 512]); lnb = din("lnb", [1, 512])
        wsT = din("wsT", [128, 2, 128]); bs = din("bs", [1, 2, 128])
        wa2 = din("wa2", [128, 256]); rkblk = din("rkblk", [128, 2, 2])
        lxg = din("lxg", [1, 256]); lxb = din("lxb", [1, 256])
        c_maskS = din("c_maskS", [64, 128]); c_maskL = din("c_maskL", [64, 64])
        wbr = din("wbr", [768, D]); wout = din("wout", [D, D])
        yP = dout("yP", [D, T])
        if first:
            vf_out = dout("vf_out", [256, T])
        else:
            vf_in = din("vf_in", [256, T])
            v1 = din("v1", [512, 32]); v2 = din("v2", [32, 256])
        sc1 = P.sb([128, 8], F32, "sc1")
        ss_ = P.sb([128, 16], F32, "ss_")
        shift = ss_[:, 0:8]

    P.phase_begin()
    wada_pool = RR(lambda i: P.sb([128, 8, 128], F32, f"wada{i}"), 2)

    def adaln(w_dram, bias_sb, nch, out_sb):
        for oc in range(nch):
            wt = wada_pool.get()
            P.dma("sp", wt, w_dram[:, oc * 128:(oc + 1) * 128].re("(kc p) c -> p kc c", p=128))
            pt = px.get()
            for kc in range(8):
                P.mm(pt[:, 0:2], wt[:, kc, :], cact2[:, kc, :], start=(kc == 0), stop=(kc == 7))
            P.tt(out_sb[:, oc:oc + 1], pt[:, 0:1], bias_sb[:, oc:oc + 1], ALU.add)

    if do_pro:
        bp = P.sb([128, 8], F32, "bp"); pg = P.sb([128, 8], F32, "pg"); gate = P.sb([128, 8], F32, "gate")
        P.dma("sp", bp, bada_p); P.dma("sp", pg, postg_p)
        adaln(wada_p, bp, 8, gate)
        P.tt(gp, gate, pg, ALU.mult)
    if do_body:
        bb = P.sb([128, 16], F32, "bb"); P.dma("sp", bb, bada_b)
        pgb = P.sb([128, 8], F32, "pgb"); P.dma("sp", pgb, preg)
        adaln(wada_b, bb, 16, ss_)
        P.stt(sc1, ss_[:, 8:16], 1.0, pgb, ALU.add, ALU.mult)
    P.phase_end()

    if do_body:
        vec = P.sb([128, 40], F32, "vec"); P.dma("sp", vec, vecs)
        omka = P.sb([128, 2], F32, "omka")
        P.ts(omka, vec[:, 18:20], -1.0, ALU.mult, 1.0, ALU.add)
        maskS = P.sb([64, 128], F32, "maskS"); P.dma("sp", maskS, c_maskS)
        maskL = P.sb([64, 64], F32, "maskL"); P.dma("sp", maskL, c_maskL)

        def load_bf(dram_view, shape, name):
            t = P.sb(shape, BF16, name)
            P.dma("pool", t, dram_view)
            return t

        wuq_sb = load_bf(wuq.re("(kc p) c -> p kc c", p=128), [128, 2, 768], "wuq")
        wukvk_sb = load_bf(wukvk, [128, 512], "wukvk")
        wukvv_sb = load_bf(wukvv, [128, 256], "wukvv")
        wsT_sb = load_bf(wsT, [128, 2, 128], "wsT")
        P.memset(wsT_sb[64:128, :, 0:64], 0.0, eng="pool")
        bs_sb = P.sb([1, 2, 128], F32, "bs"); P.dma("sp", bs_sb, bs)
        wa2_sb = P.sb([128, 256], F32, "wa2"); P.dma("sp", wa2_sb, wa2)
        rkblk_sb = P.sb([128, 2, 2], F32, "rkblk"); P.dma("sp", rkblk_sb, rkblk)
        LG = P.sb([128, 256], F32, "LG"); P.dma("sp", LG, lng[:, 0:256].with_ap(lng[:, 0:256].ap.partition_broadcast(128)))
        LBt = P.sb([128, 256], F32, "LB"); P.dma("sp", LBt, lnb[:, 0:256].with_ap(lnb[:, 0:256].ap.partition_broadcast(128)))
        LXG = P.sb([64, 256], F32, "LXG"); P.dma("sp", LXG, lxg.with_ap(lxg.ap.partition_broadcast(64)))
        LXB = P.sb([64, 256], F32, "LXB"); P.dma("sp", LXB, lxb.with_ap(lxb.ap.partition_broadcast(64)))
        wbr_sb = load_bf(wbr.re("(j p) c -> p j c", p=128), [128, 6, D], "wbr")
        wout_sb = load_bf(wout.re("(j p) c -> p j c", p=128), [128, 8, D], "wout")
        if not first:
            v1_sb = P.sb([128, 4, 32], F32, "v1"); P.dma("sp", v1_sb, v1.re("(j p) c -> p j c", p=128))
            v2_sb = P.sb([32, 256], F32, "v2"); P.dma("sp", v2_sb, v2)
        invf_sb = P.sb([64, 2], F32, "invf"); P.dma("sp", invf_sb, invf)
        TWO_PI = 2.0 * math.pi
        Kc = P.sb([128, 4, T], BF16, "Kc")
        Vc = P.sb([128, NB, 4, 65], BF16, "Vc")
        P.memset(Kc[0:32, :, :], 0.0, eng="pool")
        P.memset(Kc[0:1, :, :], 1.0, eng="pool")
        P.memset(Vc[:, :, :, 64:65], 1.0, eng="pool")
        kmax2 = P.sb([1, 4], F32, "kmax2"); P.memset(kmax2, 0.0)
        Hst = P.sb([64, 4, 64], F32, "Hst"); P.memset(Hst, 0.0)
        rwlast = P.sb([128, 9, 1], F32, "rwlast"); P.memset(rwlast, 0.0)
        hT = P.sb([128, 8, TT], BF16, "hT")
        zs = P.sb([128, 6, TT], BF16, "zs")
        brT = P.sb([128, 6, TT], BF16, "brT")
        small = P.sb([128, 64], F32, "small")
        st6 = P.sb([128, 4, 6], F32, "st6"); mv = P.sb([128, 4, 2], F32, "mv"); rs4 = P.sb([128, 4], F32, "rs4")
        nm4 = P.sb([128, 4], F32, "nm4")

        wbf = {}

        def wload(c_start, n):
            w = cur["wst"].get()
            if c_start not in wbf:
                wbf[c_start] = P.dram(f"wbf{c_start}", [128, 8, n * 128], BF16)
                P.dma("pool", w[:, :, 0:n * 128],
                      wcat[:, c_start * 128:(c_start + n) * 128].re("(kc p) c -> p kc c", p=128))
                P.dma("act", wbf[c_start], w[:, :, 0:n * 128])
            else:
                P.dma("sp", w[:, :, 0:n * 128], wbf[c_start])
            return w

        def proj_fm(w, j, M0=0, M1=128):
            pt = pp.get()
            for kc in range(8):
                P.mm(pt[0:M1 - M0, 0:TT], w[:, kc, j * 128 + M0:j * 128 + M1], hT[:, kc, :],
                     start=(kc == 0), stop=(kc == 7))
            return pt
    P.barrier()

    for ti in range(NT):
        c0t = ti * TT
        P.phase_begin()
        mkpools(nsq=2, nrstd=1)
        xt = P.sb([128, 8, TT], F32, "xt")
        P.dma("sp", xt, xT[:, c0t:c0t + TT].re("(kc p) t -> p kc t", p=128))
        if do_pro:
            ya_p = RR(lambda i: P.sb([128, TT], F32, f"ya{i}"), 2)
            yb_p = RR(lambda i: P.sb([128, TT], F32, f"yb{i}"), 2)

            def ysum_chunk(kc):
                ya = ya_p.get(); yb = yb_p.get()
                P.dma("act", ya, yA[kc * 128:(kc + 1) * 128, c0t:c0t + TT])
                P.dma("act", yb, yB[kc * 128:(kc + 1) * 128, c0t:c0t + TT])
                P.tt(ya, ya, yb, ALU.add, eng="pool")
                return ya
            pt = px.get()
            for kc in range(8):
                ys_ = ysum_chunk(kc)
                sq = cur["sq"].get()
                P.act(sq, ys_, AF.Square)
                P.mm(pt[:, 0:TT], ones, sq, start=(kc == 0), stop=(kc == 7))
            r = cur["rstd"].get()
            P.act(r, pt[:, 0:TT], AF.Sqrt, bias=eps_sb[:, 0:1], scale=1.0 / D)
            recip(r)
            for kc in range(8):
                ys_ = ysum_chunk(kc)
                P.tt(ys_, ys_, r, ALU.mult)
                P.stt(xt[:, kc, :], ys_, gp[:, kc:kc + 1], xt[:, kc, :], ALU.mult, ALU.add)
            P.dma("sp", xT_out[:, c0t:c0t + TT].re("(kc p) t -> p kc t", p=128), xt, is_output=True)
        if not do_body:
            P.phase_end()
            continue
        r = rms_rstd([xt[:, kc, :] for kc in range(8)], D)
        for kc in range(8):
            t = cur["sq"].get()
            P.tt(t, xt[:, kc, :], r, ALU.mult)
            P.ts(hT[:, kc, :], t, sc1[:, kc:kc + 1], ALU.mult, shift[:, kc:kc + 1], ALU.add)
        P.phase_end()

        if DEBUG_STOP == 0:
            break
        P.phase_begin()
        mkpools(nsq=2, nrstd=1, nscr=3, nwst=2)
        scr = cur["scr"]
        lat = P.sb([128, 3, TT], F32, "lat")
        kr = P.sb([64, 2, TT], F32, "kr")
        qn = P.sb([128, 2, TT], BF16, "qn")
        kvn = P.sb([128, TT], BF16, "kvn")
        pt_pool = RR(lambda i: P.sb([128, TT], BF16, f"ptp{i}"), 3)
        attn_tok = P.sb([128, 4, 256], F32, "attn_tok")
        CC = P.sb([64, TT], F32, "CC"); SSt = P.sb([64, TT], F32, "SS")
        ki = P.sb([64, TT], I32, "ki")
        qT = [P.sb([128, TT], BF16, f"qT{h}") for h in range(4)]
        for h in range(4):
            P.memset(qT[h][0:32, :], 0.0, eng="pool")
        ksq = P.sb([128, TT], F32, "ksq"); P.memset(ksq[0:32, :], 0.0)
        ang = cur["sq"].get()
        a64 = ang[0:64, :]
        P.dma("sp", ki, pos_in[:, c0t:c0t + TT])
        P.copy(a64, ki)
        P.ts(a64, a64, invf_sb[:, 0:1], ALU.mult)
        kq = cur["rstd"].get()[0:64, :]
        P.ts(kq, a64, 1.0 / TWO_PI, ALU.mult)
        P.copy(ki, kq)
        P.copy(kq, ki)
        P.stt(a64, kq, -6.28125, a64, ALU.mult, ALU.add)
        P.stt(a64, kq, -(TWO_PI - 6.28125), a64, ALU.mult, ALU.add)
        m = cur["sq"].get()[0:64, :]
        P.ts(m, a64, math.pi, ALU.is_gt, -TWO_PI, ALU.mult)
        P.tt(a64, a64, m, ALU.add)
        P.ts(m, a64, -math.pi, ALU.is_lt, TWO_PI, ALU.mult)
        P.tt(a64, a64, m, ALU.add)
        P.act(SSt, a64, AF.Sin)
        P.ts(SSt, SSt, invf_sb[:, 1:2], ALU.mult)
        P.ts(a64, a64, math.pi / 2, ALU.add)
        P.ts(m, a64, math.pi, ALU.is_gt, -TWO_PI, ALU.mult)
        P.tt(a64, a64, m, ALU.add)
        P.act(CC, a64, AF.Sin)
        w = wload(0, 4)
        for j in range(3):
            pt = proj_fm(w, j)
            P.copy(lat[:, j, :], pt[:, 0:TT], eng="act")
        for s in range(2):
            pt = proj_fm(w, 3, 64 * s, 64 * s + 64)
            P.copy(kr[:, s, :], pt[0:64, 0:TT], eng="act")
        w = wload(19, 4)
        for j in range(4):
            pt = proj_fm(w, j)
            P.act(zs[:, j, :], pt[:, 0:TT], AF.Silu)
        w = wload(23, 2)
        for j in range(2):
            pt = proj_fm(w, j)
            P.act(zs[:, 4 + j, :], pt[:, 0:TT], AF.Silu)
        r = rms_rstd([lat[:, 0, :], lat[:, 1, :]], 256)
        for j in range(2):
            t = cur["sq"].get()
            P.tt(t, lat[:, j, :], r, ALU.mult)
            P.ts(qn[:, j, :], t, vec[:, 9 + j:10 + j], ALU.mult)
        r = rms_rstd([lat[:, 2, :]], 128)
        t = cur["sq"].get()
        P.tt(t, lat[:, 2, :], r, ALU.mult)
        P.ts(kvn, t, vec[:, 11:12], ALU.mult)
        krot = scr.get()
        t1 = scr.get()
        P.tt(krot[32:64, :], kr[32:64, 0, :], CC[32:64, :], ALU.mult)
        P.tt(t1[32:64, :], kr[32:64, 1, :], SSt[32:64, :], ALU.mult)
        P.tt(krot[32:64, :], krot[32:64, :], t1[32:64, :], ALU.add)
        P.act(ksq[32:64, :], krot[32:64, :], AF.Square)
        for h in range(4):
            P.copy(Kc[32:64, h, c0t:c0t + TT], krot[32:64, :], eng="pool")
            pt = pp.get()
            P.mm(pt[:, 0:TT], wukvk_sb[:, h * 128:(h + 1) * 128], kvn)
            P.copy(Kc[64:128, h, c0t:c0t + TT], pt[64:128, 0:TT], eng="act")
            P.act(ksq[64:128, :], pt[64:128, 0:TT], AF.Square)
            p1 = px.get()
            P.mm(p1[0:1, 0:TT], ones[:, 0:1], ksq)
            P.op("dve", (lambda o, i: (lambda e: e.tensor_reduce(out=o, in_=i, op=ALU.max, axis=AX.X)))(small[0:1, h:h + 1].ap, p1[0:1, 0:TT].ap),
                 reads=[p1], writes=[small])
            P.tt(kmax2[0:1, h:h + 1], kmax2[0:1, h:h + 1], small[0:1, h:h + 1], ALU.max)
        for tb in range(4):
            pt = pp.get()
            P.mm(pt[:, 0:256], kvn[:, tb * 128:(tb + 1) * 128], wukvv_sb)
            blk = ti * 4 + tb
            P.copy(Vc[:, blk, :, 0:64], pt[:, 0:256].re("p (h d) -> p h d", h=4), eng="act")
        for h in range(4):
            pq = pp.get()
            for j in range(2):
                P.mm(pq[:, 0:TT], wuq_sb[:, j, h * 192:h * 192 + 128], qn[:, j, :], start=(j == 0), stop=(j == 1))
            psw = pp.get()
            for j in range(2):
                P.mm(psw[0:64, 0:TT], wuq_sb[:, j, h * 192 + 128:h * 192 + 192], qn[:, j, :], start=(j == 0), stop=(j == 1))
            qf = scr.get()
            P.copy(qf, pq[:, 0:TT], eng="act")
            sq = cur["sq"].get()
            P.act(sq, qf, AF.Square)
            p1 = px.get()
            P.mm(p1[0:1, 0:TT], ones[:, 0:1], sq)
            t1 = scr.get()
            t2 = scr.get()
            P.tt(t1[32:64, :], qf[32:64, :], CC[32:64, :], ALU.mult)
            P.tt(t2[32:64, :], psw[32:64, 0:TT], SSt[32:64, :], ALU.mult)
            P.tt(qT[h][32:64, :], t1[32:64, :], t2[32:64, :], ALU.add)
            P.copy(qT[h][64:128, :], qf[64:128, :], eng="pool")
            P.act(t1[0:1, :], p1[0:1, 0:TT], AF.Sqrt, scale=kmax2[0:1, h:h + 1])
            P.ts(qT[h][0:1, :], t1[0:1, :], -1.0, ALU.mult)
            nkb = 4 * ti + 4
            po = po_t
            first_mm = True
            for kb in range(nkb):
                j = kb - 4 * ti
                cq = 128 * j if j > 0 else 0
                sT = pa.get()
                P.mm(sT[:, cq:TT], Kc[:, h, kb * 128:(kb + 1) * 128], qT[h][:, cq:TT])
                pt_ = pt_pool.get()
                P.act(pt_[:, cq:TT], sT[:, cq:TT], AF.Exp, scale=SCALE)
                if j >= 0:
                    P.memset(pt_[64:128, cq:cq + 64], 0.0, eng="pool")
                for qb in range(4):
                    if 128 * qb < cq:
                        continue
                    o_ap, l_ap, r_ap = po[:, qb * 128:qb * 128 + 65].ap, pt_[:, qb * 128:(qb + 1) * 128].ap, Vc[:, kb, h, :].ap
                    st_, sp_ = first_mm, (kb == nkb - 1 and qb == 3)
                    P.op("pe", (lambda o, l, r_, a, b: (lambda e: e.matmul(o, l, r_, start=a, stop=b, skip_group_check=True)))(o_ap, l_ap, r_ap, st_, sp_),
                         reads=[pt_, Vc], writes=[po])
                    first_mm = False
            rec = small[:, 8:12]
            pov = po[:, 0:512].re("p (q c) -> p q c", q=4)
            P.op("dve", (lambda o, i: (lambda e: e.reciprocal(out=o, in_=i)))(rec.ap, pov[:, :, 64].ap), reads=[po], writes=[small])
            P.tt(attn_tok[:, :, h * 64:(h + 1) * 64], pov[:, :, 0:64], rec.with_ap(rec.ap.unsqueeze(2).to_broadcast([128, 4, 64])), ALU.mult)
        for j in range(2):
            ptp = px.get()
            for qb in range(4):
                P.tr(ptp[:, qb * 128:(qb + 1) * 128], attn_tok[:, qb, j * 128:(j + 1) * 128], ident)
            P.tt(brT[:, j, :], ptp[:, 0:TT], zs[:, j, :], ALU.mult)
        P.phase_end()

        if DEBUG_STOP == 1:
            break
        P.phase_begin()
        mkpools(nsq=2, nrstd=0, nscr=0, nwst=2)
        uT = P.sb([128, 2, TT], F32, "uT")
        vt = P.sb([128, 4, 512], F32, "vt")
        vnb = P.sb([128, 4, 256], BF16, "vnb")
        w = wload(4, 2)
        for j in range(2):
            pt = proj_fm(w, j)
            P.act(uT[:, j, :], pt[:, 0:TT], AF.Gelu)
        w = wload(6, 4)
        for tb in range(4):
            pt = pp.get()
            for kc in range(8):
                P.mm(pt[:, 0:512], hT[:, kc, tb * 128:(tb + 1) * 128], w[:, kc, 0:512], start=(kc == 0), stop=(kc == 7))
            P.act(vt[:, tb, :], pt[:, 0:512], AF.Gelu)
        for tb in range(4):
            P.op("dve", (lambda o, i: (lambda e: e.bn_stats(out=o, in_=i)))(st6[:, tb, :].ap, vt[:, tb, :].ap), reads=[vt], writes=[st6])
            P.op("dve", (lambda o, i: (lambda e: e.bn_aggr(out=o, in_=i)))(mv[:, tb, :].ap, st6[:, tb, :].ap), reads=[st6], writes=[mv])
        P.act(rs4, mv[:, :, 1], AF.Sqrt, bias=eps_sb[:, 1:2])
        recip(rs4)
        P.stt(nm4, mv[:, :, 0], -1.0, rs4, ALU.mult, ALU.mult)
        for tb in range(4):
            t = cur["sq"].get()
            P.ts(t[:, 0:256], vt[:, tb, 0:256], rs4[:, tb:tb + 1], ALU.mult, nm4[:, tb:tb + 1], ALU.add)
            P.tt(t[:, 0:256], t[:, 0:256], LG, ALU.mult)
            P.tt(vnb[:, tb, :], t[:, 0:256], LBt, ALU.add)
        for g in range(2):
            pg_ = px.get()
            for tb in range(4):
                P.mm(pg_[:, tb * 128:(tb + 1) * 128], vnb[:, tb, g * 128:(g + 1) * 128], wsT_sb[:, g, :], start=True, stop=False)
                P.mm(pg_[:, tb * 128:(tb + 1) * 128], ones[0:1, 0:128], bs_sb[0:1, g, :], start=False, stop=True)
            t = cur["sq"].get()
            P.tt(t, uT[:, g, :], zs[:, 2 + g, :], ALU.mult, eng="pool")
            P.tt(brT[:, 2 + g, :], pg_[:, 0:TT], t, ALU.mult)
        P.phase_end()

        if DEBUG_STOP == 2:
            break
        P.phase_begin()
        big = lambda name: P.sb([128, 2, TT], F32, name)
        e_inc = big("e_inc"); BT = big("BT"); KT = big("KT"); vm = big("vm"); rk = big("rk")
        AR = P.sb([128, 2, 8, 2, 64], F32, "AR")
        P.phase_begin()
        rws = P.sb([128, 9, TT], F32, "rws")
        P.phase_begin()
        mkpools(nsq=2, nrstd=0, nscr=0, nwst=2)
        rwraw = P.sb([128, 9, TT + 1], F32, "rwraw")
        P.copy(rwraw[:, :, 0:1], rwlast)
        for (cs_, n_, base) in ((10, 4, 0), (14, 4, 4), (18, 1, 8)):
            w = wload(cs_, n_)
            for j in range(n_):
                pt = proj_fm(w, j)
                P.copy(rwraw[:, base + j, 1:TT + 1], pt[:, 0:TT], eng="act")
        for idx in range(9):
            t = cur["sq"].get()
            P.tt(t, rwraw[:, idx, 0:TT], rwraw[:, idx, 1:TT + 1], ALU.subtract)
            P.stt(rws[:, idx, :], t, vec[:, idx:idx + 1], rwraw[:, idx, 1:TT + 1], ALU.mult, ALU.add)
        P.copy(rwlast, rwraw[:, :, TT:TT + 1])
        P.phase_end()
        if DEBUG_STOP == 10:
            P.phase_end(); P.phase_end(); break
        P.phase_begin()
        mkpools(nsq=2, nrstd=0, nscr=2, nwst=0)
        scr = cur["scr"]
        lo_sb = P.sb([32, TT], F32, "lo_sb")
        tmp1 = lambda name: P.sb([128, TT], F32, name)
        sgw = tmp1("sgw"); asig = tmp1("asig"); cs = tmp1("cs"); exl = tmp1("exl"); e_exc = tmp1("e_exc")
        e_neg = tmp1("e_neg"); kkn = tmp1("kkn"); k2 = tmp1("k2")
        v4 = lambda x: x.re("p (c t) -> p c t", t=64)
        th = P.sb([64, TT], F32, "th")
        P.act(th[0:64, :], rws[0:64, 8, :], AF.Tanh)
        if not first:
            plo = px.get()
            for j in range(4):
                P.mm(plo[0:32, 0:TT], v1_sb[:, j, :], rws[:, 4 + j, :], start=(j == 0), stop=(j == 3))
            P.copy(lo_sb, plo[0:32, 0:TT], eng="act")
        for hp in range(2):
            pw = px.get()
            P.mm(pw[:, 0:TT], wa2_sb[0:64, hp * 128:(hp + 1) * 128], th[0:64, :])
            P.act(sgw, pw[:, 0:TT], AF.Sigmoid, bias=vec[:, 12 + hp:13 + hp])
            pa_ = px.get()
            P.mm(pa_[:, 0:TT], wa2_sb[64:128, hp * 128:(hp + 1) * 128], rws[64:128, 8, :])
            P.act(asig, pa_[:, 0:TT], AF.Sigmoid, bias=vec[:, 14 + hp:15 + hp])
            for c in range(8):
                o_, a_, b_ = cs[:, c * 64:(c + 1) * 64].ap, ones[:, 0:64].ap, sgw[:, c * 64:(c + 1) * 64].ap
                P.op("dve", (lambda o, a, b: (lambda e: e.tensor_tensor_scan(out=o, data0=a, data1=b, initial=0.0, op0=ALU.mult, op1=ALU.add)))(o_, a_, b_),
                     reads=[ones, sgw], writes=[cs])
            P.tt(exl, cs, sgw, ALU.subtract)
            P.act(e_inc[:, hp, :], cs, AF.Exp, scale=-C0)
            P.act(e_exc, exl, AF.Exp, scale=-C0)
            P.act(e_neg, cs, AF.Exp, scale=C0)
            if first:
                P.copy(vm[:, hp, :], rws[:, 4 + hp, :], eng="pool")
            else:
                vf = exl
                P.dma("sp", vf, vf_in[hp * 128:(hp + 1) * 128, c0t:c0t + TT])
                pv = px.get()
                P.mm(pv[:, 0:TT], v2_sb[0:32, hp * 128:(hp + 1) * 128], lo_sb[0:32, :])
                sv = scr.get()
                P.act(sv, pv[:, 0:TT], AF.Sigmoid, bias=vec[:, 20 + hp:21 + hp])
                P.tt(vf, vf, rws[:, 4 + hp, :], ALU.subtract)
                P.tt(vf, vf, sv, ALU.mult)
                P.tt(vm[:, hp, :], vf, rws[:, 4 + hp, :], ALU.add)
            kk = scr.get()
            P.ts(kk, rws[:, 2 + hp, :], vec[:, 16 + hp:17 + hp], ALU.mult)
            sq = cur["sq"].get()
            P.act(sq, kk, AF.Square)
            pn = px.get()
            P.mm(pn[:, 0:TT], onesblk, sq)
            nr = cur["sq"].get()
            P.act(nr, pn[:, 0:TT], AF.Sqrt)
            P.ts(nr, nr, 1e-12, ALU.max)
            recip(nr)
            P.tt(kkn, kk, nr, ALU.mult)
            t = cur["sq"].get()
            P.ts(t, asig, vec[:, 18 + hp:19 + hp], ALU.mult, omka[:, hp:hp + 1], ALU.add)
            P.tt(k2, rws[:, 2 + hp, :], t, ALU.mult)
            P.stt(AR[:, hp, :, 0, :], v4(kkn), -1.0, v4(e_exc), ALU.mult, ALU.mult)
            P.tt(AR[:, hp, :, 1, :], v4(rws[:, hp, :]), v4(e_inc[:, hp, :]), ALU.mult)
            P.tt(BT[:, hp, :], kkn, asig, ALU.mult)
            P.tt(BT[:, hp, :], BT[:, hp, :], e_neg, ALU.mult)
            P.tt(KT[:, hp, :], k2, e_neg, ALU.mult)
            P.tt(rk[:, hp, :], rws[:, hp, :], k2, ALU.mult, eng="pool")
        if first:
            P.dma("sp", vf_out[:, c0t:c0t + TT].re("(j p) t -> p j t", p=128), vm, is_output=True)
        P.phase_end()
        P.phase_end()
        if DEBUG_STOP == 11:
            P.phase_end(); break
        sel = ident[:, 64:128]
        AR2 = P.sb([64, 2, 8, 2, 64], F32, "AR2")
        BT2 = P.sb([64, 2, TT], F32, "BT2"); KT2 = P.sb([64, 2, TT], F32, "KT2")
        G = P.sb([64, 8, 4], F32, "G")
        e4 = e_inc.re("p h (c t) -> p h c t", t=64)
        for hp in range(2):
            for half in range(2):
                pl = px.get()
                P.mm(pl[0:64, 0:512], sel, AR[:, hp, 4 * half:4 * half + 4, :, :].re("p c a t -> p (c a t)"))
                P.copy(AR2[:, hp, 4 * half:4 * half + 4, :, :].re("p c a t -> p (c a t)"), pl[0:64, 0:512], eng="act")
            pl = px.get()
            P.mm(pl[0:64, 0:512], sel, BT[:, hp, :])
            P.copy(BT2[:, hp, :], pl[0:64, 0:512], eng="act")
            pl = px.get()
            P.mm(pl[0:64, 0:512], sel, KT[:, hp, :])
            P.copy(KT2[:, hp, :], pl[0:64, 0:512], eng="act")
            pl = px.get()
            P.mm(pl[0:64, 0:8], sel, e4[:, hp, :, 63])
            P.copy(G[:, :, 2 * hp + 1], pl[0:64, 0:8], eng="act")
            P.copy(G[:, :, 2 * hp], e4[0:64, hp, :, 63], eng="pool")
        ARh = lambda hp, hh: (AR if hh == 0 else AR2)[0:64, hp]
        BTh = lambda hp, hh: (BT if hh == 0 else BT2)[0:64, hp]
        KTh = lambda hp, hh: (KT if hh == 0 else KT2)[0:64, hp]
        t2 = lambda name, w=64: [P.sb([64, 2, w], F32, f"{name}{hp}") for hp in range(2)]
        tokb = [P.sb([64, 3, 128], F32, f"tok{hp}") for hp in range(2)]
        bcob = [P.sb([64, 2], F32, f"bco{hp}") for hp in range(2)]
        S1mb = t2("S1m", 128); S2mb = t2("S2m", 128)
        Pmb = [t2("PmA"), t2("PmB")]; PTmb = [t2("PTmA"), t2("PTmB")]; STb = [t2("STA"), t2("STB")]
        W0b = t2("W0"); U0b = t2("U0"); Y0b = t2("Y0"); W1b = t2("W1"); Ub = t2("U"); Yb = t2("Y")
        ynb = t2("yn"); yob = t2("yo"); HGb = t2("HG")
        st6b = [P.sb([64, 2, 6], F32, f"st6h{hp}") for hp in range(2)]
        mvb = [P.sb([64, 2, 2], F32, f"mvh{hp}") for hp in range(2)]
        rsb = [P.sb([64, 2], F32, f"rsh{hp}") for hp in range(2)]
        v2h = lambda x: x.re("p (h x) -> p h x", h=2)
        mS2 = maskS.with_ap(maskS.ap.unsqueeze(1).to_broadcast([64, 2, 128]))
        mL2 = maskL.with_ap(maskL.ap.unsqueeze(1).to_broadcast([64, 2, 64]))
        idb2 = ident.with_ap(ident[0:64, 0:64].ap.unsqueeze(1).to_broadcast([64, 2, 64]))

        def chunk_gen(c, hp):
            cc = slice(c * 64, (c + 1) * 64)
            Hs = Hst[:, 2 * hp:2 * hp + 2, :].k(hp)
            tok = tokb[hp]; bco = bcob[hp]
            ptk = px.get()
            for s_, src in enumerate((BT, KT, vm)):
                P.tr(ptk[0:64, s_ * 128:(s_ + 1) * 128], src[:, hp, cc], ident)
            pbq = px.get()
            P.mm(pbq[0:64, 0:2], rk[:, hp, cc], rkblk_sb[:, hp, :])
            P.copy(tok, ptk[0:64, 0:384].re("p (s j) -> p s j", s=3), eng="act")
            P.copy(bco, pbq[0:64, 0:2], eng="act")
            yield
            p1 = px.get(); p3 = px.get()
            for hh in range(2):
                arr = ARh(hp, hh)[:, c, :, :].re("p a t -> p (a t)")
                P.mm(p1[0:64, hh * 128:(hh + 1) * 128], BTh(hp, hh)[:, cc], arr)
                P.mm(p1[0:64, 256 + hh * 128:256 + (hh + 1) * 128], KTh(hp, hh)[:, cc], arr)
                P.mm(p3[0:64, hh * 64:(hh + 1) * 64], ARh(hp, hh)[:, c, 0, :], BTh(hp, hh)[:, cc])
            S1m = S1mb[hp]; S2m = S2mb[hp]
            Pm = Pmb[0][hp]; PTm = PTmb[0][hp]; ST = STb[0][hp]
            P.tt(S1m, v2h(p1[0:64, 0:256]), mS2, ALU.mult)
            P.tt(S2m, v2h(p1[0:64, 256:512]), mS2, ALU.mult)
            P.tt(Pm, v2h(p3[0:64, 0:128]), mL2, ALU.mult)
            P.copy(PTm, S1m[:, :, 0:64], eng="pool")
            P.tt(ST, S1m[:, :, 0:64], idb2, ALU.add)
            yield
            for it in range(5):
                pq_ = px.get()
                for hh in range(2):
                    P.mm(pq_[0:64, hh * 64:(hh + 1) * 64], PTm[:, hh, :], Pm[:, hh, :])
                    if it < 4:
                        P.mm(pq_[0:64, 128 + hh * 64:128 + (hh + 1) * 64], Pm[:, hh, :], PTm[:, hh, :])
                nb = (it + 1) % 2
                Pn = Pmb[nb][hp]; PTn = PTmb[nb][hp]; STn = STb[nb][hp]
                P.copy(Pn, v2h(pq_[0:64, 0:128]), eng="act")
                if it < 4:
                    P.copy(PTn, v2h(pq_[0:64, 128:256]), eng="act")
                yield
                ps_ = px.get()
                for hh in range(2):
                    P.mm(ps_[0:64, hh * 64:(hh + 1) * 64], Pn[:, hh, :], ST[:, hh, :])
                P.tt(STn, v2h(ps_[0:64, 0:128]), ST, ALU.add)
                Pm, PTm, ST = Pn, PTn, STn
                yield
            vtok = lambda hh: tok[:, 2, hh * 64:(hh + 1) * 64]
            pw0 = px.get()
            for hh in range(2):
                P.mm(pw0[0:64, hh * 64:(hh + 1) * 64], S2m[:, hh, 0:64], vtok(hh))
                P.mm(pw0[0:64, 128 + hh * 64:128 + (hh + 1) * 64], S2m[:, hh, 64:128], vtok(hh))
            W0 = W0b[hp]; Y0 = Y0b[hp]; U0 = U0b[hp]; W1 = W1b[hp]; U = Ub[hp]; Y = Yb[hp]; HG = HGb[hp]
            P.copy(W0, v2h(pw0[0:64, 0:128]), eng="act")
            P.copy(Y0, v2h(pw0[0:64, 128:256]), eng="act")
            yield
            pu0 = px.get()
            for hh in range(2):
                P.mm(pu0[0:64, hh * 64:(hh + 1) * 64], ST[:, hh, :], W0[:, hh, :])
            P.copy(U0, v2h(pu0[0:64, 0:128]), eng="act")
            yield
            pw1 = px.get()
            for hh in range(2):
                P.mm(pw1[0:64, hh * 64:(hh + 1) * 64], ARh(hp, hh)[:, c, 0, :], Hs[:, hh, :])
            P.copy(W1, v2h(pw1[0:64, 0:128]))
            yield
            pu = px.get()
            for hh in range(2):
                P.mm(pu[0:64, hh * 64:(hh + 1) * 64], ST[:, hh, :], W1[:, hh, :])
            P.tt(U, v2h(pu[0:64, 0:128]), U0, ALU.add)
            yield
            py = px.get()
            for hh in range(2):
                P.mm(py[0:64, hh * 64:(hh + 1) * 64], ARh(hp, hh)[:, c, 1, :], Hs[:, hh, :], start=True, stop=False)
                P.mm(py[0:64, hh * 64:(hh + 1) * 64], S1m[:, hh, 64:128], U[:, hh, :], start=False, stop=True)
            for hh in range(2):
                P.mm(py[0:64, 128 + hh * 64:128 + (hh + 1) * 64], tok[:, 1, hh * 64:(hh + 1) * 64], vtok(hh), start=True, stop=False)
                P.mm(py[0:64, 128 + hh * 64:128 + (hh + 1) * 64], tok[:, 0, hh * 64:(hh + 1) * 64], U[:, hh, :], start=False, stop=True)
            Gb = G.with_ap(G[:, c, 2 * hp:2 * hp + 2].ap.unsqueeze(2).to_broadcast([64, 2, 64]))
            P.tt(HG, v2h(py[0:64, 128:256]), Hs, ALU.add)
            P.tt(Hs, HG, Gb, ALU.mult)
            P.tt(Y, v2h(py[0:64, 0:128]), Y0, ALU.add)
            yield
            st6h = st6b[hp]; mvh = mvb[hp]; rsh = rsb[hp]; yn = ynb[hp]; yo = yob[hp]
            for hh in range(2):
                P.op("dve", (lambda o, i: (lambda e: e.bn_stats(out=o, in_=i)))(st6h[:, hh, :].ap, Y[:, hh, :].ap), reads=[Y], writes=[st6h])
                P.op("dve", (lambda o, i: (lambda e: e.bn_aggr(out=o, in_=i)))(mvh[:, hh, :].ap, st6h[:, hh, :].ap), reads=[st6h], writes=[mvh])
            P.act(rsh, mvh[:, :, 1], AF.Sqrt, bias=eps_sb[0:64, 2:3])
            yield
            recip(rsh)
            for hh in range(2):
                P.ts(yn[:, hh, :], Y[:, hh, :], mvh[:, hh, 0:1], ALU.subtract, rsh[:, hh:hh + 1], ALU.mult)
            ynf = yn.re("p h x -> p (h x)")
            P.tt(ynf, ynf, LXG[:, hp * 128:(hp + 1) * 128], ALU.mult, eng="pool")
            P.tt(ynf, ynf, LXB[:, hp * 128:(hp + 1) * 128], ALU.add, eng="pool")
            yield
            for hh in range(2):
                P.stt(yo[:, hh, :], vtok(hh), bco[:, hh:hh + 1], yn[:, hh, :], ALU.mult, ALU.add)
            pyt = px.get()
            P.tr(pyt[:, 0:64], yo.re("p h x -> p (h x)"), ident[0:64, 0:64])
            P.tt(brT[:, 4 + hp, cc], pyt[:, 0:64], zs[:, 4 + hp, cc], ALU.mult)
            yield

        def drive(gens):
            gens = list(gens)
            while gens:
                for g_ in list(gens):
                    try:
                        next(g_)
                    except StopIteration:
                        gens.remove(g_)

        for c in range(8):
            drive([chunk_gen(c, 0), chunk_gen(c, 1)])
        P.phase_end()

        if DEBUG_STOP == 3:
            break
        P.phase_begin()
        mkpools(nsq=1, nrstd=0, nscr=3, nwst=2)
        scr = cur["scr"]
        mT = P.sb([128, 8, TT], BF16, "mT")
        sg = RR(lambda i: P.sb([128, 3, TT], F32, f"sg{i}"), 1)
        ystage = RR(lambda i: P.sb([128, TT], F32, f"ystage{i}"), 2)
        for dc in range(8):
            w = wload(G_BASE + dc * 3, 3)
            sgt = sg.get()
            for n in range(3):
                pt = proj_fm(w, n)
                P.act(sgt[:, n, :], pt[:, 0:TT], AF.Sigmoid)
            ts_ = []
            for n in range(3):
                pj = pp.get()
                for j in range(2):
                    P.mm(pj[:, 0:TT], wbr_sb[:, n * 2 + j, dc * 128:(dc + 1) * 128], brT[:, n * 2 + j, :], start=(j == 0), stop=(j == 1))
                t = scr.get()
                P.tt(t, pj[:, 0:TT], sgt[:, n, :], ALU.mult)
                ts_.append(t)
            P.tt(ts_[0], ts_[0], ts_[1], ALU.add, eng="pool")
            P.tt(mT[:, dc, :], ts_[0], ts_[2], ALU.add, eng="pool")
        for dc2 in range(8):
            pt = pp.get()
            for dc in range(8):
                P.mm(pt[:, 0:TT], wout_sb[:, dc, dc2 * 128:(dc2 + 1) * 128], mT[:, dc, :], start=(dc == 0), stop=(dc == 7))
            ys = ystage.get()
            P.copy(ys, pt[:, 0:TT], eng="act")
            P.dma("sp", yP[dc2 * 128:(dc2 + 1) * 128, c0t:c0t + TT], ys, is_output=True)
        P.phase_end()

    P.emit()
    return nc


_Q, _KV, _KR = 0, 256, 384
_GMU, _GMV = 416, 928
_RWR, _RWK, _RWV, _RWW, _RWA = 1440, 1952, 2464, 2976, 3040
_ZM, _ZG, _ZR = 3104, 3616, 4128
_G = 4640


def _pc(v, n):
    return np.ascontiguousarray(np.asarray(v, np.float32).reshape(n, 128).T)


def consts():
    ident = np.eye(128, dtype=np.float32)
    onesblk = np.zeros((128, 128), np.float32)
    onesblk[:64, :64] = 1.0
    onesblk[64:, 64:] = 1.0
    s = np.arange(64)[:, None]
    t = np.arange(64)[None, :]
    maskS = np.concatenate([(s < t), (s <= t)], axis=1).astype(np.float32)
    maskL = (np.arange(64)[None, :] < np.arange(64)[:, None]).astype(np.float32)
    return dict(c_ident=ident, c_onesblk=onesblk, c_maskS=maskS, c_maskL=maskL)


def prep_body(inp, l, b, hs, T):
    W = np.asarray(inp["w_in"][l], np.float32)
    own = slice(256 * hs, 256 * hs + 256)
    oth = slice(256 * (1 - hs), 256 * (1 - hs) + 256)
    z32 = np.zeros((1024, 32), np.float32)
    kr = W[:, _KR:_KR + 32]
    krsw = np.concatenate([kr[:, 16:32], kr[:, 0:16]], axis=1)
    cols = [W[:, _Q:_Q + 256], W[:, _KV:_KV + 128], z32, kr, z32, krsw,
            W[:, _GMU:_GMU + 512][:, own],
            W[:, _GMV:_GMV + 512][:, own], W[:, _GMV:_GMV + 512][:, oth],
            W[:, _RWR:_RWR + 512][:, own], W[:, _RWK:_RWK + 512][:, own],
            W[:, _RWV:_RWV + 512][:, own], W[:, _RWV:_RWV + 512][:, oth],
            W[:, _RWW:_RWW + 64], W[:, _RWA:_RWA + 64],
            W[:, _ZM:_ZM + 512][:, own], W[:, _ZG:_ZG + 512][:, own], W[:, _ZR:_ZR + 512][:, own]]
    Wg = W[:, _G:_G + 3072].reshape(1024, 3, 8, 128)
    cols.append(np.ascontiguousarray(Wg.transpose(0, 2, 1, 3)).reshape(1024, 3072))
    wcat = np.ascontiguousarray(np.concatenate(cols, axis=1))
    assert wcat.shape[1] == NCH_W * 128, wcat.shape
    wada = np.asarray(inp["w_ada"][l], np.float32)
    bada = np.asarray(inp["b_ada"][l], np.float32)
    mu = np.asarray(inp["rw_mu"][l], np.float32)
    mu_r, mu_k, mu_v = mu[0:512], mu[512:1024], mu[1024:1536]
    mu_l = mu[1536:1664]
    vec = np.zeros((128, 40), np.float32)
    vec[:, 0:2] = _pc(mu_r[own], 2); vec[:, 2:4] = _pc(mu_k[own], 2)
    vec[:, 4:6] = _pc(mu_v[own], 2); vec[:, 6:8] = _pc(mu_v[oth], 2)
    vec[:, 8] = mu_l
    vec[:, 9:11] = _pc(inp["mla_q_norm"][l], 2)
    vec[:, 11] = np.asarray(inp["mla_kv_norm"][l], np.float32)
    vec[:, 12:14] = _pc(np.asarray(inp["rw_w0"][l])[own], 2)
    vec[:, 14:16] = _pc(np.asarray(inp["rw_a0"][l])[own], 2)
    vec[:, 16:18] = _pc(np.asarray(inp["rw_k_k"][l])[own], 2)
    vec[:, 18:20] = _pc(np.asarray(inp["rw_k_a"][l])[own], 2)
    if l > 0:
        vec[:, 20:22] = _pc(np.asarray(inp["rw_v0"][l - 1])[own], 2)
    pos = np.ascontiguousarray(np.broadcast_to(np.asarray(inp["positions"][b], np.int32)[None, :T], (64, T)))
    invf = np.zeros((64, 2), np.float32)
    f = (10000.0 ** (-np.arange(0, 32, 2, dtype=np.float32) / 32)).astype(np.float32)
    invf[32:48, 0] = f; invf[48:64, 0] = f
    invf[32:48, 1] = -1.0; invf[48:64, 1] = 1.0
    wuq_full = np.asarray(inp["mla_w_uq"][l], np.float32)
    wuq = np.zeros((256, 4, 192), np.float32)
    for hh in range(4):
        h = 4 * hs + hh
        nope = wuq_full[:, h * 96:h * 96 + 64]
        rope = wuq_full[:, h * 96 + 64:h * 96 + 96]
        wuq[:, hh, 32:64] = rope
        wuq[:, hh, 64:128] = nope
        wuq[:, hh, 160:176] = rope[:, 16:32]
        wuq[:, hh, 176:192] = rope[:, 0:16]
    wukv = np.asarray(inp["mla_w_ukv"][l], np.float32)
    wukvk = np.zeros((128, 4, 128), np.float32)
    wukvv = np.zeros((128, 4, 64), np.float32)
    for hh in range(4):
        h = 4 * hs + hh
        wukvk[:, hh, 64:128] = wukv[:, h * 128:h * 128 + 64]
        wukvv[:, hh, :] = wukv[:, h * 128 + 64:h * 128 + 128]
    lng = np.asarray(inp["gm_ln_g"][l], np.float32); lnb = np.asarray(inp["gm_ln_b"][l], np.float32)
    lng = np.concatenate([lng[own], lng[oth]])[None, :]
    lnb = np.concatenate([lnb[own], lnb[oth]])[None, :]
    ws = np.asarray(inp["gm_w_s"][l], np.float32)[2 * hs:2 * hs + 2]
    wsT = np.ascontiguousarray(ws.transpose(2, 0, 1))
    bs = np.ascontiguousarray(np.asarray(inp["gm_b_s"][l], np.float32)[2 * hs:2 * hs + 2][None])
    wa2 = np.concatenate([np.asarray(inp["rw_w2"][l], np.float32)[:, own],
                          np.asarray(inp["rw_a2"][l], np.float32)[:, own]], axis=0)
    rk = np.asarray(inp["rw_r_k"][l], np.float32)[4 * hs:4 * hs + 4]
    rkblk = np.zeros((128, 2, 2), np.float32)
    for hp in range(2):
        for hh in range(2):
            rkblk[64 * hh:64 * hh + 64, hp, hh] = rk[2 * hp + hh]
    d = dict(wcat=wcat, wada_b=np.ascontiguousarray(wada[:, 0:2048]), bada_b=_pc(bada[0:2048], 16),
             preg=_pc(inp["pre_g"][l], 8), vecs=vec, pos=pos, invf=invf,
             wuq=wuq.reshape(256, 768), wukvk=wukvk.reshape(128, 512), wukvv=wukvv.reshape(128, 256),
             lng=np.ascontiguousarray(lng), lnb=np.ascontiguousarray(lnb), wsT=wsT, bs=bs,
             wa2=np.ascontiguousarray(wa2), rkblk=rkblk,
             lxg=np.ascontiguousarray(np.asarray(inp["rw_lnx_g"][l], np.float32)[own][None, :]),
             lxb=np.ascontiguousarray(np.asarray(inp["rw_lnx_b"][l], np.float32)[own][None, :]),
             wbr=np.ascontiguousarray(np.asarray(inp["w_br"][l], np.float32)[:, own, :].reshape(768, 1024)),
             wout=np.ascontiguousarray(np.asarray(inp["w_out"][l], np.float32)))
    c = consts()
    d["c_maskS"] = c["c_maskS"]; d["c_maskL"] = c["c_maskL"]
    if l > 0:
        v1 = np.asarray(inp["rw_v1"][l - 1], np.float32)
        d["v1"] = np.ascontiguousarray(np.concatenate([v1[own], v1[oth]], axis=0))
        d["v2"] = np.ascontiguousarray(np.asarray(inp["rw_v2"][l - 1], np.float32)[:, own])
    return d


def prep_common(inp, b):
    c = consts()
    return dict(c_ident=c["c_ident"], c_onesblk=c["c_onesblk"], cvec=_pc(inp["c"][b], 8))


def prep_pro(inp, lp):
    wada = np.asarray(inp["w_ada"][lp], np.float32)
    bada = np.asarray(inp["b_ada"][lp], np.float32)
    return dict(wada_p=np.ascontiguousarray(wada[:, 2048:3072]), bada_p=_pc(bada[2048:3072], 8),
                postg_p=_pc(inp["post_g"][lp], 8))


_PROGS = {}


def get_prog(T, do_pro, do_body, first):
    key = (T, do_pro, do_body, first)
    if key not in _PROGS:
        _PROGS[key] = build(T, do_pro, do_body, first)
    return _PROGS[key]


def run_layers(inp, T, B, n_layers):
    ncore = 2 * B
    xT = [np.ascontiguousarray(np.asarray(inp["x"][b], np.float32)[:T].T) for b in range(B)]
    yparts = None
    vf = None
    for l in range(n_layers + 1):
        do_pro = l > 0
        do_body = l < n_layers
        nc = build(T, do_pro, do_body, l == 0)
        maps = []
        for core in range(ncore):
            b, hs = divmod(core, 2)
            m = prep_common(inp, b)
            m["xT"] = xT[b]
            if do_pro:
                m.update(prep_pro(inp, l - 1))
                m["yA"] = yparts[2 * b]; m["yB"] = yparts[2 * b + 1]
            if do_body:
                m.update(prep_body(inp, l, b, hs, T))
                if l > 0:
                    m["vf_in"] = vf[core]
            maps.append(m)
        res = run_bass_kernel_spmd(nc, maps, core_ids=list(range(ncore))).results
        if do_pro:
            xT = [np.asarray(res[2 * b]["xT_out"]) for b in range(B)]
        if do_body:
            yparts = [np.asarray(res[c]["yP"]) for c in range(ncore)]
            if l == 0:
                vf = [np.asarray(res[c]["vf_out"]) for c in range(ncore)]
    return np.stack([x.T for x in xT], axis=0)


def kernel(**inputs):
    out = run_layers(inputs, 4096, 4, 4)
    return np.ascontiguousarray(out.astype(np.float32))
```

```python
import numpy as np
import concourse.bass as bass
import concourse.mybir as mybir
from concourse.bass_utils import run_bass_kernel_spmd

F32 = mybir.dt.float32
BF16 = mybir.dt.bfloat16
I32 = mybir.dt.int32
AF = mybir.ActivationFunctionType
ALU = mybir.AluOpType
AX = mybir.AxisListType

ENGS = ("pe", "dve", "act", "pool", "sp")
EPOCH = 30000
NDMA = 12


class V:
    __slots__ = ("ap", "key")

    def __init__(self, ap, key):
        self.ap = ap
        self.key = key

    def __getitem__(self, idx):
        return V(self.ap[idx], self.key)

    def k(self, sub):
        return V(self.ap, (self.key, sub))

    def re(self, pat, **kw):
        return V(self.ap.rearrange(pat, **kw), self.key)

    def bc(self, shape):
        return V(self.ap.to_broadcast(shape), self.key)

    def with_ap(self, ap):
        return V(ap, self.key)


class Prog:
    def __init__(self, nc):
        self.nc = nc
        self.engobj = {"pe": nc.tensor, "dve": nc.vector, "act": nc.scalar,
                       "pool": nc.gpsimd, "sp": nc.sync}
        self.stream = {e: [] for e in ENGS}
        self.cnt = {e: 0 for e in ENGS}
        self.seen = {e: {} for e in ENGS}
        self.last_w = {}
        self.readers = {}
        self.sems = {}
        self.dma_slot_uses = {}
        self.dma_rr = {e: 0 for e in ENGS}
        self.ntile = 0
        self.out_events = []

    def sb(self, shape, dt, name=None):
        self.ntile += 1
        name = name or f"t{self.ntile}"
        nm = f"{name}_{self.ntile}"
        if getattr(self, "phase_guards", None) is not None:
            g = self.nc.sbuf_tensor(nm, list(shape), dt)
            h = g.__enter__()
            self.phase_guards.append(g)
        else:
            h = self.nc.alloc_sbuf_tensor(nm, list(shape), dt)
        return V(h.ap(), nm)

    def phase_begin(self):
        if not hasattr(self, "phase_stack"):
            self.phase_stack = []
        self.phase_stack.append(getattr(self, "phase_guards", None))
        self.phase_guards = []

    def phase_end(self):
        self.barrier()
        for g in reversed(self.phase_guards):
            g.__exit__(None, None, None)
        self.phase_guards = self.phase_stack.pop()

    def barrier(self):
        evs = []
        for e2 in ENGS:
            n = self.cnt[e2]
            if n > 0:
                epoch, idx = divmod(n - 1, EPOCH)
                evs.append((("e", e2, epoch), idx + 1))
        for sk, uses in self.dma_slot_uses.items():
            evs.append((sk, 16 * uses))
        for e in ENGS:
            for ev in evs:
                self._need(e, ev)
        self.last_w = {}
        self.readers = {}

    def ps(self, shape, dt=F32, name=None):
        self.ntile += 1
        name = name or f"p{self.ntile}"
        h = self.nc.alloc_psum_tensor(f"{name}_{self.ntile}", list(shape), dt)
        return V(h.ap() if hasattr(h, "ap") else h[:], f"{name}_{self.ntile}")

    def dram(self, name, shape, dt, kind="Internal"):
        h = self.nc.dram_tensor(name, list(shape), dt, kind=kind)
        return V(h.ap(), "dram_" + name)

    def _sem(self, key):
        if key not in self.sems:
            self.sems[key] = self.nc.alloc_semaphore("s_" + "_".join(str(x) for x in key))
        return self.sems[key]

    def _need(self, eng, ev):
        if ev is None:
            return
        sk, val = ev
        if self.seen[eng].get(sk, 0) >= val:
            return
        self.seen[eng][sk] = val
        self.stream[eng].append(("wait", sk, val))

    def _deps(self, eng, reads, writes, self_sk):
        for r in reads:
            ev = self.last_w.get(r.key)
            if ev is not None:
                if ev[0] == self_sk and eng == "pe":
                    continue
                self._need(eng, ev)
        for w in writes:
            ev = self.last_w.get(w.key)
            if ev is not None and ev[0] != self_sk:
                self._need(eng, ev)
            for rv in self.readers.get(w.key, ()):
                if rv[0] != self_sk:
                    self._need(eng, rv)

    def _commit(self, ev, reads, writes):
        for r in reads:
            self.readers.setdefault(r.key, []).append(ev)
            lst = self.readers[r.key]
            if len(lst) > 24:
                d = {}
                for sk, v in lst:
                    d[sk] = max(d.get(sk, 0), v)
                self.readers[r.key] = list(d.items())
        for w in writes:
            self.last_w[w.key] = ev
            self.readers[w.key] = []

    def op(self, eng, fn, reads=(), writes=()):
        n = self.cnt[eng]
        epoch, idx = divmod(n, EPOCH)
        sk = ("e", eng, epoch)
        self._deps(eng, reads, writes, sk)
        ev = (sk, idx + 1)
        self.stream[eng].append(("op", fn, sk, 1))
        self.cnt[eng] = n + 1
        self._commit(ev, reads, writes)
        return ev

    def dma(self, q, out, in_, is_output=False):
        slot = self.dma_rr[q] % NDMA
        self.dma_rr[q] += 1
        sk = ("d", q, slot)
        uses = self.dma_slot_uses.get(sk, 0)
        if uses > 0:
            self._need(q, (sk, 16 * uses))
        self._deps(q, [in_], [out], sk)
        ev = (sk, 16 * (uses + 1))
        self.dma_slot_uses[sk] = uses + 1
        o_ap, i_ap = out.ap, in_.ap
        self.stream[q].append(("op", lambda e: e.dma_start(out=o_ap, in_=i_ap), sk, 16))
        self._commit(ev, [in_], [out])
        if is_output:
            self.out_events.append(ev)
        return ev

    def finish(self):
        for ev in self.out_events:
            self._need("sp", ev)

    def emit(self):
        self.finish()
        nc = self.nc
        for sk in set(x[1] for e in ENGS for x in self.stream[e] if x[0] == "wait") | \
                set(x[2] for e in ENGS for x in self.stream[e] if x[0] == "op"):
            self._sem(sk)
        with nc.Block() as block:
            def run(e):
                def body(engine):
                    for item in self.stream[e]:
                        if item[0] == "wait":
                            engine.wait_ge(self.sems[item[1]], item[2])
                        else:
                            inst = item[1](engine)
                            inst.then_inc(self.sems[item[2]], item[3])
                return body
            block.tensor(run("pe"))
            block.vector(run("dve"))
            block.scalar(run("act"))
            block.gpsimd(run("pool"))
            block.sync(run("sp"))

    def mm(self, out, lhsT, rhs, start=True, stop=True):
        o, l, r = out.ap, lhsT.ap, rhs.ap
        return self.op("pe", lambda e: e.matmul(o, l, r, start=start, stop=stop),
                       reads=[lhsT, rhs], writes=[out])

    def tr(self, out, in_, ident):
        o, i, d = out.ap, in_.ap, ident.ap
        return self.op("pe", lambda e: e.transpose(o, i, d), reads=[in_, ident], writes=[out])

    def act(self, out, in_, func, bias=None, scale=1.0, accum=None, eng="act"):
        o, i = out.ap, in_.ap
        reads = [in_]
        kw = {}
        if isinstance(bias, V):
            reads.append(bias); kw["bias"] = bias.ap
        elif bias is not None:
            kw["bias"] = bias
        if isinstance(scale, V):
            reads.append(scale); kw["scale"] = scale.ap
        else:
            kw["scale"] = scale
        writes = [out]
        if accum is not None:
            writes.append(accum); kw["accum_out"] = accum.ap
        return self.op(eng, lambda e: e.activation(out=o, in_=i, func=func, **kw),
                       reads=reads, writes=writes)

    def tt(self, out, a, b, op, eng="dve"):
        o, x, y = out.ap, a.ap, b.ap
        return self.op(eng, lambda e: e.tensor_tensor(out=o, in0=x, in1=y, op=op),
                       reads=[a, b], writes=[out])

    def ts(self, out, a, s1, op0, s2=None, op1=None, eng="dve", accum=None):
        o, x = out.ap, a.ap
        reads = [a]
        s1v = s1.ap if isinstance(s1, V) else s1
        s2v = s2.ap if isinstance(s2, V) else s2
        if isinstance(s1, V): reads.append(s1)
        if isinstance(s2, V): reads.append(s2)
        kw = {}
        writes = [out]
        if op1 is not None:
            kw["op1"] = op1
        if accum is not None:
            kw["accum_out"] = accum.ap; writes.append(accum)
        return self.op(eng, lambda e: e.tensor_scalar(out=o, in0=x, scalar1=s1v, scalar2=s2v, op0=op0, **kw),
                       reads=reads, writes=writes)

    def stt(self, out, a, s, b, op0, op1, eng="dve"):
        o, x, y = out.ap, a.ap, b.ap
        reads = [a, b]
        sv = s.ap if isinstance(s, V) else s
        if isinstance(s, V): reads.append(s)
        return self.op(eng, lambda e: e.scalar_tensor_tensor(out=o, in0=x, scalar=sv, in1=y, op0=op0, op1=op1),
                       reads=reads, writes=[out])

    def copy(self, out, in_, eng="dve"):
        o, i = out.ap, in_.ap
        if eng == "act":
            return self.op(eng, lambda e: e.copy(out=o, in_=i), reads=[in_], writes=[out])
        return self.op(eng, lambda e: e.tensor_copy(out=o, in_=i), reads=[in_], writes=[out])

    def memset(self, out, val, eng="dve"):
        o = out.ap
        return self.op(eng, lambda e: e.memset(o, val), reads=[], writes=[out])

import math

TT = 512
CH = 64
D = 1024
KC = 8
EPS = 1e-6
C0 = math.exp(-0.5)
SCALE = 96 ** -0.5
NCH_W = 49


class RR:
    def __init__(self, mk, n):
        self.t = [mk(i) for i in range(n)]
        self.i = 0

    def get(self):
        t = self.t[self.i % len(self.t)]
        self.i += 1
        return t


W_GROUPS = [
    (0, 4, "lat"), (4, 2, "gmu"), (6, 4, "gmv"), (10, 4, "rwrk"), (14, 4, "rwv"),
    (18, 1, "rwl"), (19, 4, "z0"), (23, 2, "z1"),
]
G_BASE = 25


DEBUG_STOP = None
F32R = mybir.dt.float32r


def RR_(v):
    return v.with_ap(v.ap.bitcast(F32R))

DEBUG_FLAGS = {}


def build(T, do_pro, do_body, first):
    nc = bass.Bass("TRN2", target_bir_lowering=False)
    P = Prog(nc)
    NT = T // TT
    NB = T // 128
    din = lambda name, shape, dt=F32: P.dram(name, shape, dt, kind="ExternalInput")
    dout = lambda name, shape, dt=F32: P.dram(name, shape, dt, kind="ExternalOutput")

    xT = din("xT", [D, T])
    c_ident = din("c_ident", [128, 128])
    c_onesblk = din("c_onesblk", [128, 128])
    cvec = din("cvec", [128, 8])

    ident = P.sb([128, 128], F32, "ident")
    onesblk = P.sb([128, 128], F32, "onesblk")
    ones = P.sb([128, 128], F32, "ones")
    P.dma("sp", ident, c_ident)
    P.dma("sp", onesblk, c_onesblk)
    P.memset(ones, 1.0)
    cact2 = P.sb([128, 8, 2], F32, "cact2")
    ctmp = P.sb([128, 8], F32, "ctmp")
    P.dma("sp", ctmp, cvec)
    P.act(cact2[:, :, 0], ctmp, AF.Silu)
    P.act(cact2[:, :, 1], ctmp, AF.Silu)

    pp = RR(lambda i: P.ps([128, 512], F32, f"pp{i}"), 2)
    pa = RR(lambda i: P.ps([128, 512], F32, f"pa{i}"), 2)
    po_t = P.ps([128, 512], F32, "po")
    px = RR(lambda i: P.ps([128, 512], F32, f"px{i}"), 3)

    eps_sb = P.sb([128, 4], F32, "eps")
    P.memset(eps_sb[:, 0:1], EPS)
    P.memset(eps_sb[:, 1:2], 1e-5)
    P.memset(eps_sb[:, 2:3], 64e-5)
    P.memset(eps_sb[:, 3:4], 0.0)
    cur = {}

    def mkpools(nsq=2, nrstd=1, nscr=0, nwst=0):
        cur["sq"] = RR(lambda i: P.sb([128, TT], F32, f"sq{i}"), nsq)
        cur["rstd"] = RR(lambda i: P.sb([128, TT], F32, f"rstd{i}"), nrstd) if nrstd else None
        cur["scr"] = RR(lambda i: P.sb([128, TT], F32, f"scr{i}"), nscr) if nscr else None
        cur["wst"] = RR(lambda i: P.sb([128, 8, 512], BF16, f"wst{i}"), nwst) if nwst else None

    def recip(v):
        P.op("dve", (lambda o, i: (lambda e: e.reciprocal(out=o, in_=i)))(v.ap, v.ap), reads=[v], writes=[v])

    def rms_rstd(chunks, n_feat):
        pt = px.get()
        n = len(chunks)
        for i, cv in enumerate(chunks):
            sq = cur["sq"].get()
            P.act(sq, cv, AF.Square)
            P.mm(pt[:, 0:TT], ones, sq, start=(i == 0), stop=(i == n - 1))
        r = cur["rstd"].get()
        P.act(r, pt[:, 0:TT], AF.Sqrt, bias=eps_sb[:, 0:1], scale=1.0 / n_feat)
        recip(r)
        return r

    if do_pro:
        yA = din("yA", [D, T]); yB = din("yB", [D, T])
        wada_p = din("wada_p", [D, D]); bada_p = din("bada_p", [128, 8]); postg_p = din("postg_p", [128, 8])
        xT_out = dout("xT_out", [D, T])
        gp = P.sb([128, 8], F32, "gp")
    if do_body:
        wcat = din("wcat", [D, NCH_W * 128])
        wada_b = din("wada_b", [D, 2 * D]); bada_b = din("bada_b", [128, 16]); preg = din("preg", [128, 8])
        vecs = din("vecs", [128, 40])
        pos_in = din("pos", [64, T], I32)
        invf = din("invf", [64, 2])
        wuq = din("wuq", [256, 4 * 192]); wukvk = din("wukvk", [128, 512]); wukvv = din("wukvv", [128, 256])
        lng = din("lng", [1, 512]); lnb = din("lnb", [1, 512])
        wsT = din("wsT", [128, 2, 128]); bs = din("bs", [1, 2, 128])
        wa2 = din("wa2", [128, 256]); rkblk = din("rkblk", [128, 2, 2])
        lxg = din("lxg", [1, 256]); lxb = din("lxb", [1, 256])
        c_maskS = din("c_maskS", [64, 128]); c_maskL = din("c_maskL", [64, 64])
        wbr = din("wbr", [768, D]); wout = din("wout", [D, D])
        yP = dout("yP", [D, T])
        if first:
            vf_out = dout("vf_out", [256, T])
        else:
            vf_in = din("vf_in", [256, T])
            v1 = din("v1", [512, 32]); v2 = din("v2", [32, 256])
        sc1 = P.sb([128, 8], F32, "sc1")
        ss_ = P.sb([128, 16], F32, "ss_")
        shift = ss_[:, 0:8]

    P.phase_begin()
    wada_pool = RR(lambda i: P.sb([128, 8, 128], F32, f"wada{i}"), 2)

    def adaln(w_dram, bias_sb, nch, out_sb):
        for oc in range(nch):
            wt = wada_pool.get()
            P.dma("sp", wt, w_dram[:, oc * 128:(oc + 1) * 128].re("(kc p) c -> p kc c", p=128))
            pt = px.get()
            for kc in range(8):
                P.mm(pt[:, 0:2], wt[:, kc, :], cact2[:, kc, :], start=(kc == 0), stop=(kc == 7))
            P.tt(out_sb[:, oc:oc + 1], pt[:, 0:1], bias_sb[:, oc:oc + 1], ALU.add)

    if do_pro:
        bp = P.sb([128, 8], F32, "bp"); pg = P.sb([128, 8], F32, "pg"); gate = P.sb([128, 8], F32, "gate")
        P.dma("sp", bp, bada_p); P.dma("sp", pg, postg_p)
        adaln(wada_p, bp, 8, gate)
        P.tt(gp, gate, pg, ALU.mult)
    if do_body:
        bb = P.sb([128, 16], F32, "bb"); P.dma("sp", bb, bada_b)
        pgb = P.sb([128, 8], F32, "pgb"); P.dma("sp", pgb, preg)
        adaln(wada_b, bb, 16, ss_)
        P.stt(sc1, ss_[:, 8:16], 1.0, pgb, ALU.add, ALU.mult)
    P.phase_end()

    if do_body:
        vec = P.sb([128, 40], F32, "vec"); P.dma("sp", vec, vecs)
        omka = P.sb([128, 2], F32, "omka")
        P.ts(omka, vec[:, 18:20], -1.0, ALU.mult, 1.0, ALU.add)
        maskS = P.sb([64, 128], F32, "maskS"); P.dma("sp", maskS, c_maskS)
        maskL = P.sb([64, 64], F32, "maskL"); P.dma("sp", maskL, c_maskL)

        def load_bf(dram_view, shape, name):
            t = P.sb(shape, BF16, name)
            P.dma("pool", t, dram_view)
            return t

        wuq_sb = load_bf(wuq.re("(kc p) c -> p kc c", p=128), [128, 2, 768], "wuq")
        wukvk_sb = load_bf(wukvk, [128, 512], "wukvk")
        wukvv_sb = load_bf(wukvv, [128, 256], "wukvv")
        wsT_sb = load_bf(wsT, [128, 2, 128], "wsT")
        P.memset(wsT_sb[64:128, :, 0:64], 0.0, eng="pool")
        bs_sb = P.sb([1, 2, 128], F32, "bs"); P.dma("sp", bs_sb, bs)
        wa2_sb = P.sb([128, 256], F32, "wa2"); P.dma("sp", wa2_sb, wa2)
        rkblk_sb = P.sb([128, 2, 2], F32, "rkblk"); P.dma("sp", rkblk_sb, rkblk)
        LG = P.sb([128, 256], F32, "LG"); P.dma("sp", LG, lng[:, 0:256].with_ap(lng[:, 0:256].ap.partition_broadcast(128)))
        LBt = P.sb([128, 256], F32, "LB"); P.dma("sp", LBt, lnb[:, 0:256].with_ap(lnb[:, 0:256].ap.partition_broadcast(128)))
        LXG = P.sb([64, 256], F32, "LXG"); P.dma("sp", LXG, lxg.with_ap(lxg.ap.partition_broadcast(64)))
        LXB = P.sb([64, 256], F32, "LXB"); P.dma("sp", LXB, lxb.with_ap(lxb.ap.partition_broadcast(64)))
        wbr_sb = load_bf(wbr.re("(j p) c -> p j c", p=128), [128, 6, D], "wbr")
        wout_sb = load_bf(wout.re("(j p) c -> p j c", p=128), [128, 8, D], "wout")
        if not first:
            v1_sb = P.sb([128, 4, 32], F32, "v1"); P.dma("sp", v1_sb, v1.re("(j p) c -> p j c", p=128))
            v2_sb = P.sb([32, 256], F32, "v2"); P.dma("sp", v2_sb, v2)
        invf_sb = P.sb([64, 2], F32, "invf"); P.dma("sp", invf_sb, invf)
        TWO_PI = 2.0 * math.pi
        Kc = P.sb([128, 4, T], BF16, "Kc")
        Vc = P.sb([128, NB, 4, 65], BF16, "Vc")
        P.memset(Kc[0:32, :, :], 0.0, eng="pool")
        P.memset(Kc[0:1, :, :], 1.0, eng="pool")
        P.memset(Vc[:, :, :, 64:65], 1.0, eng="pool")
        kmax2 = P.sb([1, 4], F32, "kmax2"); P.memset(kmax2, 0.0)
        Hst = P.sb([64, 4, 64], F32, "Hst"); P.ts(RR_(Hst), ones.with_ap(ones[0:64, 0:64].ap.unsqueeze(1).to_broadcast([64, 4, 64])), 0.0, ALU.mult)
        rwlast = P.sb([128, 9, 1], F32, "rwlast"); P.memset(rwlast, 0.0)
        hT = P.sb([128, 8, TT], BF16, "hT")
        zs = P.sb([128, 6, TT], BF16, "zs")
        brT = P.sb([128, 6, TT], BF16, "brT")
        small = P.sb([128, 64], F32, "small")
        st6 = P.sb([128, 4, 6], F32, "st6"); mv = P.sb([128, 4, 2], F32, "mv"); rs4 = P.sb([128, 4], F32, "rs4")
        nm4 = P.sb([128, 4], F32, "nm4")

        wbf = {}

        def wload(c_start, n):
            w = cur["wst"].get()
            if c_start not in wbf:
                wbf[c_start] = P.dram(f"wbf{c_start}", [128, 8, n * 128], BF16)
                P.dma("pool", w[:, :, 0:n * 128],
                      wcat[:, c_start * 128:(c_start + n) * 128].re("(kc p) c -> p kc c", p=128))
                P.dma("act", wbf[c_start], w[:, :, 0:n * 128])
            else:
                P.dma("sp", w[:, :, 0:n * 128], wbf[c_start])
            return w

        def proj_fm(w, j, M0=0, M1=128):
            pt = pp.get()
            for kc in range(8):
                P.mm(pt[0:M1 - M0, 0:TT], w[:, kc, j * 128 + M0:j * 128 + M1], hT[:, kc, :],
                     start=(kc == 0), stop=(kc == 7))
            return pt
    P.barrier()

    for ti in range(NT):
        c0t = ti * TT
        P.phase_begin()
        mkpools(nsq=2, nrstd=1)
        xt = P.sb([128, 8, TT], F32, "xt")
        P.dma("sp", xt, xT[:, c0t:c0t + TT].re("(kc p) t -> p kc t", p=128))
        if do_pro:
            ya_p = RR(lambda i: P.sb([128, TT], F32, f"ya{i}"), 2)
            yb_p = RR(lambda i: P.sb([128, TT], F32, f"yb{i}"), 2)

            def ysum_chunk(kc):
                ya = ya_p.get(); yb = yb_p.get()
                P.dma("act", ya, yA[kc * 128:(kc + 1) * 128, c0t:c0t + TT])
                P.dma("act", yb, yB[kc * 128:(kc + 1) * 128, c0t:c0t + TT])
                P.tt(ya, ya, yb, ALU.add, eng="pool")
                return ya
            pt = px.get()
            for kc in range(8):
                ys_ = ysum_chunk(kc)
                sq = cur["sq"].get()
                P.act(sq, ys_, AF.Square)
                P.mm(pt[:, 0:TT], ones, sq, start=(kc == 0), stop=(kc == 7))
            r = cur["rstd"].get()
            P.act(r, pt[:, 0:TT], AF.Sqrt, bias=eps_sb[:, 0:1], scale=1.0 / D)
            recip(r)
            for kc in range(8):
                ys_ = ysum_chunk(kc)
                P.tt(ys_, ys_, r, ALU.mult)
                P.stt(xt[:, kc, :], ys_, gp[:, kc:kc + 1], xt[:, kc, :], ALU.mult, ALU.add)
            P.dma("sp", xT_out[:, c0t:c0t + TT].re("(kc p) t -> p kc t", p=128), xt, is_output=True)
        if not do_body:
            P.phase_end()
            continue
        r = rms_rstd([xt[:, kc, :] for kc in range(8)], D)
        for kc in range(8):
            t = cur["sq"].get()
            P.tt(t, xt[:, kc, :], r, ALU.mult)
            P.ts(hT[:, kc, :], t, sc1[:, kc:kc + 1], ALU.mult, shift[:, kc:kc + 1], ALU.add)
        P.phase_end()

        if DEBUG_STOP == 0:
            break
        P.phase_begin()
        mkpools(nsq=2, nrstd=1, nscr=3, nwst=2)
        scr = cur["scr"]
        lat = P.sb([128, 3, TT], F32, "lat")
        kr = P.sb([64, 2, TT], F32, "kr")
        qn = P.sb([128, 2, TT], BF16, "qn")
        kvn = P.sb([128, TT], BF16, "kvn")
        pt_pool = RR(lambda i: P.sb([128, TT], BF16, f"ptp{i}"), 3)
        attn_tok = P.sb([128, 4, 256], F32, "attn_tok")
        CC = P.sb([64, TT], F32, "CC"); SSt = P.sb([64, TT], F32, "SS")
        ki = P.sb([64, TT], I32, "ki")
        qT = [P.sb([128, TT], BF16, f"qT{h}") for h in range(4)]
        for h in range(4):
            P.memset(qT[h][0:32, :], 0.0, eng="pool")
        ksq = P.sb([128, TT], F32, "ksq"); P.memset(ksq[0:32, :], 0.0)
        ang = cur["sq"].get()
        a64 = ang[0:64, :]
        P.dma("sp", ki, pos_in[:, c0t:c0t + TT])
        P.copy(a64, ki)
        P.ts(a64, a64, invf_sb[:, 0:1], ALU.mult)
        kq = cur["rstd"].get()[0:64, :]
        P.ts(kq, a64, 1.0 / TWO_PI, ALU.mult)
        P.copy(ki, kq)
        P.copy(kq, ki)
        P.stt(a64, kq, -6.28125, a64, ALU.mult, ALU.add)
        P.stt(a64, kq, -(TWO_PI - 6.28125), a64, ALU.mult, ALU.add)
        m = cur["sq"].get()[0:64, :]
        P.ts(m, a64, math.pi, ALU.is_gt, -TWO_PI, ALU.mult)
        P.tt(a64, a64, m, ALU.add)
        P.ts(m, a64, -math.pi, ALU.is_lt, TWO_PI, ALU.mult)
        P.tt(a64, a64, m, ALU.add)
        P.act(SSt, a64, AF.Sin)
        P.ts(SSt, SSt, invf_sb[:, 1:2], ALU.mult)
        P.ts(a64, a64, math.pi / 2, ALU.add)
        P.ts(m, a64, math.pi, ALU.is_gt, -TWO_PI, ALU.mult)
        P.tt(a64, a64, m, ALU.add)
        P.act(CC, a64, AF.Sin)
        w = wload(0, 4)
        for j in range(3):
            pt = proj_fm(w, j)
            P.copy(lat[:, j, :], pt[:, 0:TT], eng="act")
        for s in range(2):
            pt = proj_fm(w, 3, 64 * s, 64 * s + 64)
            P.copy(kr[:, s, :], pt[0:64, 0:TT], eng="act")
        w = wload(19, 4)
        for j in range(4):
            pt = proj_fm(w, j)
            P.act(zs[:, j, :], pt[:, 0:TT], AF.Silu)
        w = wload(23, 2)
        for j in range(2):
            pt = proj_fm(w, j)
            P.act(zs[:, 4 + j, :], pt[:, 0:TT], AF.Silu)
        r = rms_rstd([lat[:, 0, :], lat[:, 1, :]], 256)
        for j in range(2):
            t = cur["sq"].get()
            P.tt(t, lat[:, j, :], r, ALU.mult)
            P.ts(qn[:, j, :], t, vec[:, 9 + j:10 + j], ALU.mult)
        r = rms_rstd([lat[:, 2, :]], 128)
        t = cur["sq"].get()
        P.tt(t, lat[:, 2, :], r, ALU.mult)
        P.ts(kvn, t, vec[:, 11:12], ALU.mult)
        krot = scr.get()
        t1 = scr.get()
        P.tt(krot[32:64, :], kr[32:64, 0, :], CC[32:64, :], ALU.mult)
        P.tt(t1[32:64, :], kr[32:64, 1, :], SSt[32:64, :], ALU.mult)
        P.tt(krot[32:64, :], krot[32:64, :], t1[32:64, :], ALU.add)
        P.act(ksq[32:64, :], krot[32:64, :], AF.Square)
        for h in range(4):
            P.copy(Kc[32:64, h, c0t:c0t + TT], krot[32:64, :], eng="pool")
            pt = pp.get()
            P.mm(pt[:, 0:TT], wukvk_sb[:, h * 128:(h + 1) * 128], kvn)
            P.copy(Kc[64:128, h, c0t:c0t + TT], pt[64:128, 0:TT], eng="act")
            P.act(ksq[64:128, :], pt[64:128, 0:TT], AF.Square)
            p1 = px.get()
            P.mm(p1[0:1, 0:TT], ones[:, 0:1], ksq)
            P.op("dve", (lambda o, i: (lambda e: e.tensor_reduce(out=o, in_=i, op=ALU.max, axis=AX.X)))(small[0:1, h:h + 1].ap, p1[0:1, 0:TT].ap),
                 reads=[p1], writes=[small])
            P.tt(kmax2[0:1, h:h + 1], kmax2[0:1, h:h + 1], small[0:1, h:h + 1], ALU.max)
        for tb in range(4):
            pt = pp.get()
            P.mm(pt[:, 0:256], kvn[:, tb * 128:(tb + 1) * 128], wukvv_sb)
            blk = ti * 4 + tb
            P.copy(Vc[:, blk, :, 0:64], pt[:, 0:256].re("p (h d) -> p h d", h=4), eng="act")
        for h in range(4):
            pq = pp.get()
            for j in range(2):
                P.mm(pq[:, 0:TT], wuq_sb[:, j, h * 192:h * 192 + 128], qn[:, j, :], start=(j == 0), stop=(j == 1))
            psw = pp.get()
            for j in range(2):
                P.mm(psw[0:64, 0:TT], wuq_sb[:, j, h * 192 + 128:h * 192 + 192], qn[:, j, :], start=(j == 0), stop=(j == 1))
            qf = scr.get()
            P.copy(qf, pq[:, 0:TT], eng="act")
            sq = cur["sq"].get()
            P.act(sq, qf, AF.Square)
            p1 = px.get()
            P.mm(p1[0:1, 0:TT], ones[:, 0:1], sq)
            t1 = scr.get()
            t2 = scr.get()
            P.tt(t1[32:64, :], qf[32:64, :], CC[32:64, :], ALU.mult)
            P.tt(t2[32:64, :], psw[32:64, 0:TT], SSt[32:64, :], ALU.mult)
            P.tt(qT[h][32:64, :], t1[32:64, :], t2[32:64, :], ALU.add)
            P.copy(qT[h][64:128, :], qf[64:128, :], eng="pool")
            P.act(t1[0:1, :], p1[0:1, 0:TT], AF.Sqrt, scale=kmax2[0:1, h:h + 1])
            P.ts(qT[h][0:1, :], t1[0:1, :], -1.0, ALU.mult)
        for h in range(4):
            nkb = 4 * ti + 4
            po = po_t
            first_mm = True
            for kb in range(nkb):
                j = kb - 4 * ti
                cq = 128 * j if j > 0 else 0
                sT = pa.get()
                P.mm(sT[:, cq:TT], Kc[:, h, kb * 128:(kb + 1) * 128], qT[h][:, cq:TT])
                pt_ = pt_pool.get()
                P.act(pt_[:, cq:TT], sT[:, cq:TT], AF.Exp, scale=SCALE)
                if j >= 0:
                    P.memset(pt_[64:128, cq:cq + 64], 0.0, eng="pool")
                for qb in range(4):
                    if 128 * qb < cq:
                        continue
                    o_ap, l_ap, r_ap = po[:, qb * 128:qb * 128 + 65].ap, pt_[:, qb * 128:(qb + 1) * 128].ap, Vc[:, kb, h, :].ap
                    st_, sp_ = first_mm, (kb == nkb - 1 and qb == 3)
                    P.op("pe", (lambda o, l, r_, a, b: (lambda e: e.matmul(o, l, r_, start=a, stop=b, skip_group_check=True)))(o_ap, l_ap, r_ap, st_, sp_),
                         reads=[pt_, Vc], writes=[po])
                    first_mm = False
            rec = small[:, 8:12]
            pov = po[:, 0:512].re("p (q c) -> p q c", q=4)
            P.op("dve", (lambda o, i: (lambda e: e.reciprocal(out=o, in_=i)))(rec.ap, pov[:, :, 64].ap), reads=[po], writes=[small])
            P.tt(attn_tok[:, :, h * 64:(h + 1) * 64], pov[:, :, 0:64], rec.with_ap(rec.ap.unsqueeze(2).to_broadcast([128, 4, 64])), ALU.mult)
        for j in range(2):
            ptp = px.get()
            for qb in range(4):
                P.tr(ptp[:, qb * 128:(qb + 1) * 128], attn_tok[:, qb, j * 128:(j + 1) * 128], ident)
            P.tt(brT[:, j, :], ptp[:, 0:TT], zs[:, j, :], ALU.mult)
        P.phase_end()

        if DEBUG_STOP == 1:
            break
        P.phase_begin()
        mkpools(nsq=2, nrstd=0, nscr=0, nwst=2)
        uT = P.sb([128, 2, TT], F32, "uT")
        vt = P.sb([128, 4, 512], F32, "vt")
        vnb = P.sb([128, 4, 256], BF16, "vnb")
        w = wload(4, 2)
        for j in range(2):
            pt = proj_fm(w, j)
            P.act(uT[:, j, :], pt[:, 0:TT], AF.Gelu)
        w = wload(6, 4)
        for tb in range(4):
            pt = pp.get()
            for kc in range(8):
                P.mm(pt[:, 0:512], hT[:, kc, tb * 128:(tb + 1) * 128], w[:, kc, 0:512], start=(kc == 0), stop=(kc == 7))
            P.act(vt[:, tb, :], pt[:, 0:512], AF.Gelu)
        for tb in range(4):
            P.op("dve", (lambda o, i: (lambda e: e.bn_stats(out=o, in_=i)))(st6[:, tb, :].ap, vt[:, tb, :].ap), reads=[vt], writes=[st6])
            P.op("dve", (lambda o, i: (lambda e: e.bn_aggr(out=o, in_=i)))(mv[:, tb, :].ap, st6[:, tb, :].ap), reads=[st6], writes=[mv])
        P.act(rs4, mv[:, :, 1], AF.Sqrt, bias=eps_sb[:, 1:2])
        recip(rs4)
        P.stt(nm4, mv[:, :, 0], -1.0, rs4, ALU.mult, ALU.mult)
        for tb in range(4):
            t = cur["sq"].get()
            P.ts(t[:, 0:256], vt[:, tb, 0:256], rs4[:, tb:tb + 1], ALU.mult, nm4[:, tb:tb + 1], ALU.add)
            P.tt(t[:, 0:256], t[:, 0:256], LG, ALU.mult)
            P.tt(vnb[:, tb, :], t[:, 0:256], LBt, ALU.add)
        for g in range(2):
            pg_ = px.get()
            for tb in range(4):
                P.mm(pg_[:, tb * 128:(tb + 1) * 128], vnb[:, tb, g * 128:(g + 1) * 128], wsT_sb[:, g, :], start=True, stop=False)
                P.mm(pg_[:, tb * 128:(tb + 1) * 128], ones[0:1, 0:128], bs_sb[0:1, g, :], start=False, stop=True)
            t = cur["sq"].get()
            P.tt(t, uT[:, g, :], zs[:, 2 + g, :], ALU.mult, eng="pool")
            P.tt(brT[:, 2 + g, :], pg_[:, 0:TT], t, ALU.mult)
        P.phase_end()

        if DEBUG_STOP == 2:
            break
        P.phase_begin()
        big = lambda name: P.sb([128, 2, TT], F32, name)
        e_inc = big("e_inc"); BT = big("BT"); KT = big("KT"); vm = big("vm"); rk = big("rk")
        AR = P.sb([128, 2, 8, 2, 64], F32, "AR")
        P.phase_begin()
        rws = P.sb([128, 9, TT], F32, "rws")
        P.phase_begin()
        mkpools(nsq=2, nrstd=0, nscr=0, nwst=2)
        rwraw = P.sb([128, 9, TT + 1], F32, "rwraw")
        P.copy(rwraw[:, :, 0:1], rwlast)
        for (cs_, n_, base) in ((10, 4, 0), (14, 4, 4), (18, 1, 8)):
            w = wload(cs_, n_)
            for j in range(n_):
                pt = proj_fm(w, j)
                P.copy(rwraw[:, base + j, 1:TT + 1], pt[:, 0:TT], eng="act")
        for idx in range(9):
            t = cur["sq"].get()
            P.tt(t, rwraw[:, idx, 0:TT], rwraw[:, idx, 1:TT + 1], ALU.subtract)
            P.stt(rws[:, idx, :], t, vec[:, idx:idx + 1], rwraw[:, idx, 1:TT + 1], ALU.mult, ALU.add)
        P.copy(rwlast, rwraw[:, :, TT:TT + 1])
        P.phase_end()
        if DEBUG_STOP == 10:
            P.phase_end(); P.phase_end(); break
        P.phase_begin()
        mkpools(nsq=2, nrstd=0, nscr=2, nwst=0)
        scr = cur["scr"]
        lo_sb = P.sb([32, TT], F32, "lo_sb")
        tmp1 = lambda name: P.sb([128, TT], F32, name)
        sgw = tmp1("sgw"); asig = tmp1("asig"); cs = tmp1("cs"); exl = tmp1("exl"); e_exc = tmp1("e_exc")
        e_neg = tmp1("e_neg"); kkn = tmp1("kkn"); k2 = tmp1("k2")
        v4 = lambda x: x.re("p (c t) -> p c t", t=64)
        th = P.sb([64, TT], F32, "th")
        P.act(th[0:64, :], rws[0:64, 8, :], AF.Tanh)
        if not first:
            plo = px.get()
            for j in range(4):
                P.mm(plo[0:32, 0:TT], v1_sb[:, j, :], rws[:, 4 + j, :], start=(j == 0), stop=(j == 3))
            P.copy(lo_sb, plo[0:32, 0:TT], eng="act")
        for hp in range(2):
            pw = px.get()
            P.mm(pw[:, 0:TT], wa2_sb[0:64, hp * 128:(hp + 1) * 128], th[0:64, :])
            P.act(sgw, pw[:, 0:TT], AF.Sigmoid, bias=vec[:, 12 + hp:13 + hp])
            pa_ = px.get()
            P.mm(pa_[:, 0:TT], wa2_sb[64:128, hp * 128:(hp + 1) * 128], rws[64:128, 8, :])
            P.act(asig, pa_[:, 0:TT], AF.Sigmoid, bias=vec[:, 14 + hp:15 + hp])
            for c in range(8):
                o_, a_, b_ = cs[:, c * 64:(c + 1) * 64].ap, ones[:, 0:64].ap, sgw[:, c * 64:(c + 1) * 64].ap
                P.op("dve", (lambda o, a, b: (lambda e: e.tensor_tensor_scan(out=o, data0=a, data1=b, initial=0.0, op0=ALU.mult, op1=ALU.add)))(o_, a_, b_),
                     reads=[ones, sgw], writes=[cs])
            P.tt(exl, cs, sgw, ALU.subtract)
            P.act(e_inc[:, hp, :], cs, AF.Exp, scale=-C0)
            P.act(e_exc, exl, AF.Exp, scale=-C0)
            P.act(e_neg, cs, AF.Exp, scale=C0)
            if first:
                P.copy(vm[:, hp, :], rws[:, 4 + hp, :], eng="pool")
            else:
                vf = exl
                P.dma("sp", vf, vf_in[hp * 128:(hp + 1) * 128, c0t:c0t + TT])
                pv = px.get()
                P.mm(pv[:, 0:TT], v2_sb[0:32, hp * 128:(hp + 1) * 128], lo_sb[0:32, :])
                sv = scr.get()
                P.act(sv, pv[:, 0:TT], AF.Sigmoid, bias=vec[:, 20 + hp:21 + hp])
                P.tt(vf, vf, rws[:, 4 + hp, :], ALU.subtract)
                P.tt(vf, vf, sv, ALU.mult)
                P.tt(vm[:, hp, :], vf, rws[:, 4 + hp, :], ALU.add)
            kk = scr.get()
            P.ts(kk, rws[:, 2 + hp, :], vec[:, 16 + hp:17 + hp], ALU.mult)
            sq = cur["sq"].get()
            P.act(sq, kk, AF.Square)
            pn = px.get()
            P.mm(pn[:, 0:TT], onesblk, sq)
            nr = cur["sq"].get()
            P.act(nr, pn[:, 0:TT], AF.Sqrt)
            P.ts(nr, nr, 1e-12, ALU.max)
            recip(nr)
            P.tt(kkn, kk, nr, ALU.mult)
            t = cur["sq"].get()
            P.ts(t, asig, vec[:, 18 + hp:19 + hp], ALU.mult, omka[:, hp:hp + 1], ALU.add)
            P.tt(k2, rws[:, 2 + hp, :], t, ALU.mult)
            P.stt(RR_(AR[:, hp, :, 0, :]), v4(kkn), -1.0, v4(e_exc), ALU.mult, ALU.mult)
            P.tt(RR_(AR[:, hp, :, 1, :]), v4(rws[:, hp, :]), v4(e_inc[:, hp, :]), ALU.mult)
            tb_ = cur["sq"].get()
            P.tt(tb_, kkn, asig, ALU.mult)
            P.tt(RR_(BT[:, hp, :]), tb_, e_neg, ALU.mult)
            P.tt(RR_(KT[:, hp, :]), k2, e_neg, ALU.mult)
            P.tt(rk[:, hp, :], rws[:, hp, :], k2, ALU.mult, eng="pool")
        if first:
            P.dma("sp", vf_out[:, c0t:c0t + TT].re("(j p) t -> p j t", p=128), vm, is_output=True)
        P.phase_end()
        P.phase_end()
        if DEBUG_STOP == 11:
            P.phase_end(); break
        selF = P.sb([128, 64], F32, "selF")
        P.copy(RR_(selF), ident[:, 64:128])
        sel = RR_(selF)
        AR2 = P.sb([64, 2, 8, 2, 64], F32, "AR2")
        BT2 = P.sb([64, 2, TT], F32, "BT2"); KT2 = P.sb([64, 2, TT], F32, "KT2")
        G = P.sb([64, 8, 4], F32, "G")
        e4 = e_inc.re("p h (c t) -> p h c t", t=64)
        for hp in range(2):
            for half in range(2):
                pl = px.get()
                P.mm(pl[0:64, 0:512], sel, RR_(AR[:, hp, 4 * half:4 * half + 4, :, :].re("p c a t -> p (c a t)")))
                P.copy(RR_(AR2[:, hp, 4 * half:4 * half + 4, :, :].re("p c a t -> p (c a t)")), pl[0:64, 0:512], eng="act")
            pl = px.get()
            P.mm(pl[0:64, 0:512], sel, RR_(BT[:, hp, :]))
            P.copy(RR_(BT2[:, hp, :]), pl[0:64, 0:512], eng="act")
            pl = px.get()
            P.mm(pl[0:64, 0:512], sel, RR_(KT[:, hp, :]))
            P.copy(RR_(KT2[:, hp, :]), pl[0:64, 0:512], eng="act")
            pl = px.get()
            P.mm(pl[0:64, 0:8], ident[:, 64:128], e4[:, hp, :, 63])
            P.copy(G[:, :, 2 * hp + 1], pl[0:64, 0:8], eng="act")
            P.copy(G[:, :, 2 * hp], e4[0:64, hp, :, 63], eng="pool")
        ARh = lambda hp, hh: (AR if hh == 0 else AR2)[0:64, hp]
        BTh = lambda hp, hh: (BT if hh == 0 else BT2)[0:64, hp]
        KTh = lambda hp, hh: (KT if hh == 0 else KT2)[0:64, hp]
        t2 = lambda name, w=64: [P.sb([64, 2, w], F32, f"{name}{hp}") for hp in range(2)]
        tokb = [P.sb([64, 3, 128], F32, f"tok{hp}") for hp in range(2)]
        bcob = [P.sb([64, 2], F32, f"bco{hp}") for hp in range(2)]
        S1mb = t2("S1m", 128); S2mb = t2("S2m", 128)
        Pmb = [t2("PmA"), t2("PmB")]; PTmb = [t2("PTmA"), t2("PTmB")]; STb = [t2("STA"), t2("STB")]
        W0b = t2("W0"); U0b = t2("U0"); Y0b = t2("Y0"); W1b = t2("W1"); Ub = t2("U"); Yb = t2("Y")
        ynb = t2("yn"); yob = t2("yo"); HGb = t2("HG")
        st6b = [P.sb([64, 2, 6], F32, f"st6h{hp}") for hp in range(2)]
        mvb = [P.sb([64, 2, 2], F32, f"mvh{hp}") for hp in range(2)]
        rsb = [P.sb([64, 2], F32, f"rsh{hp}") for hp in range(2)]
        v2h = lambda x: x.re("p (h x) -> p h x", h=2)
        mS2 = maskS.with_ap(maskS.ap.unsqueeze(1).to_broadcast([64, 2, 128]))
        mL2 = maskL.with_ap(maskL.ap.unsqueeze(1).to_broadcast([64, 2, 64]))
        idb2 = ident.with_ap(ident[0:64, 0:64].ap.unsqueeze(1).to_broadcast([64, 2, 64]))

        def chunk_gen(c, hp):
            cc = slice(c * 64, (c + 1) * 64)
            Hs = Hst[:, 2 * hp:2 * hp + 2, :].k(hp)
            tok = tokb[hp]; bco = bcob[hp]
            ptk = px.get()
            for s_, src in enumerate((BT, KT, vm)):
                P.tr(ptk[0:64, s_ * 128:(s_ + 1) * 128], src[:, hp, cc], ident)
            pbq = px.get()
            P.mm(pbq[0:64, 0:2], rk[:, hp, cc], rkblk_sb[:, hp, :])
            P.copy(RR_(tok), ptk[0:64, 0:384].re("p (s j) -> p s j", s=3), eng="act")
            P.copy(bco, pbq[0:64, 0:2], eng="act")
            yield
            p1 = px.get(); p3 = px.get()
            for hh in range(2):
                arr = ARh(hp, hh)[:, c, :, :].re("p a t -> p (a t)")
                P.mm(p1[0:64, hh * 128:(hh + 1) * 128], RR_(BTh(hp, hh)[:, cc]), RR_(arr))
                P.mm(p1[0:64, 256 + hh * 128:256 + (hh + 1) * 128], RR_(KTh(hp, hh)[:, cc]), RR_(arr))
                P.mm(p3[0:64, hh * 64:(hh + 1) * 64], RR_(ARh(hp, hh)[:, c, 0, :]), RR_(BTh(hp, hh)[:, cc]))
            S1m = S1mb[hp]; S2m = S2mb[hp]
            Pm = Pmb[0][hp]; PTm = PTmb[0][hp]; ST = STb[0][hp]
            P.tt(RR_(S1m), v2h(p1[0:64, 0:256]), mS2, ALU.mult)
            P.tt(RR_(S2m), v2h(p1[0:64, 256:512]), mS2, ALU.mult)
            P.tt(RR_(Pm), v2h(p3[0:64, 0:128]), mL2, ALU.mult)
            P.copy(RR_(PTm), S1m[:, :, 0:64])
            P.tt(RR_(ST), S1m[:, :, 0:64], idb2, ALU.add)
            yield
            for it in range(5):
                pq_ = px.get()
                for hh in range(2):
                    P.mm(pq_[0:64, hh * 64:(hh + 1) * 64], RR_(PTm[:, hh, :]), RR_(Pm[:, hh, :]))
                    if it < 4:
                        P.mm(pq_[0:64, 128 + hh * 64:128 + (hh + 1) * 64], RR_(Pm[:, hh, :]), RR_(PTm[:, hh, :]))
                nb = (it + 1) % 2
                Pn = Pmb[nb][hp]; PTn = PTmb[nb][hp]; STn = STb[nb][hp]
                P.copy(RR_(Pn), v2h(pq_[0:64, 0:128]), eng="act")
                if it < 4:
                    P.copy(RR_(PTn), v2h(pq_[0:64, 128:256]), eng="act")
                yield
                ps_ = px.get()
                for hh in range(2):
                    P.mm(ps_[0:64, hh * 64:(hh + 1) * 64], RR_(Pn[:, hh, :]), RR_(ST[:, hh, :]))
                P.tt(RR_(STn), v2h(ps_[0:64, 0:128]), ST, ALU.add)
                Pm, PTm, ST = Pn, PTn, STn
                yield
            vtok = lambda hh: tok[:, 2, hh * 64:(hh + 1) * 64]
            pw0 = px.get()
            for hh in range(2):
                P.mm(pw0[0:64, hh * 64:(hh + 1) * 64], RR_(S2m[:, hh, 0:64]), RR_(vtok(hh)))
                P.mm(pw0[0:64, 128 + hh * 64:128 + (hh + 1) * 64], RR_(S2m[:, hh, 64:128]), RR_(vtok(hh)))
            W0 = W0b[hp]; Y0 = Y0b[hp]; U0 = U0b[hp]; W1 = W1b[hp]; U = Ub[hp]; Y = Yb[hp]; HG = HGb[hp]
            P.copy(RR_(W0), v2h(pw0[0:64, 0:128]), eng="act")
            P.copy(Y0, v2h(pw0[0:64, 128:256]), eng="act")
            yield
            pu0 = px.get()
            for hh in range(2):
                P.mm(pu0[0:64, hh * 64:(hh + 1) * 64], RR_(ST[:, hh, :]), RR_(W0[:, hh, :]))
            P.copy(U0, v2h(pu0[0:64, 0:128]), eng="act")
            yield
            pw1 = px.get()
            for hh in range(2):
                P.mm(pw1[0:64, hh * 64:(hh + 1) * 64], RR_(ARh(hp, hh)[:, c, 0, :]), RR_(Hs[:, hh, :]))
            P.copy(RR_(W1), v2h(pw1[0:64, 0:128]))
            yield
            pu = px.get()
            for hh in range(2):
                P.mm(pu[0:64, hh * 64:(hh + 1) * 64], RR_(ST[:, hh, :]), RR_(W1[:, hh, :]))
            P.tt(RR_(U), v2h(pu[0:64, 0:128]), U0, ALU.add)
            yield
            py = px.get()
            for hh in range(2):
                P.mm(py[0:64, hh * 64:(hh + 1) * 64], RR_(ARh(hp, hh)[:, c, 1, :]), RR_(Hs[:, hh, :]), start=True, stop=False)
                P.mm(py[0:64, hh * 64:(hh + 1) * 64], RR_(S1m[:, hh, 64:128]), RR_(U[:, hh, :]), start=False, stop=True)
            for hh in range(2):
                P.mm(py[0:64, 128 + hh * 64:128 + (hh + 1) * 64], RR_(tok[:, 1, hh * 64:(hh + 1) * 64]), RR_(vtok(hh)), start=True, stop=False)
                P.mm(py[0:64, 128 + hh * 64:128 + (hh + 1) * 64], RR_(tok[:, 0, hh * 64:(hh + 1) * 64]), RR_(U[:, hh, :]), start=False, stop=True)
            Gb = G.with_ap(G[:, c, 2 * hp:2 * hp + 2].ap.unsqueeze(2).to_broadcast([64, 2, 64]))
            P.tt(HG, v2h(py[0:64, 128:256]), Hs, ALU.add)
            P.tt(RR_(Hs), HG, Gb, ALU.mult)
            P.tt(Y, v2h(py[0:64, 0:128]), Y0, ALU.add)
            yield
            st6h = st6b[hp]; mvh = mvb[hp]; rsh = rsb[hp]; yn = ynb[hp]; yo = yob[hp]
            for hh in range(2):
                P.op("dve", (lambda o, i: (lambda e: e.bn_stats(out=o, in_=i)))(st6h[:, hh, :].ap, Y[:, hh, :].ap), reads=[Y], writes=[st6h])
                P.op("dve", (lambda o, i: (lambda e: e.bn_aggr(out=o, in_=i)))(mvh[:, hh, :].ap, st6h[:, hh, :].ap), reads=[st6h], writes=[mvh])
            P.act(rsh, mvh[:, :, 1], AF.Sqrt, bias=eps_sb[0:64, 2:3])
            yield
            recip(rsh)
            for hh in range(2):
                P.ts(yn[:, hh, :], Y[:, hh, :], mvh[:, hh, 0:1], ALU.subtract, rsh[:, hh:hh + 1], ALU.mult)
            ynf = yn.re("p h x -> p (h x)")
            P.tt(ynf, ynf, LXG[:, hp * 128:(hp + 1) * 128], ALU.mult, eng="pool")
            P.tt(ynf, ynf, LXB[:, hp * 128:(hp + 1) * 128], ALU.add, eng="pool")
            yield
            for hh in range(2):
                P.stt(yo[:, hh, :], vtok(hh), bco[:, hh:hh + 1], yn[:, hh, :], ALU.mult, ALU.add)
            pyt = px.get()
            P.tr(pyt[:, 0:64], yo.re("p h x -> p (h x)"), ident[0:64, 0:64])
            P.tt(brT[:, 4 + hp, cc], pyt[:, 0:64], zs[:, 4 + hp, cc], ALU.mult)
            yield

        def drive(gens):
            gens = list(gens)
            while gens:
                for g_ in list(gens):
                    try:
                        next(g_)
                    except StopIteration:
                        gens.remove(g_)

        for c in range(8):
            drive([chunk_gen(c, 0), chunk_gen(c, 1)])
        P.phase_end()

        if DEBUG_STOP == 3:
            break
        P.phase_begin()
        mkpools(nsq=1, nrstd=0, nscr=3, nwst=2)
        scr = cur["scr"]
        mT = P.sb([128, 8, TT], BF16, "mT")
        sg = RR(lambda i: P.sb([128, 3, TT], F32, f"sg{i}"), 1)
        ystage = RR(lambda i: P.sb([128, TT], F32, f"ystage{i}"), 2)
        for dc in range(8):
            w = wload(G_BASE + dc * 3, 3)
            sgt = sg.get()
            for n in range(3):
                pt = proj_fm(w, n)
                P.act(sgt[:, n, :], pt[:, 0:TT], AF.Sigmoid)
            ts_ = []
            for n in range(3):
                pj = pp.get()
                for j in range(2):
                    P.mm(pj[:, 0:TT], wbr_sb[:, n * 2 + j, dc * 128:(dc + 1) * 128], brT[:, n * 2 + j, :], start=(j == 0), stop=(j == 1))
                t = scr.get()
                P.tt(t, pj[:, 0:TT], sgt[:, n, :], ALU.mult)
                ts_.append(t)
            P.tt(ts_[0], ts_[0], ts_[1], ALU.add, eng="pool")
            P.tt(mT[:, dc, :], ts_[0], ts_[2], ALU.add, eng="pool")
        for dc2 in range(8):
            pt = pp.get()
            for dc in range(8):
                P.mm(pt[:, 0:TT], wout_sb[:, dc, dc2 * 128:(dc2 + 1) * 128], mT[:, dc, :], start=(dc == 0), stop=(dc == 7))
            ys = ystage.get()
            P.copy(ys, pt[:, 0:TT], eng="act")
            P.dma("sp", yP[dc2 * 128:(dc2 + 1) * 128, c0t:c0t + TT], ys, is_output=True)
        P.phase_end()

    P.emit()
    return nc


_Q, _KV, _KR = 0, 256, 384
_GMU, _GMV = 416, 928
_RWR, _RWK, _RWV, _RWW, _RWA = 1440, 1952, 2464, 2976, 3040
_ZM, _ZG, _ZR = 3104, 3616, 4128
_G = 4640


def _pc(v, n):
    return np.ascontiguousarray(np.asarray(v, np.float32).reshape(n, 128).T)


def consts():
    ident = np.eye(128, dtype=np.float32)
    onesblk = np.zeros((128, 128), np.float32)
    onesblk[:64, :64] = 1.0
    onesblk[64:, 64:] = 1.0
    s = np.arange(64)[:, None]
    t = np.arange(64)[None, :]
    maskS = np.concatenate([(s < t), (s <= t)], axis=1).astype(np.float32)
    maskL = (np.arange(64)[None, :] < np.arange(64)[:, None]).astype(np.float32)
    return dict(c_ident=ident, c_onesblk=onesblk, c_maskS=maskS, c_maskL=maskL)


def prep_body(inp, l, b, hs, T):
    W = np.asarray(inp["w_in"][l], np.float32)
    own = slice(256 * hs, 256 * hs + 256)
    oth = slice(256 * (1 - hs), 256 * (1 - hs) + 256)
    z32 = np.zeros((1024, 32), np.float32)
    kr = W[:, _KR:_KR + 32]
    krsw = np.concatenate([kr[:, 16:32], kr[:, 0:16]], axis=1)
    cols = [W[:, _Q:_Q + 256], W[:, _KV:_KV + 128], z32, kr, z32, krsw,
            W[:, _GMU:_GMU + 512][:, own],
            W[:, _GMV:_GMV + 512][:, own], W[:, _GMV:_GMV + 512][:, oth],
            W[:, _RWR:_RWR + 512][:, own], W[:, _RWK:_RWK + 512][:, own],
            W[:, _RWV:_RWV + 512][:, own], W[:, _RWV:_RWV + 512][:, oth],
            W[:, _RWW:_RWW + 64], W[:, _RWA:_RWA + 64],
            W[:, _ZM:_ZM + 512][:, own], W[:, _ZG:_ZG + 512][:, own], W[:, _ZR:_ZR + 512][:, own]]
    Wg = W[:, _G:_G + 3072].reshape(1024, 3, 8, 128)
    cols.append(np.ascontiguousarray(Wg.transpose(0, 2, 1, 3)).reshape(1024, 3072))
    wcat = np.ascontiguousarray(np.concatenate(cols, axis=1))
    assert wcat.shape[1] == NCH_W * 128, wcat.shape
    wada = np.asarray(inp["w_ada"][l], np.float32)
    bada = np.asarray(inp["b_ada"][l], np.float32)
    mu = np.asarray(inp["rw_mu"][l], np.float32)
    mu_r, mu_k, mu_v = mu[0:512], mu[512:1024], mu[1024:1536]
    mu_l = mu[1536:1664]
    vec = np.zeros((128, 40), np.float32)
    vec[:, 0:2] = _pc(mu_r[own], 2); vec[:, 2:4] = _pc(mu_k[own], 2)
    vec[:, 4:6] = _pc(mu_v[own], 2); vec[:, 6:8] = _pc(mu_v[oth], 2)
    vec[:, 8] = mu_l
    vec[:, 9:11] = _pc(inp["mla_q_norm"][l], 2)
    vec[:, 11] = np.asarray(inp["mla_kv_norm"][l], np.float32)
    vec[:, 12:14] = _pc(np.asarray(inp["rw_w0"][l])[own], 2)
    vec[:, 14:16] = _pc(np.asarray(inp["rw_a0"][l])[own], 2)
    vec[:, 16:18] = _pc(np.asarray(inp["rw_k_k"][l])[own], 2)
    vec[:, 18:20] = _pc(np.asarray(inp["rw_k_a"][l])[own], 2)
    if l > 0:
        vec[:, 20:22] = _pc(np.asarray(inp["rw_v0"][l - 1])[own], 2)
    pos = np.ascontiguousarray(np.broadcast_to(np.asarray(inp["positions"][b], np.int32)[None, :T], (64, T)))
    invf = np.zeros((64, 2), np.float32)
    f = (10000.0 ** (-np.arange(0, 32, 2, dtype=np.float32) / 32)).astype(np.float32)
    invf[32:48, 0] = f; invf[48:64, 0] = f
    invf[32:48, 1] = -1.0; invf[48:64, 1] = 1.0
    wuq_full = np.asarray(inp["mla_w_uq"][l], np.float32)
    wuq = np.zeros((256, 4, 192), np.float32)
    for hh in range(4):
        h = 4 * hs + hh
        nope = wuq_full[:, h * 96:h * 96 + 64]
        rope = wuq_full[:, h * 96 + 64:h * 96 + 96]
        wuq[:, hh, 32:64] = rope
        wuq[:, hh, 64:128] = nope
        wuq[:, hh, 160:176] = rope[:, 16:32]
        wuq[:, hh, 176:192] = rope[:, 0:16]
    wukv = np.asarray(inp["mla_w_ukv"][l], np.float32)
    wukvk = np.zeros((128, 4, 128), np.float32)
    wukvv = np.zeros((128, 4, 64), np.float32)
    for hh in range(4):
        h = 4 * hs + hh
        wukvk[:, hh, 64:128] = wukv[:, h * 128:h * 128 + 64]
        wukvv[:, hh, :] = wukv[:, h * 128 + 64:h * 128 + 128]
    lng = np.asarray(inp["gm_ln_g"][l], np.float32); lnb = np.asarray(inp["gm_ln_b"][l], np.float32)
    lng = np.concatenate([lng[own], lng[oth]])[None, :]
    lnb = np.concatenate([lnb[own], lnb[oth]])[None, :]
    ws = np.asarray(inp["gm_w_s"][l], np.float32)[2 * hs:2 * hs + 2]
    wsT = np.ascontiguousarray(ws.transpose(2, 0, 1))
    bs = np.ascontiguousarray(np.asarray(inp["gm_b_s"][l], np.float32)[2 * hs:2 * hs + 2][None])
    wa2 = np.concatenate([np.asarray(inp["rw_w2"][l], np.float32)[:, own],
                          np.asarray(inp["rw_a2"][l], np.float32)[:, own]], axis=0)
    rk = np.asarray(inp["rw_r_k"][l], np.float32)[4 * hs:4 * hs + 4]
    rkblk = np.zeros((128, 2, 2), np.float32)
    for hp in range(2):
        for hh in range(2):
            rkblk[64 * hh:64 * hh + 64, hp, hh] = rk[2 * hp + hh]
    d = dict(wcat=wcat, wada_b=np.ascontiguousarray(wada[:, 0:2048]), bada_b=_pc(bada[0:2048], 16),
             preg=_pc(inp["pre_g"][l], 8), vecs=vec, pos=pos, invf=invf,
             wuq=wuq.reshape(256, 768), wukvk=wukvk.reshape(128, 512), wukvv=wukvv.reshape(128, 256),
             lng=np.ascontiguousarray(lng), lnb=np.ascontiguousarray(lnb), wsT=wsT, bs=bs,
             wa2=np.ascontiguousarray(wa2), rkblk=rkblk,
             lxg=np.ascontiguousarray(np.asarray(inp["rw_lnx_g"][l], np.float32)[own][None, :]),
             lxb=np.ascontiguousarray(np.asarray(inp["rw_lnx_b"][l], np.float32)[own][None, :]),
             wbr=np.ascontiguousarray(np.asarray(inp["w_br"][l], np.float32)[:, own, :].reshape(768, 1024)),
             wout=np.ascontiguousarray(np.asarray(inp["w_out"][l], np.float32)))
    c = consts()
    d["c_maskS"] = c["c_maskS"]; d["c_maskL"] = c["c_maskL"]
    if l > 0:
        v1 = np.asarray(inp["rw_v1"][l - 1], np.float32)
        d["v1"] = np.ascontiguousarray(np.concatenate([v1[own], v1[oth]], axis=0))
        d["v2"] = np.ascontiguousarray(np.asarray(inp["rw_v2"][l - 1], np.float32)[:, own])
    return d


def prep_common(inp, b):
    c = consts()
    return dict(c_ident=c["c_ident"], c_onesblk=c["c_onesblk"], cvec=_pc(inp["c"][b], 8))


def prep_pro(inp, lp):
    wada = np.asarray(inp["w_ada"][lp], np.float32)
    bada = np.asarray(inp["b_ada"][lp], np.float32)
    return dict(wada_p=np.ascontiguousarray(wada[:, 2048:3072]), bada_p=_pc(bada[2048:3072], 8),
                postg_p=_pc(inp["post_g"][lp], 8))


_PROGS = {}


def get_prog(T, do_pro, do_body, first):
    key = (T, do_pro, do_body, first)
    if key not in _PROGS:
        _PROGS[key] = build(T, do_pro, do_body, first)
    return _PROGS[key]


def run_layers(inp, T, B, n_layers):
    ncore = 2 * B
    xT = [np.ascontiguousarray(np.asarray(inp["x"][b], np.float32)[:T].T) for b in range(B)]
    yparts = None
    vf = None
    for l in range(n_layers + 1):
        do_pro = l > 0
        do_body = l < n_layers
        nc = build(T, do_pro, do_body, l == 0)
        maps = []
        for core in range(ncore):
            b, hs = divmod(core, 2)
            m = prep_common(inp, b)
            m["xT"] = xT[b]
            if do_pro:
                m.update(prep_pro(inp, l - 1))
                m["yA"] = yparts[2 * b]; m["yB"] = yparts[2 * b + 1]
            if do_body:
                m.update(prep_body(inp, l, b, hs, T))
                if l > 0:
                    m["vf_in"] = vf[core]
            maps.append(m)
        res = run_bass_kernel_spmd(nc, maps, core_ids=list(range(ncore))).results
        if do_pro:
            xT = [np.asarray(res[2 * b]["xT_out"]) for b in range(B)]
        if do_body:
            yparts = [np.asarray(res[c]["yP"]) for c in range(ncore)]
            if l == 0:
                vf = [np.asarray(res[c]["vf_out"]) for c in range(ncore)]
    return np.stack([x.T for x in xT], axis=0)


def kernel(**inputs):
    out = run_layers(inputs, 4096, 4, 4)
    return np.ascontiguousarray(out.astype(np.float32))
```
